# Optimizing a Trainium2 kernel written in Bass

```python
import math
import jax
import jax.numpy as jnp
from jax import lax
import numpy as np

D_MODEL = 1024
BATCH = 16
SEQ = 4096
DEPTH = 4
DEC_BATCH = 32
DEC_SEQ = 2048
PAST_LEN = 128

GRID_W = 64
HEAD_DIM = 64
Q_BLOCK = 128
EPS = 1e-6
NEG_INF = -1e30

NA_HEADS = 4
NA_WIN_R = 8
NA_WIN_C = 16
GQA_HEADS = 4
GQA_KV_HEADS = 2
ROPE_THETA = 10000.0
DIFF_HEADS = 4
DIFF_QK_DIM = 32
DIFF_V_DIM = 2 * DIFF_QK_DIM
SWA_HEADS = 4
SWA_KV_HEADS = 2
SWA_WINDOW = 128

N_ALIBI_HEADS = DIFF_HEADS + SWA_HEADS
N_BRANCH = 4
BRANCH_W = 256
MIX_W = N_BRANCH * BRANCH_W
D_FF = 2816
CONV_W = 3

IN_SIZES = (NA_HEADS * HEAD_DIM, NA_HEADS * HEAD_DIM, NA_HEADS * HEAD_DIM,
            GQA_HEADS * HEAD_DIM, GQA_KV_HEADS * HEAD_DIM, GQA_KV_HEADS * HEAD_DIM,
            DIFF_HEADS * 2 * DIFF_QK_DIM, DIFF_HEADS * 2 * DIFF_QK_DIM, DIFF_HEADS * DIFF_V_DIM,
            SWA_HEADS * HEAD_DIM, SWA_KV_HEADS * HEAD_DIM, SWA_KV_HEADS * HEAD_DIM,
            N_BRANCH * D_MODEL)
IN_COLS = sum(IN_SIZES)

kernel_name = 'hybrid_gated_encoder'


def _rmsnorm(x, gain):
    xf = x.astype(jnp.float32)
    xf = xf * lax.rsqrt(jnp.mean(xf * xf, axis=-1, keepdims=True) + EPS)
    return (xf * gain.astype(jnp.float32)).astype(x.dtype)


def _alibi_slopes():
    s = 2.0 ** (-8.0 * np.arange(1, N_ALIBI_HEADS + 1) / N_ALIBI_HEADS)
    return jnp.asarray(s[0::2], dtype=jnp.float32), jnp.asarray(s[1::2], dtype=jnp.float32)


def _axial_rope_tables(S):
    t = jnp.arange(S)
    row = (t // GRID_W).astype(jnp.float32)
    col = (t % GRID_W).astype(jnp.float32)
    axis_dim = HEAD_DIM // 2
    inv = ROPE_THETA ** (-jnp.arange(0, axis_dim, 2, dtype=jnp.float32) / axis_dim)
    ang_r = row[:, None] * inv
    ang_c = col[:, None] * inv
    return jnp.cos(ang_r), jnp.sin(ang_r), jnp.cos(ang_c), jnp.sin(ang_c)


def _rotate(x, c, s):
    x1, x2 = jnp.split(x, 2, axis=-1)
    c = c[None, :, None, :]
    s = s[None, :, None, :]
    return jnp.concatenate([x1 * c - x2 * s, x2 * c + x1 * s], axis=-1)


def _apply_axial_rope(x, cos_r, sin_r, cos_c, sin_c):
    xf = x.astype(jnp.float32)
    xr, xc = jnp.split(xf, 2, axis=-1)
    return jnp.concatenate([_rotate(xr, cos_r, sin_r), _rotate(xc, cos_c, sin_c)], axis=-1).astype(x.dtype)


def _neighbourhood_attention(q, k, v, rpb):
    B, S, H, hd = q.shape
    rows = S // GRID_W
    win_r = min(NA_WIN_R, rows)
    grid = lambda t: t.reshape(B, rows, GRID_W, H, hd)
    qg, kg, vg = grid(q), grid(k), grid(v)
    cols = jnp.arange(GRID_W)
    col_start = jnp.clip(cols - NA_WIN_C // 2, 0, GRID_W - NA_WIN_C)
    col_idx = col_start[:, None] + jnp.arange(NA_WIN_C)[None, :]
    dc_idx = col_idx - cols[:, None] + NA_WIN_C - 1
    scale = hd ** -0.5

    def row_block(args):
        r, q_r = args
        r0 = jnp.clip(r - win_r // 2, 0, rows - win_r)
        k_rows = lax.dynamic_slice_in_dim(kg, r0, win_r, axis=1)
        v_rows = lax.dynamic_slice_in_dim(vg, r0, win_r, axis=1)
        k_nb = k_rows[:, :, col_idx]
        v_nb = v_rows[:, :, col_idx]
        logits = jnp.einsum('bqhd,brqchd->bhqrc', q_r, k_nb).astype(jnp.float32) * scale
        dr_idx = r0 + jnp.arange(win_r) - r + NA_WIN_R - 1
        bias = rpb[:, dr_idx[None, :, None], dc_idx[:, None, :]]
        logits = logits + bias.astype(jnp.float32)[None]
        p = jax.nn.softmax(logits.reshape(B, H, GRID_W, win_r * NA_WIN_C), axis=-1)
        p = p.reshape(B, H, GRID_W, win_r, NA_WIN_C).astype(v.dtype)
        return jnp.einsum('bhqrc,brqchd->bqhd', p, v_nb)

    out = lax.map(row_block, (jnp.arange(rows), jnp.moveaxis(qg, 1, 0)))
    return jnp.moveaxis(out, 0, 1).reshape(B, S, H * hd)


def _dense_gqa(q, k, v):
    B, S, H, hd = q.shape
    KVH = k.shape[2]
    G = H // KVH
    nb = S // Q_BLOCK
    qb = jnp.moveaxis(q.reshape(B, nb, Q_BLOCK, KVH, G, hd), 1, 0)
    scale = hd ** -0.5

    def blk(q_blk):
        logits = jnp.einsum('bqkgd,bskd->bkgqs', q_blk, k).astype(jnp.float32) * scale
        p = jax.nn.softmax(logits, axis=-1).astype(v.dtype)
        return jnp.einsum('bkgqs,bskd->bqkgd', p, v)

    out = lax.map(blk, qb)
    return jnp.moveaxis(out, 0, 1).reshape(B, S, H * hd)


def _diff_attention(q, k, v, lam, slopes):
    B, S, H, _, d = q.shape
    nb = S // Q_BLOCK
    qb = jnp.moveaxis(q.reshape(B, nb, Q_BLOCK, H, 2, d), 1, 0)
    key_pos = jnp.arange(S)
    scale = d ** -0.5

    def blk(args):
        i, q_blk = args
        q_pos = i * Q_BLOCK + jnp.arange(Q_BLOCK)
        dist = jnp.abs(q_pos[:, None] - key_pos[None, :]).astype(jnp.float32)
        logits = jnp.einsum('bqhjd,bshjd->bhjqs', q_blk, k).astype(jnp.float32) * scale
        logits = logits - slopes[None, :, None, None, None] * dist[None, None, None]
        p = jax.nn.softmax(logits, axis=-1)
        p = p[:, :, 0] - lam * p[:, :, 1]
        return jnp.einsum('bhqs,bshe->bqhe', p.astype(v.dtype), v)

    out = lax.map(blk, (jnp.arange(nb), qb))
    return jnp.moveaxis(out, 0, 1).reshape(B, S, H, 2 * d)


def _window_attention(q, k, v, sink, slopes):
    B, S, H, hd = q.shape
    KVH = k.shape[2]
    G = H // KVH
    nb = S // Q_BLOCK
    span = Q_BLOCK + 2 * SWA_WINDOW
    pad = ((0, 0), (SWA_WINDOW, SWA_WINDOW), (0, 0), (0, 0))
    kp = jnp.pad(k, pad)
    vp = jnp.pad(v, pad)
    qb = jnp.moveaxis(q.reshape(B, nb, Q_BLOCK, KVH, G, hd), 1, 0)
    sl = slopes.reshape(KVH, G)
    sk = sink.astype(jnp.float32).reshape(KVH, G)
    scale = hd ** -0.5

    def blk(args):
        i, q_blk = args
        start = i * Q_BLOCK
        k_blk = lax.dynamic_slice_in_dim(kp, start, span, axis=1)
        v_blk = lax.dynamic_slice_in_dim(vp, start, span, axis=1)
        q_pos = start + jnp.arange(Q_BLOCK)
        k_pos = start - SWA_WINDOW + jnp.arange(span)
        rel = jnp.abs(k_pos[None, :] - q_pos[:, None])
        valid = (rel <= SWA_WINDOW) & (k_pos >= 0)[None, :] & (k_pos < S)[None, :]
        logits = jnp.einsum('bqkgd,bskd->bkgqs', q_blk, k_blk).astype(jnp.float32) * scale
        logits = logits - sl[None, :, :, None, None] * rel.astype(jnp.float32)[None, None, None]
        logits = jnp.where(valid[None, None, None], logits, NEG_INF)
        sink_col = jnp.broadcast_to(sk[None, :, :, None, None], logits.shape[:-1] + (1,))
        p = jax.nn.softmax(jnp.concatenate([logits, sink_col], axis=-1), axis=-1)[..., :-1]
        return jnp.einsum('bkgqs,bskd->bqkgd', p.astype(v.dtype), v_blk)

    out = lax.map(blk, (jnp.arange(nb), qb))
    return jnp.moveaxis(out, 0, 1).reshape(B, S, H * hd)


def _conv_ffn(h, w_up, conv_w, conv_b, w_down):
    u = jnp.einsum('bsd,df->bsf', h, w_up)
    S = u.shape[1]
    p = CONV_W // 2
    up = jnp.pad(u, ((0, 0), (p, p), (0, 0)))
    c = conv_b
    for j in range(CONV_W):
        c = c + up[:, j:j + S] * conv_w[j]
    gate, val = jnp.split(c, 2, axis=-1)
    return jnp.einsum('bsf,fd->bsd', jax.nn.gelu(gate, approximate=True) * val, w_down)


def setup_inputs(seed: int = 0) -> dict:
    key = jax.random.key(seed)
    ks = jax.random.split(key, 24)
    f32 = jnp.float32
    nrm = lambda k, shape, s: s * jax.random.normal(k, shape, f32)
    gain = lambda k, shape: 1.0 + 0.05 * jax.random.normal(k, shape, f32)
    return {
        'x_prompt': nrm(ks[0], (BATCH, SEQ, D_MODEL), 1.0),
        'x_sample': nrm(ks[1], (DEC_BATCH, DEC_SEQ, D_MODEL), 1.0),
        'norm_mix_pre': gain(ks[2], (DEPTH, D_MODEL)),
        'norm_mix_post': gain(ks[3], (DEPTH, D_MODEL)),
        'norm_ffn_pre': gain(ks[4], (DEPTH, D_MODEL)),
        'norm_ffn_post': gain(ks[5], (DEPTH, D_MODEL)),
        'w_in': nrm(ks[6], (DEPTH, D_MODEL, IN_COLS), D_MODEL ** -0.5),
        'na_rpb': nrm(ks[7], (DEPTH, NA_HEADS, 2 * NA_WIN_R - 1, 2 * NA_WIN_C - 1), 0.1),
        'gqa_q_norm': gain(ks[8], (DEPTH, HEAD_DIM)),
        'gqa_k_norm': gain(ks[9], (DEPTH, HEAD_DIM)),
        'diff_lambda_q1': nrm(ks[10], (DEPTH, DIFF_QK_DIM), 0.1),
        'diff_lambda_k1': nrm(ks[11], (DEPTH, DIFF_QK_DIM), 0.1),
        'diff_lambda_q2': nrm(ks[12], (DEPTH, DIFF_QK_DIM), 0.1),
        'diff_lambda_k2': nrm(ks[13], (DEPTH, DIFF_QK_DIM), 0.1),
        'diff_subln': gain(ks[14], (DEPTH, DIFF_V_DIM)),
        'swa_sink': nrm(ks[15], (DEPTH, SWA_HEADS), 0.5),
        'w_branch': nrm(ks[16], (DEPTH, MIX_W, D_MODEL), BRANCH_W ** -0.5),
        'w_out': nrm(ks[17], (DEPTH, D_MODEL, D_MODEL), D_MODEL ** -0.5),
        'ffn_w_up': nrm(ks[18], (DEPTH, D_MODEL, 2 * D_FF), D_MODEL ** -0.5),
        'ffn_conv_w': nrm(ks[19], (DEPTH, CONV_W, 2 * D_FF), 0.5),
        'ffn_conv_b': nrm(ks[20], (DEPTH, 2 * D_FF), 0.01),
        'ffn_w_down': nrm(ks[21], (DEPTH, D_FF, D_MODEL), D_FF ** -0.5),
    }


def reference(x_prompt, x_sample, norm_mix_pre, norm_mix_post, norm_ffn_pre, norm_ffn_post,
              w_in, na_rpb, gqa_q_norm, gqa_k_norm, diff_lambda_q1, diff_lambda_k1,
              diff_lambda_q2, diff_lambda_k2, diff_subln, swa_sink, w_branch, w_out,
              ffn_w_up, ffn_conv_w, ffn_conv_b, ffn_w_down):
    diff_slopes, swa_slopes = _alibi_slopes()
    split_points = tuple(int(s) for s in np.cumsum(IN_SIZES)[:-1])

    def run_trunk(x):
        B, S, _ = x.shape
        cos_r, sin_r, cos_c, sin_c = _axial_rope_tables(S)
        heads = lambda t, n: t.reshape(B, S, n, HEAD_DIM)
        for l in range(DEPTH):
            lambda_init = 0.8 - 0.6 * math.exp(-0.3 * l)
            h = _rmsnorm(x, norm_mix_pre[l])
            proj = jnp.einsum('bsd,de->bse', h, w_in[l])
            (a_q, a_k, a_v, b_q, b_k, b_v, c_q, c_k, c_v,
             d_q, d_k, d_v, g) = jnp.split(proj, split_points, axis=-1)
            o_a = _neighbourhood_attention(heads(a_q, NA_HEADS), heads(a_k, NA_HEADS),
                                           heads(a_v, NA_HEADS), na_rpb[l])
            qb = _apply_axial_rope(_rmsnorm(heads(b_q, GQA_HEADS), gqa_q_norm[l]), cos_r, sin_r, cos_c, sin_c)
            kb = _apply_axial_rope(_rmsnorm(heads(b_k, GQA_KV_HEADS), gqa_k_norm[l]), cos_r, sin_r, cos_c, sin_c)
            o_b = _dense_gqa(qb, kb, heads(b_v, GQA_KV_HEADS))
            lam = (jnp.exp(jnp.sum(diff_lambda_q1[l].astype(jnp.float32) * diff_lambda_k1[l].astype(jnp.float32)))
                   - jnp.exp(jnp.sum(diff_lambda_q2[l].astype(jnp.float32) * diff_lambda_k2[l].astype(jnp.float32)))
                   + lambda_init)
            o_c = _diff_attention(c_q.reshape(B, S, DIFF_HEADS, 2, DIFF_QK_DIM),
                                  c_k.reshape(B, S, DIFF_HEADS, 2, DIFF_QK_DIM),
                                  c_v.reshape(B, S, DIFF_HEADS, DIFF_V_DIM), lam, diff_slopes)
            o_c = (_rmsnorm(o_c, diff_subln[l]) * (1.0 - lambda_init)).reshape(B, S, BRANCH_W)
            o_d = _window_attention(heads(d_q, SWA_HEADS), heads(d_k, SWA_KV_HEADS),
                                    heads(d_v, SWA_KV_HEADS), swa_sink[l], swa_slopes)
            gates = jax.nn.sigmoid(g.reshape(B, S, N_BRANCH, D_MODEL))
            wb = w_branch[l].reshape(N_BRANCH, BRANCH_W, D_MODEL)
            branches = (o_a, o_b, o_c, o_d)
            merged = gates[:, :, 0] * jnp.einsum('bsc,cd->bsd', branches[0], wb[0])
            for i in range(1, N_BRANCH):
                merged = merged + gates[:, :, i] * jnp.einsum('bsc,cd->bsd', branches[i], wb[i])
            mix = jnp.einsum('bsd,de->bse', merged, w_out[l])
            x = x + _rmsnorm(mix, norm_mix_post[l])
            hf = _rmsnorm(x, norm_ffn_pre[l])
            f = _conv_ffn(hf, ffn_w_up[l], ffn_conv_w[l], ffn_conv_b[l], ffn_w_down[l])
            x = x + _rmsnorm(f, norm_ffn_post[l])
        return x

    y_prompt = run_trunk(x_prompt)
    y_sample = run_trunk(x_sample)
    return (y_prompt, y_sample)
```

```python
import math
from contextlib import ExitStack
import numpy as np
import concourse.bass as bass
import concourse.mybir as mybir
from concourse.bass_utils import run_bass_kernel_spmd

F32 = mybir.dt.float32
BF16 = mybir.dt.bfloat16
AF = mybir.ActivationFunctionType
ALU = mybir.AluOpType
AX = mybir.AxisListType

COMPUTE = ("pe", "act", "dve", "pool")
ALLENG = COMPUTE + ("sp",)
NCHAINS = 48
EPS = 1e-6
NEG = -1e30


class Buf:
    __slots__ = ("w", "r")

    def __init__(self):
        self.w = None
        self.r = []


class Op:
    __slots__ = ("eng", "fn", "waits", "signal", "pos", "is_dma", "chain", "dma_val", "cnt")

    def __init__(self, eng, fn):
        self.eng = eng
        self.fn = fn
        self.waits = []
        self.signal = False
        self.pos = -1
        self.is_dma = False
        self.chain = None
        self.dma_val = 0
        self.cnt = 0


class Chain:
    __slots__ = ("sem", "count", "last")

    def __init__(self, sem):
        self.sem = sem
        self.count = 0
        self.last = None


class Prog:
    def __init__(self, nc):
        self.nc = nc
        self.ops = {e: [] for e in ALLENG}
        self.sem = {e: nc.alloc_semaphore(name="s_" + e) for e in ALLENG}
        self.known = {e: {p: -1 for p in ALLENG} for e in ALLENG}
        self.known_chain = {e: {} for e in ALLENG}
        self.chains = [Chain(nc.alloc_semaphore(name=f"c{i}")) for i in range(NCHAINS)]
        self.ci = 0
        self.nops = 0

    def _add_wait(self, op, d):
        E = op.eng
        if d.is_dma:
            kc = self.known_chain[E]
            if kc.get(id(d.chain), 0) >= d.dma_val:
                return
            kc[id(d.chain)] = d.dma_val
            op.waits.append(d)
        else:
            if d.eng == E and E == "pe":
                return
            kn = self.known[E]
            if kn[d.eng] >= d.pos:
                return
            kn[d.eng] = d.pos
            d.signal = True
            op.waits.append(d)

    def _deps(self, op, reads, writes):
        for b in reads:
            if b.w is not None:
                self._add_wait(op, b.w)
        for b in writes:
            if b.w is not None:
                self._add_wait(op, b.w)
            for r in b.r:
                self._add_wait(op, r)
        for b in reads:
            b.r.append(op)
        for b in writes:
            b.w = op
            b.r = []

    def op(self, eng, fn, reads=(), writes=()):
        o = Op(eng, fn)
        o.pos = len(self.ops[eng])
        self._deps(o, reads, writes)
        self.ops[eng].append(o)
        self.nops += 1
        return o

    def dma(self, fn, reads=(), writes=(), queue="sp"):
        ch = self.chains[self.ci]
        self.ci = (self.ci + 1) % NCHAINS
        o = Op(queue, fn)
        o.is_dma = True
        o.chain = ch
        o.pos = len(self.ops[queue])
        if ch.last is not None:
            self._add_wait(o, ch.last)
        self._deps(o, reads, writes)
        ch.count += 16
        o.dma_val = ch.count
        ch.last = o
        self.ops[queue].append(o)
        self.nops += 1
        return o

    def barrier(self):
        sp_wait = []
        for e in COMPUTE:
            if self.ops[e]:
                last = self.ops[e][-1]
                if last.is_dma or last.fn is None:
                    last = self.op(e, lambda en: en.nop())
                last.signal = True
                sp_wait.append(last)
        o = Op("sp", lambda en: en.nop())
        o.pos = len(self.ops["sp"])
        o.waits = sp_wait + [c.last for c in self.chains if c.last is not None]
        o.signal = True
        self.ops["sp"].append(o)
        for e in COMPUTE:
            w = Op(e, None)
            w.pos = len(self.ops[e])
            w.waits = [o]
            self.ops[e].append(w)
        for e in ALLENG:
            for q in ALLENG:
                self.known[e][q] = len(self.ops[q]) - 1
            for c in self.chains:
                self.known_chain[e][id(c)] = c.count

    def emit(self):
        nc = self.nc
        for e in ALLENG:
            c = 0
            for o in self.ops[e]:
                if o.signal and not o.is_dma:
                    c += 1
                o.cnt = c
        sem = self.sem

        def run(e, eng):
            for o in self.ops[e]:
                for d in o.waits:
                    if d.is_dma:
                        eng.wait_ge(d.chain.sem, d.dma_val)
                    else:
                        eng.wait_ge(sem[d.eng], d.cnt)
                if o.fn is None:
                    continue
                ins = o.fn(eng)
                if o.is_dma:
                    ins.then_inc(o.chain.sem, 16)
                elif o.signal:
                    ins.then_inc(sem[e], 1)

        with nc.Block() as block:
            @block.tensor
            def _(eng):
                run("pe", eng)

            @block.scalar
            def _(eng):
                run("act", eng)

            @block.vector
            def _(eng):
                run("dve", eng)

            @block.gpsimd
            def _(eng):
                run("pool", eng)

            @block.sync
            def _(eng):
                run("sp", eng)


class Ring:
    def __init__(self, tiles):
        self.t = tiles
        self.b = [Buf() for _ in tiles]
        self.i = -1

    def next(self):
        self.i = (self.i + 1) % len(self.t)
        return self.t[self.i], self.b[self.i]


D = 1024
NPJ = 2560
QK_SRC = [0, 128, 256, 384, 768, 896, 1024, 1280, 1408, 1536, 1664, 2048, 2176, 2304]
MIX = {
    "A": dict(qb=(0, 1), kb=(2, 3), vc=512, vw=256, nkv=4, idx=0),
    "B": dict(qb=(4, 5), kb=(6,), vc=1152, vw=128, nkv=2, idx=1),
    "C": dict(qb=(7, 8), kb=(9, 10), vc=1792, vw=256, nkv=4, idx=2),
    "D": dict(qb=(11, 12), kb=(13,), vc=2432, vw=128, nkv=2, idx=3),
}
_SL = 2.0 ** (-8.0 * np.arange(1, 9) / 8)
DIFF_SLOPES = [float(v) for v in _SL[0::2]]
SWA_SLOPES = [float(v) for v in _SL[1::2]]
NA_CLASSES = 5
NA_SLABS = 21


def na_tiles(m, T):
    if T <= 4:
        raise NotImplementedError
    if m == 0:
        return [(j, 5 + j) for j in range(4)]
    if m == 1:
        return [(j, 9 + j) for j in range(4)]
    if m == T - 2:
        return [(T - 4 + i, 13 + i) for i in range(4)]
    if m == T - 1:
        return [(T - 4 + i, 17 + i) for i in range(4)]
    return [(m - 2 + i, i) for i in range(5)]


def _na_slab_defs():
    T = 16
    reps = {0: 8, 1: 0, 2: 1, 3: T - 2, 4: T - 1}
    out = [None] * NA_SLABS
    for cls, m in reps.items():
        for (j, slab) in na_tiles(m, T):
            out[slab] = (m, j, T)
    return out


def build_abias(rpb):
    defs = _na_slab_defs()
    res = np.full((128, NA_SLABS, 4, 128), NEG, np.float32)
    kk = np.arange(128)
    qq = np.arange(128)
    for slab, (m, j, T) in enumerate(defs):
        rows = 2 * T
        qr = 2 * m + qq // 64
        qc = qq % 64
        kr = 2 * j + kk // 64
        kc = kk % 64
        r0 = np.clip(qr - 4, 0, rows - 8)
        c0 = np.clip(qc - 8, 0, 64 - 16)
        vr = (kr[:, None] >= r0[None, :]) & (kr[:, None] < r0[None, :] + 8)
        vc = (kc[:, None] >= c0[None, :]) & (kc[:, None] < c0[None, :] + 16)
        valid = vr & vc
        dr = np.clip(kr[:, None] - qr[None, :] + 7, 0, 14)
        dc = np.clip(kc[:, None] - qc[None, :] + 15, 0, 30)
        for hp, h in enumerate((0, 2, 1, 3)):
            g = rpb[h][dr, dc]
            res[:, slab, hp, :] = np.where(valid, g, np.float32(NEG))
    return res


def build_dbias():
    res = np.full((128, 3, 4, 128), NEG, np.float32)
    kk = np.arange(128)[:, None]
    qq = np.arange(128)[None, :]
    for di, dl in enumerate((-1, 0, 1)):
        rel = np.abs(dl * 128 + kk - qq)
        for hp, h in enumerate((0, 2, 1, 3)):
            res[:, di, hp, :] = np.where(rel <= 128, -np.float32(SWA_SLOPES[h]) * rel.astype(np.float32), np.float32(NEG))
    return res


def rope_tables(S):
    t = np.arange(S)
    row = (t // 64).astype(np.float32)
    col = (t % 64).astype(np.float32)
    inv = (10000.0 ** (-np.arange(0, 32, 2, dtype=np.float32) / 32)).astype(np.float32)
    ar = row[:, None] * inv
    ac = col[:, None] * inv
    cr, sr, cc, sc = np.cos(ar), np.sin(ar), np.cos(ac), np.sin(ac)
    cosf = np.concatenate([cr, cr, cc, cc], 1).astype(np.float32)
    sinf = np.concatenate([-sr, sr, -sc, sc], 1).astype(np.float32)
    return cosf, sinf


def build(seqs, depth, dbg=False):
    NT = sum(seqs)
    NTILES = NT // 128
    SMAX = max(seqs)
    TMAX = SMAX // 128
    seq_start = [sum(seqs[:i]) for i in range(len(seqs))]
    nc = bass.Bass("TRN2", target_bir_lowering=False)

    def din(name, shape, dt=F32):
        return nc.dram_tensor(name, list(shape), dt, kind="ExternalInput").ap()

    x_in = din("x", [NT, D])
    w_in = din("w_in", [depth, 128, 8, 6656])
    w_br = din("w_br", [depth, 128, 8, D])
    w_out = din("w_out", [depth, 128, 8, D])
    w_up = din("w_up", [depth, 128, 8, 5632])
    w_dn = din("w_dn", [depth, 128, 22, D])
    gT_pre = din("gT_pre", [depth, 128, 16])
    g_post = din("g_post", [depth, 2, D])
    convp = din("convp", [depth, 128, 44, 4])
    ropec = din("ropec", [SMAX, 128])
    gqk = din("gqk", [depth, 384])
    abias = din("abias", [depth, 128, NA_SLABS * 512])
    dbias = din("dbias", [128, 3 * 512])
    cdist = din("cdist", [128, 5 * 512])
    lamv = din("lamv", [depth, 128])
    subln = din("subln", [depth, 64, 1])
    sink = din("sink", [depth, 4])
    y = nc.dram_tensor("y", [NT, D], F32, kind="ExternalOutput").ap()
    pj_d = nc.dram_tensor("pj_s", [NT, NPJ], BF16).ap()
    qkT_d = nc.dram_tensor("qkT_s", [14 * 128, NT], BF16).ap()
    g_d = nc.dram_tensor("g_s", [NT, 4096], BF16).ap()
    oT_d = nc.dram_tensor("oT_s", [D, NT], BF16).ap()
    hT_d = nc.dram_tensor("hT_s", [D, NT], BF16).ap()
    dbg_o = {}
    if dbg:
        dbg_o["pj"] = nc.dram_tensor("dbg_pj", [NT, NPJ], BF16, kind="ExternalOutput").ap()
        dbg_o["oT"] = nc.dram_tensor("dbg_oT", [D, NT], BF16, kind="ExternalOutput").ap()
        dbg_o["x1"] = nc.dram_tensor("dbg_x1", [NT, D], F32, kind="ExternalOutput").ap()

    p = Prog(nc)
    qkT_r = qkT_d.rearrange("(b p) n -> p b n", p=128)
    oT_c = oT_d.rearrange("(c p) n -> p c n", p=128)
    oT_h = oT_d.rearrange("(h d) n -> d h n", d=64)
    hT_c = hT_d.rearrange("(c p) n -> p c n", p=128)

    with ExitStack() as top:
        uid = [0]

        def mk(es):
            def sb(shape, dt):
                uid[0] += 1
                return es.enter_context(nc.sbuf_tensor(f"sb{uid[0]}", list(shape), dt))

            def ps(shape, dt):
                uid[0] += 1
                return es.enter_context(nc.psum_tensor(f"ps{uid[0]}", list(shape), dt))
            return sb, ps

        sbT, _ = mk(top)
        ident = sbT([128, 128], BF16)
        identf = sbT([128, 128], F32)
        onesf = sbT([128, 64], F32)
        b_const = Buf()
        p.op("pool", lambda e: e.memset(identf[:], 0.0), writes=[b_const])
        p.op("pool", lambda e: e.affine_select(out=identf[:], in_=identf[:], compare_op=ALU.not_equal,
                                               fill=1.0, base=0, pattern=[[-1, 128]], channel_multiplier=1),
             writes=[b_const])
        p.op("pool", lambda e: e.tensor_copy(ident[:], identf[:]), writes=[b_const])
        p.op("pool", lambda e: e.memset(onesf[:], 1.0), writes=[b_const])

        def phase1(l):
            src = x_in if l == 0 else y
            with ExitStack() as es:
                sb, ps = mk(es)
                W = sb([128, 8, 6656], BF16)
                bW = [Buf() for _ in range(8)]
                for k in range(8):
                    p.dma(lambda e, k=k: e.dma_start(out=W[:, k, :], in_=w_in[l, :, k, :]), writes=[bW[k]], queue="pool")
                gT = sb([128, 16], F32)
                gqkt = sb([128, 384], F32)
                bP = Buf()
                p.dma(lambda e: e.dma_start(out=gT[:], in_=gT_pre[l]), writes=[bP])
                p.dma(lambda e: e.dma_start(out=gqkt[:], in_=gqk[l:l + 1, :].to_broadcast([128, 384])), writes=[bP])
                xin_r = Ring([sb([128, D], F32) for _ in range(2)])
                xbf_r = Ring([sb([128, D], BF16) for _ in range(2)])
                junk = sb([128, D], BF16)
                bjunk = Buf()
                st_r = Ring([sb([128, 8], F32) for _ in range(2)])
                xT_r = Ring([sb([128, 8, 128], BF16) for _ in range(2)])
                pj_r = Ring([sb([128, NPJ], BF16) for _ in range(2)])
                gt_r = Ring([sb([128, 4096], BF16) for _ in range(2)])
                bq_r = Ring([sb([128, 384], F32) for _ in range(2)])
                wk_r = Ring([sb([128, 5, 384], F32) for _ in range(2)])
                sb6_r = Ring([sb([128, 16], F32) for _ in range(2)])
                rp_r = Ring([sb([128, 128], F32) for _ in range(2)])
                qk_r = Ring([sb([128, 14, 128], BF16) for _ in range(2)])
                pT_r = Ring([ps([128, 8, 128], BF16) for _ in range(1)])
                po_r = Ring([ps([128, 512], F32) for _ in range(5)])
                pq_r = Ring([ps([128, 16, 128], BF16) for _ in range(1)])
                for t in range(NTILES):
                    si = max(i for i in range(len(seqs)) if seq_start[i] <= t * 128)
                    pos = t * 128 - seq_start[si]
                    r0 = t * 128
                    xin, bxin = xin_r.next()
                    p.dma(lambda e, xin=xin, r0=r0: e.dma_start(out=xin[:], in_=src[r0:r0 + 128, :]), writes=[bxin])
                    rp, brp = rp_r.next()
                    p.dma(lambda e, rp=rp, pos=pos: e.dma_start(out=rp[:], in_=ropec[pos:pos + 128, :]), writes=[brp])
                    st, bst = st_r.next()
                    p.op("pool", lambda e, st=st: e.memset(st[:], 0.0), writes=[bst])
                    p.op("act", lambda e, xin=xin, st=st: e.activation(junk[:], xin[:], AF.Square, accum_out=st[:, 0:1]),
                         reads=[bxin], writes=[bst, bjunk])
                    p.op("act", lambda e, st=st: e.activation(st[:, 1:2], st[:, 0:1], AF.Sqrt, bias=EPS, scale=1.0 / D),
                         reads=[bst], writes=[bst])
                    p.op("dve", lambda e, st=st: e.reciprocal(st[:, 2:3], st[:, 1:2]), reads=[bst], writes=[bst])
                    xbf, bxbf = xbf_r.next()
                    p.op("dve", lambda e, xbf=xbf, xin=xin: e.tensor_copy(xbf[:], xin[:]), reads=[bxin], writes=[bxbf])
                    pT, bpT = pT_r.next()
                    for k in range(8):
                        p.op("pe", lambda e, pT=pT, xbf=xbf, k=k: e.transpose(pT[:, k, :], xbf[:, k * 128:(k + 1) * 128], ident[:]),
                             reads=[bxbf, b_const], writes=[bpT])
                    xT, bxT = xT_r.next()
                    p.op("dve", lambda e, xT=xT, pT=pT: e.tensor_tensor(xT[:], pT[:], gT[:, 0:8].unsqueeze(2).to_broadcast([128, 8, 128]), ALU.mult),
                         reads=[bpT, bP], writes=[bxT])
                    pj, bpj = pj_r.next()
                    gt, bgt = gt_r.next()
                    bq, bbq = bq_r.next()
                    rs = st[:, 2:3]
                    for n in range(13):
                        po, bpo = po_r.next()
                        for k in range(8):
                            p.op("pe", lambda e, po=po, xT=xT, k=k, n=n: e.matmul(po[:], lhsT=xT[:, k, :], rhs=W[:, k, n * 512:(n + 1) * 512],
                                                                             start=(k == 0), stop=(k == 7)),
                                 reads=[bxT, bW[k]], writes=[bpo])
                        if n >= 5:
                            p.op("act", lambda e, po=po, gt=gt, n=n, rs=rs: e.activation(gt[:, (n - 5) * 512:(n - 4) * 512], po[:], AF.Sigmoid, scale=rs),
                                 reads=[bpo, bst], writes=[bgt])
                        elif n == 1:
                            p.op("dve", lambda e, po=po, pj=pj, rs=rs: e.tensor_scalar(pj[:, 512:768], po[:, 0:256], rs, None, op0=ALU.mult),
                                 reads=[bpo, bst], writes=[bpj])
                            p.op("dve", lambda e, po=po, bq=bq, rs=rs: e.tensor_scalar(bq[:, 0:256], po[:, 256:512], rs, None, op0=ALU.mult),
                                 reads=[bpo, bst], writes=[bbq])
                        elif n == 2:
                            p.op("dve", lambda e, po=po, bq=bq, rs=rs: e.tensor_scalar(bq[:, 256:384], po[:, 0:128], rs, None, op0=ALU.mult),
                                 reads=[bpo, bst], writes=[bbq])
                            p.op("dve", lambda e, po=po, pj=pj, rs=rs: e.tensor_scalar(pj[:, 1152:1536], po[:, 128:512], rs, None, op0=ALU.mult),
                                 reads=[bpo, bst], writes=[bpj])
                        else:
                            p.op("dve", lambda e, po=po, pj=pj, rs=rs, n=n: e.tensor_scalar(pj[:, n * 512:(n + 1) * 512], po[:], rs, None, op0=ALU.mult),
                                 reads=[bpo, bst], writes=[bpj])
                    wk, bwk = wk_r.next()
                    s6, bs6 = sb6_r.next()
                    bq3 = bq[:].rearrange("p (h d) -> p h d", d=64)
                    sq3 = wk[:, 0, :].rearrange("p (h d) -> p h d", d=64)
                    qn = wk[:, 1, :]
                    qn3 = qn.rearrange("p (h d) -> p h d", d=64)
                    qn4 = qn.rearrange("p (a b c) -> p a b c", b=2, c=16)
                    rot4 = wk[:, 2, :].rearrange("p (a b c) -> p a b c", b=2, c=16)
                    rot3 = wk[:, 2, :].rearrange("p (h d) -> p h d", d=64)
                    t13 = wk[:, 3, :].rearrange("p (h d) -> p h d", d=64)
                    t23 = wk[:, 4, :].rearrange("p (h d) -> p h d", d=64)
                    p.op("pool", lambda e, wk=wk, bq=bq: e.tensor_tensor(wk[:, 0, :], bq[:], bq[:], ALU.mult), reads=[bbq], writes=[bwk])
                    p.op("dve", lambda e, s6=s6, sq3=sq3: e.tensor_reduce(s6[:, 0:6], sq3, axis=AX.X, op=ALU.add), reads=[bwk], writes=[bs6])
                    p.op("act", lambda e, s6=s6: e.activation(s6[:, 6:12], s6[:, 0:6], AF.Sqrt, bias=EPS, scale=1.0 / 64), reads=[bs6], writes=[bs6])
                    p.op("dve", lambda e, s6=s6: e.reciprocal(s6[:, 0:6], s6[:, 6:12]), reads=[bs6], writes=[bs6])
                    p.op("dve", lambda e, qn3=qn3, bq3=bq3, s6=s6: e.tensor_tensor(qn3, bq3, s6[:, 0:6].unsqueeze(2).to_broadcast([128, 6, 64]), ALU.mult),
                         reads=[bbq, bs6], writes=[bwk])
                    p.op("pool", lambda e, qn=qn: e.tensor_tensor(qn, qn, gqkt[:], ALU.mult), reads=[bwk, bP], writes=[bwk])
                    p.op("pool", lambda e, rot4=rot4, qn4=qn4: e.tensor_copy(rot4[:, :, 0, :], qn4[:, :, 1, :]), reads=[bwk], writes=[bwk])
                    p.op("pool", lambda e, rot4=rot4, qn4=qn4: e.tensor_copy(rot4[:, :, 1, :], qn4[:, :, 0, :]), reads=[bwk], writes=[bwk])
                    p.op("dve", lambda e, t13=t13, qn3=qn3, rp=rp: e.tensor_tensor(t13, qn3, rp[:, 0:64].unsqueeze(1).to_broadcast([128, 6, 64]), ALU.mult),
                         reads=[bwk, brp], writes=[bwk])
                    p.op("pool", lambda e, t23=t23, rot3=rot3, rp=rp: e.tensor_tensor(t23, rot3, rp[:, 64:128].unsqueeze(1).to_broadcast([128, 6, 64]), ALU.mult),
                         reads=[bwk, brp], writes=[bwk])
                    p.op("dve", lambda e, pj=pj, wk=wk: e.tensor_tensor(pj[:, 768:1152], wk[:, 3, :], wk[:, 4, :], ALU.add),
                         reads=[bwk], writes=[bpj])
                    pq, bpq = pq_r.next()
                    for bi, c0 in enumerate(QK_SRC):
                        p.op("pe", lambda e, pq=pq, pj=pj, bi=bi, c0=c0: e.transpose(pq[:, bi, :], pj[:, c0:c0 + 128], ident[:]),
                             reads=[bpj, b_const], writes=[bpq])
                    qk, bqk = qk_r.next()
                    p.op("act", lambda e, qk=qk, pq=pq: e.copy(qk[:], pq[:, 0:14, :]), reads=[bpq], writes=[bqk])
                    p.dma(lambda e, pj=pj, r0=r0: e.dma_start(out=pj_d[r0:r0 + 128, :], in_=pj[:]), reads=[bpj], queue="pool")
                    p.dma(lambda e, gt=gt, r0=r0: e.dma_start(out=g_d[r0:r0 + 128, :], in_=gt[:]), reads=[bgt], queue="pool")
                    p.dma(lambda e, qk=qk, r0=r0: e.dma_start(out=qkT_r[:, :, r0:r0 + 128], in_=qk[:]), reads=[bqk], queue="pool")
                    if dbg and l == 0:
                        p.dma(lambda e, pj=pj, r0=r0: e.dma_start(out=dbg_o["pj"][r0:r0 + 128, :], in_=pj[:]), reads=[bpj], queue="pool")
            p.barrier()

        def phase2(l):
            lambda_init = 0.8 - 0.6 * math.exp(-0.3 * l)
            with ExitStack() as es:
                sb, ps = mk(es)
                Qt = [sb([128, SMAX], BF16) for _ in range(2)]
                Kt = [sb([128, SMAX], BF16) for _ in range(2)]
                bQ = [Buf(), Buf()]
                bK = [Buf(), Buf()]
                Vraw = sb([128, TMAX, 256], BF16)
                bVraw = Buf()
                Va = [sb([128, TMAX, 65], BF16) for _ in range(4)]
                bVa = [Buf() for _ in range(4)]
                ab = sb([128, NA_SLABS * 512], BF16)
                db = sb([128, 3 * 512], F32)
                cd = sb([128, 5 * 512], F32)
                sm = sb([128, 64], F32)
                gcol = sb([64, 2], F32)
                bpar = Buf()
                p.dma(lambda e: e.dma_start(out=ab[:], in_=abias[l]), writes=[bpar], queue="pool")
                p.dma(lambda e: e.dma_start(out=db[:], in_=dbias), writes=[bpar])
                p.dma(lambda e: e.dma_start(out=cd[:], in_=cdist), writes=[bpar])
                bsm = Buf()
                p.op("dve", lambda e: e.memset(sm[:], 0.0), writes=[bsm])
                p.dma(lambda e: e.dma_start(out=sm[64:65, 0:4], in_=sink[l:l + 1, :]), writes=[bsm])
                lam4 = sb([128, 128], F32)
                p.op("dve", lambda e: e.memset(lam4[:], 0.0), writes=[bsm])
                p.dma(lambda e: e.dma_start(out=lam4[64:65, :], in_=lamv[l:l + 1, :]), writes=[bsm])
                p.dma(lambda e: e.dma_start(out=gcol[:, 0:1], in_=subln[l]), writes=[bsm])
                p.op("act", lambda e: e.activation(sm[64:65, 4:8], sm[64:65, 0:4], AF.Exp), reads=[bsm], writes=[bsm])
                p.op("dve", lambda e: e.tensor_tensor(lam4[64:65, 0:32], lam4[64:65, 0:32], lam4[64:65, 32:64], ALU.mult), reads=[bsm], writes=[bsm])
                p.op("dve", lambda e: e.tensor_tensor(lam4[64:65, 64:96], lam4[64:65, 64:96], lam4[64:65, 96:128], ALU.mult), reads=[bsm], writes=[bsm])
                p.op("dve", lambda e: e.tensor_reduce(sm[64:65, 8:9], lam4[64:65, 0:32], axis=AX.X, op=ALU.add), reads=[bsm], writes=[bsm])
                p.op("dve", lambda e: e.tensor_reduce(sm[64:65, 9:10], lam4[64:65, 64:96], axis=AX.X, op=ALU.add), reads=[bsm], writes=[bsm])
                p.op("act", lambda e: e.activation(sm[64:65, 10:12], sm[64:65, 8:10], AF.Exp), reads=[bsm], writes=[bsm])
                p.op("dve", lambda e: e.tensor_tensor(sm[64:65, 12:13], sm[64:65, 11:12], sm[64:65, 10:11], ALU.subtract), reads=[bsm], writes=[bsm])
                p.op("dve", lambda e: e.tensor_scalar(sm[64:65, 13:14], sm[64:65, 12:13], -lambda_init, None, op0=ALU.add), reads=[bsm], writes=[bsm])
                nlam = sm[64:65, 13:14]
                p.op("act", lambda e: e.mul(gcol[:, 1:2], gcol[:, 0:1], 1.0 - lambda_init), reads=[bsm], writes=[bsm])

                cbt = sb([128, 512], F32)
                cbt_map = {}
                bcbt = Buf()
                for S_ in sorted(set(seqs)):
                    for qb_ in range(S_ // 512):
                        for kt_ in range(S_ // 128):
                            dl_ = qb_ * 512 - kt_ * 128
                            for h_ in range(4):
                                v_ = float(-DIFF_SLOPES[h_] * abs(dl_)) if (dl_ >= 128 or dl_ <= -512) else 0.0
                                if v_ not in cbt_map:
                                    cbt_map[v_] = len(cbt_map)
                assert len(cbt_map) <= 512
                for v_, idx_ in cbt_map.items():
                    p.op("pool", lambda e, idx_=idx_, v_=v_: e.memset(cbt[:, idx_:idx_ + 1], v_), writes=[bcbt])

                def bias_col(v):
                    i = cbt_map[float(v)]
                    return cbt[:, i:i + 1]

                sbank_r = Ring([ps([128, 512], F32) for _ in range(3)])
                acc_r = Ring([ps([128, 512], F32) for _ in range(4)])
                fin_r = Ring([ps([128, 512], F32) for _ in range(1)])
                tmp_r = Ring([sb([128, 512], F32) for _ in range(3)])
                pt_r = Ring([sb([128, 512], BF16) for _ in range(12)])
                rl_r = Ring([sb([128, 512], F32) for _ in range(2)])
                bcs_r = Ring([sb([64, 512], F32) for _ in range(2)])
                ot_r = Ring([sb([64, 512], BF16) for _ in range(3)])
                f32w_r = Ring([sb([64, 512], F32) for _ in range(6)])

                def load_mixer(name, si):
                    S = seqs[si]
                    T = S // 128
                    s0 = seq_start[si]
                    mx = MIX[name]
                    for i, blk in enumerate(mx["qb"]):
                        p.dma(lambda e, i=i, blk=blk: e.dma_start(out=Qt[i][:, 0:S], in_=qkT_r[:, blk, s0:s0 + S]), writes=[bQ[i]])
                    if len(mx["kb"]) == 2:
                        for i, blk in enumerate(mx["kb"]):
                            p.dma(lambda e, i=i, blk=blk: e.dma_start(out=Kt[i][:, 0:S], in_=qkT_r[:, blk, s0:s0 + S]), writes=[bK[i]])
                    else:
                        blk = mx["kb"][0]
                        for j in range(2):
                            for half in range(2):
                                p.dma(lambda e, j=j, half=half: e.dma_start(out=Kt[j][half * 64:(half + 1) * 64, 0:S],
                                                                            in_=qkT_r[j * 64:(j + 1) * 64, blk, s0:s0 + S]), writes=[bK[j]])
                    vw = mx["vw"]
                    p.dma(lambda e: e.dma_start(out=Vraw[:, 0:T, 0:vw],
                                                in_=pj_d[s0:s0 + S, mx["vc"]:mx["vc"] + vw].rearrange("(t p) c -> p t c", p=128)),
                          writes=[bVraw])
                    for j in range(mx["nkv"]):
                        p.op("pool", lambda e, j=j: e.tensor_copy(Va[j][:, 0:T, 0:64], Vraw[:, 0:T, j * 64:(j + 1) * 64]),
                             reads=[bVraw], writes=[bVa[j]])
                        p.op("pool", lambda e, j=j: e.memset(Va[j][:, 0:T, 64:65], 1.0), writes=[bVa[j]])

                def finalize_simple(acc, bacc, ncols, dest_fn, add_sink=False, split4=False):
                    rl, brl = rl_r.next()
                    if add_sink:
                        for h in range(4):
                            p.op("dve", lambda e, h=h: e.tensor_scalar(rl[64:65, h * 128:(h + 1) * 128], acc[64:65, h * 128:(h + 1) * 128],
                                                                      sm[64:65, 4 + h:5 + h], None, op0=ALU.add), reads=[bacc, bsm], writes=[brl])
                        p.op("dve", lambda e: e.reciprocal(rl[64:65, 0:ncols], rl[64:65, 0:ncols]), reads=[brl], writes=[brl])
                    else:
                        p.op("dve", lambda e: e.reciprocal(rl[64:65, 0:ncols], acc[64:65, 0:ncols]), reads=[bacc], writes=[brl])
                    fin, bfin = fin_r.next()
                    p.op("pe", lambda e: e.matmul(fin[0:64, 0:ncols], lhsT=onesf[64:65, 0:64], rhs=rl[64:65, 0:ncols], start=True, stop=True),
                         reads=[brl, b_const], writes=[bfin])
                    bcs, bbcs = bcs_r.next()
                    p.op("act", lambda e: e.copy(bcs[:, 0:ncols], fin[0:64, 0:ncols]), reads=[bfin], writes=[bbcs])
                    ot, bot = ot_r.next()
                    p.op("dve", lambda e: e.tensor_tensor(ot[:, 0:ncols], acc[0:64, 0:ncols], bcs[:, 0:ncols], ALU.mult),
                         reads=[bacc, bbcs], writes=[bot])
                    if split4:
                        for h4 in range(4):
                            p.dma(lambda e, h4=h4: e.dma_start(out=dest_fn(h4), in_=ot[:, h4 * 128:(h4 + 1) * 128]), reads=[bot], queue="pool")
                    else:
                        p.dma(lambda e: e.dma_start(out=dest_fn(), in_=ot[:, 0:ncols]), reads=[bot], queue="pool")
                    return ot

                def local_attn(name, si):
                    import os
                    LSK = os.environ.get("MK_LSK", "")
                    S = seqs[si]
                    T = S // 128
                    s0 = seq_start[si]
                    mx = MIX[name]
                    isA = name == "A"
                    gqa = not isA

                    def keyt(m):
                        if isA:
                            return na_tiles(m, T)
                        return [(m + dl, dl + 1) for dl in (-1, 0, 1) if 0 <= m + dl < T]

                    def stage1(m):
                        pts = []
                        for (j, slab) in keyt(m):
                            tmp, btmp = tmp_r.next()
                            for g in range(2):
                                sbk, bsb = sbank_r.next()
                                pr = slice(64 * g, 64 * g + 64)
                                for i, h in enumerate((g, g + 2)):
                                    Kh, bKh = (Kt[h // 2], bK[h // 2])
                                    Qh, bQh = Qt[h // 2], bQ[h // 2]
                                    p.op("pe", lambda e, sbk=sbk, i=i, pr=pr, Kh=Kh, Qh=Qh, j=j, m=m: e.matmul(
                                        sbk[:, i * 128:(i + 1) * 128], lhsT=Kh[pr, j * 128:(j + 1) * 128], rhs=Qh[pr, m * 128:(m + 1) * 128],
                                        start=True, stop=True), reads=[bKh, bQh], writes=[bsb])
                                p.op("act", lambda e, tmp=tmp, sbk=sbk, g=g: e.mul(tmp[:, g * 256:(g + 1) * 256], sbk[:, 0:256], 0.125), reads=[bsb], writes=[btmp])
                            bias_ap = ab[:, slab * 512:(slab + 1) * 512] if isA else db[:, slab * 512:(slab + 1) * 512]
                            p.op("dve", lambda e, tmp=tmp, bias_ap=bias_ap: e.tensor_tensor(tmp[:], tmp[:], bias_ap, ALU.add),
                                 reads=[bpar], writes=[btmp])
                            pt, bpt = pt_r.next()
                            p.op("act", lambda e, pt=pt, tmp=tmp: e.activation(pt[:], tmp[:], AF.Exp), reads=[btmp], writes=[bpt])
                            pts.append((j, pt, bpt))
                        return pts

                    def stage2(m, pts):
                        if "2" in LSK:
                            return
                        acc, bacc = acc_r.next()
                        for h in range(4):
                            kv = h if isA else h // 2
                            for ii, (j, pt, bpt) in enumerate(pts):
                                hp = (0, 2, 1, 3)[h]
                                p.op("pe", lambda e, acc=acc, h=h, kv=kv, j=j, pt=pt, ii=ii, hp=hp: e.matmul(
                                    acc[0:65, h * 128:(h + 1) * 128], lhsT=Va[kv][:, j, 0:65], rhs=pt[:, hp * 128:(hp + 1) * 128],
                                    start=(ii == 0), stop=(ii == len(pts) - 1)), reads=[bVa[kv], bpt], writes=[bacc])
                        t0 = s0 + m * 128
                        if "f" in LSK:
                            return
                        finalize_simple(acc, bacc, 512,
                                        lambda h4: oT_h[:, mx["idx"] * 4 + h4, t0:t0 + 128],
                                        add_sink=(name == "D"), split4=True)

                    prev = stage1(0)
                    for m in range(T):
                        nxt = stage1(m + 1) if m + 1 < T else None
                        stage2(m, prev)
                        prev = nxt

                def finalize_c(qb, h, t0, stream):
                    a1, ba1 = acc_r.next()
                    a2, ba2 = acc_r.next()
                    stream(qb, h, 0, a1, ba1)
                    stream(qb, h, 1, a2, ba2)
                    rl, brl = rl_r.next()
                    rl2, brl2 = rl_r.next()
                    p.op("dve", lambda e: e.reciprocal(rl[64:65, :], a1[64:65, :]), reads=[ba1], writes=[brl])
                    p.op("dve", lambda e: e.reciprocal(rl2[64:65, :], a2[64:65, :]), reads=[ba2], writes=[brl2])
                    p.op("dve", lambda e: e.tensor_scalar(rl2[64:65, :], rl2[64:65, :], nlam, None, op0=ALU.mult), reads=[brl2, bsm], writes=[brl2])
                    o1, bo1 = f32w_r.next()
                    o2, bo2 = f32w_r.next()
                    for (rr, brr, aa, baa, oo, boo) in ((rl, brl, a1, ba1, o1, bo1), (rl2, brl2, a2, ba2, o2, bo2)):
                        fin, bfin = fin_r.next()
                        p.op("pe", lambda e, fin=fin, rr=rr: e.matmul(fin[0:64, :], lhsT=onesf[64:65, 0:64], rhs=rr[64:65, :], start=True, stop=True),
                             reads=[brr, b_const], writes=[bfin])
                        bcs, bbcs = bcs_r.next()
                        p.op("act", lambda e, bcs=bcs, fin=fin: e.copy(bcs[:], fin[0:64, :]), reads=[bfin], writes=[bbcs])
                        p.op("dve", lambda e, oo=oo, aa=aa, bcs=bcs: e.tensor_tensor(oo[:], aa[0:64, :], bcs[:], ALU.mult),
                             reads=[baa, bbcs], writes=[boo])
                    p.op("pool", lambda e: e.tensor_tensor(o1[:], o1[:], o2[:], ALU.add), reads=[bo2], writes=[bo1])
                    p.op("pool", lambda e: e.tensor_tensor(o2[:], o1[:], o1[:], ALU.mult), reads=[bo1], writes=[bo2])
                    fin2, bfin2 = fin_r.next()
                    p.op("pe", lambda e: e.matmul(fin2[0:64, :], lhsT=onesf[0:64, 0:64], rhs=o2[:], start=True, stop=True),
                         reads=[bo2, b_const], writes=[bfin2])
                    sd, bsd = f32w_r.next()
                    p.op("act", lambda e: e.activation(sd[:], fin2[0:64, :], AF.Sqrt, bias=EPS, scale=1.0 / 64), reads=[bfin2], writes=[bsd])
                    p.op("dve", lambda e: e.reciprocal(sd[:], sd[:]), reads=[bsd], writes=[bsd])
                    ot, bot = ot_r.next()
                    p.op("dve", lambda e: e.scalar_tensor_tensor(out=ot[:], in0=o1[:], scalar=gcol[:, 1:2], in1=sd[:], op0=ALU.mult, op1=ALU.mult),
                         reads=[bo1, bsd, bsm], writes=[bot])
                    p.dma(lambda e: e.dma_start(out=oT_h[:, 8 + h, t0:t0 + 512], in_=ot[:]), reads=[bot], queue="pool")

                def dense_attn(name, si):
                    S = seqs[si]
                    T = S // 128
                    s0 = seq_start[si]
                    mx = MIX[name]
                    isC = name == "C"
                    nq = S // 512
                    scale = (32 ** -0.5) if isC else 0.125

                    def stream(qb, h, mp, acc, bacc):
                        if isC:
                            base = 64 * (h % 2) + 32 * mp
                            pr = slice(base, base + 32)
                            Kh, bKh, Qh, bQh = Kt[h // 2], bK[h // 2], Qt[h // 2], bQ[h // 2]
                            kv = h
                            cs = DIFF_SLOPES[h] / scale
                        else:
                            base = 64 * (h % 2)
                            pr = slice(base, base + 64)
                            Kh, bKh, Qh, bQh = Kt[h // 2], bK[h // 2], Qt[h // 2], bQ[h // 2]
                            kv = h // 2
                        kw = dict(tile_position=(base, 0)) if base == 96 else {}

                        def qk(kt):
                            sbk, bsb = sbank_r.next()
                            p.op("pe", lambda e: e.matmul(sbk[:], lhsT=Kh[pr, kt * 128:(kt + 1) * 128], rhs=Qh[pr, qb * 512:(qb + 1) * 512],
                                                          start=True, stop=True, **kw), reads=[bKh, bQh], writes=[bsb])
                            pt, bpt = pt_r.next()
                            if not isC:
                                p.op("act", lambda e: e.activation(pt[:], sbk[:], AF.Exp, scale=scale), reads=[bsb], writes=[bpt])
                            else:
                                dl = qb * 512 - kt * 128
                                tmp, btmp = tmp_r.next()
                                if dl >= 128:
                                    src_ap, coef, bias = cd[:, 0:512], -cs, -DIFF_SLOPES[h] * dl
                                elif dl <= -512:
                                    src_ap, coef, bias = cd[:, 0:512], cs, DIFF_SLOPES[h] * dl
                                else:
                                    di = {0: 1, -128: 2, -256: 3, -384: 4}[dl]
                                    src_ap, coef, bias = cd[:, di * 512:(di + 1) * 512], -cs, 0.0
                                p.op("dve", lambda e: e.scalar_tensor_tensor(out=tmp[:], in0=src_ap, scalar=coef, in1=sbk[:],
                                                                             op0=ALU.mult, op1=ALU.add), reads=[bsb, bpar], writes=[btmp])
                                bc_ap = bias_col(bias)
                                p.op("act", lambda e: e.activation(pt[:], tmp[:], AF.Exp, bias=bc_ap, scale=scale), reads=[btmp, bcbt], writes=[bpt])
                            return pt, bpt

                        def pv(kt, pt, bpt):
                            p.op("pe", lambda e: e.matmul(acc[0:65, :], lhsT=Va[kv][:, kt, 0:65], rhs=pt[:], start=(kt == 0), stop=(kt == T - 1)),
                                 reads=[bVa[kv], bpt], writes=[bacc])

                        prev = qk(0)
                        for kt in range(T):
                            nxt = qk(kt + 1) if kt + 1 < T else None
                            pv(kt, *prev)
                            prev = nxt

                    for qb in range(nq):
                        t0 = s0 + qb * 512
                        for h in range(4):
                            if not isC:
                                acc, bacc = acc_r.next()
                                stream(qb, h, 0, acc, bacc)
                                finalize_simple(acc, bacc, 512, lambda h=h, t0=t0: oT_h[:, 4 + h, t0:t0 + 512])
                            else:
                                finalize_c(qb, h, t0, stream)
                import os
                for si in range(len(seqs)):
                    for name in os.environ.get("MK_MIX", "ABCD"):
                        load_mixer(name, si)
                        if name in ("A", "D"):
                            local_attn(name, si)
                        else:
                            dense_attn(name, si)
            p.barrier()
            import os
            if dbg and l == 0 and "d" not in os.environ.get("MK_SKIP", ""):
                with ExitStack() as es2:
                    sb2, _ = mk(es2)
                    dr = Ring([sb2([128, 8, 128], BF16) for _ in range(2)])
                    dbg_c = dbg_o["oT"].rearrange("(c p) n -> p c n", p=128)
                    for t in range(NTILES):
                        tt, btt = dr.next()
                        p.dma(lambda e, tt=tt, t=t: e.dma_start(out=tt[:], in_=oT_c[:, :, t * 128:(t + 1) * 128]), writes=[btt])
                        p.dma(lambda e, tt=tt, t=t: e.dma_start(out=dbg_c[:, :, t * 128:(t + 1) * 128], in_=tt[:]), reads=[btt], queue="pool")
                p.barrier()

        def phase3a(l):
            src = x_in if l == 0 else y
            with ExitStack() as es:
                sb, ps = mk(es)
                Wb = sb([128, 8, D], BF16)
                Wo = sb([128, 8, D], BF16)
                bW = Buf()
                bWo = Buf()
                p.dma(lambda e: e.dma_start(out=Wb[:], in_=w_br[l]), writes=[bW], queue="pool")
                p.dma(lambda e: e.dma_start(out=Wo[:], in_=w_out[l]), writes=[bWo], queue="pool")
                gT = sb([128, 16], F32)
                gpo = sb([128, D], F32)
                bP = Buf()
                p.dma(lambda e: e.dma_start(out=gT[:], in_=gT_pre[l]), writes=[bP])
                p.dma(lambda e: e.dma_start(out=gpo[:], in_=g_post[l, 0:1, :].to_broadcast([128, D])), writes=[bP])
                xin_r = Ring([sb([128, D], F32) for _ in range(2)])
                oTt_r = Ring([sb([128, 8, 128], BF16) for _ in range(2)])
                gt_r = Ring([sb([128, 4096], BF16) for _ in range(2)])
                mg_r = Ring([sb([128, D], F32) for _ in range(2)])
                tm_r = Ring([sb([128, 512], F32) for _ in range(3)])
                mbf_r = Ring([sb([128, D], BF16) for _ in range(2)])
                mT_r = Ring([sb([128, 8, 128], BF16) for _ in range(2)])
                st_r = Ring([sb([128, 16], F32) for _ in range(2)])
                xn_r = Ring([sb([128, D], F32) for _ in range(2)])
                hbf_r = Ring([sb([128, D], BF16) for _ in range(2)])
                hT_r = Ring([sb([128, 8, 128], BF16) for _ in range(2)])
                junk = sb([128, D], BF16)
                bjunk = Buf()
                br_r = Ring([ps([128, 512], F32) for _ in range(2)])
                pT_r = Ring([ps([128, 8, 128], BF16) for _ in range(1)])
                ob_r = Ring([ps([128, 512], F32) for _ in range(4)])
                pT2_r = Ring([ps([128, 8, 128], BF16) for _ in range(1)])
                for t in range(NTILES):
                    r0 = t * 128
                    xin, bxin = xin_r.next()
                    p.dma(lambda e, xin=xin, r0=r0: e.dma_start(out=xin[:], in_=src[r0:r0 + 128, :]), writes=[bxin])
                    oTt, boT = oTt_r.next()
                    p.dma(lambda e, oTt=oTt, r0=r0: e.dma_start(out=oTt[:], in_=oT_c[:, :, r0:r0 + 128]), writes=[boT])
                    gt, bgt = gt_r.next()
                    p.dma(lambda e, gt=gt, r0=r0: e.dma_start(out=gt[:], in_=g_d[r0:r0 + 128, :]), writes=[bgt])
                    mg, bmg = mg_r.next()
                    for nh in range(2):
                        for i in range(4):
                            br, bbr = br_r.next()
                            for c in range(2):
                                p.op("pe", lambda e, br=br, oTt=oTt, i=i, c=c, nh=nh: e.matmul(
                                    br[:], lhsT=oTt[:, 2 * i + c, :], rhs=Wb[:, 2 * i + c, nh * 512:(nh + 1) * 512], start=(c == 0), stop=(c == 1)),
                                    reads=[boT, bW], writes=[bbr])
                            gsl = gt[:, i * 1024 + nh * 512:i * 1024 + (nh + 1) * 512]
                            if i == 0:
                                p.op("dve", lambda e, mg=mg, br=br, gsl=gsl, nh=nh: e.tensor_tensor(mg[:, nh * 512:(nh + 1) * 512], br[:], gsl, ALU.mult),
                                     reads=[bbr, bgt], writes=[bmg])
                            else:
                                tm, btm = tm_r.next()
                                p.op("dve", lambda e, tm=tm, br=br, gsl=gsl: e.tensor_tensor(tm[:], br[:], gsl, ALU.mult), reads=[bbr, bgt], writes=[btm])
                                p.op("pool", lambda e, mg=mg, tm=tm, nh=nh: e.tensor_tensor(mg[:, nh * 512:(nh + 1) * 512], mg[:, nh * 512:(nh + 1) * 512], tm[:], ALU.add),
                                     reads=[btm], writes=[bmg])
                    mbf, bmbf = mbf_r.next()
                    p.op("act", lambda e, mbf=mbf, mg=mg: e.copy(mbf[:], mg[:]), reads=[bmg], writes=[bmbf])
                    pT, bpT = pT_r.next()
                    for k in range(8):
                        p.op("pe", lambda e, pT=pT, mbf=mbf, k=k: e.transpose(pT[:, k, :], mbf[:, k * 128:(k + 1) * 128], ident[:]),
                             reads=[bmbf, b_const], writes=[bpT])
                    mT, bmT = mT_r.next()
                    p.op("dve", lambda e, mT=mT, pT=pT: e.tensor_copy(mT[:], pT[:]), reads=[bpT], writes=[bmT])
                    st, bst = st_r.next()
                    p.op("pool", lambda e, st=st: e.memset(st[:], 0.0), writes=[bst])
                    obs = []
                    for nh in range(2):
                        ob, bob = ob_r.next()
                        obs.append((ob, bob))
                        for k in range(8):
                            p.op("pe", lambda e, ob=ob, mT=mT, k=k, nh=nh: e.matmul(ob[:], lhsT=mT[:, k, :], rhs=Wo[:, k, nh * 512:(nh + 1) * 512],
                                                                               start=(k == 0), stop=(k == 7)), reads=[bmT, bWo], writes=[bob])
                        p.op("act", lambda e, ob=ob, st=st, nh=nh: e.activation(junk[:, 0:512], ob[:], AF.Square, accum_out=st[:, nh:nh + 1]),
                             reads=[bob], writes=[bst, bjunk])
                    p.op("dve", lambda e, st=st: e.tensor_tensor(st[:, 2:3], st[:, 0:1], st[:, 1:2], ALU.add), reads=[bst], writes=[bst])
                    p.op("act", lambda e, st=st: e.activation(st[:, 3:4], st[:, 2:3], AF.Sqrt, bias=EPS, scale=1.0 / D), reads=[bst], writes=[bst])
                    p.op("dve", lambda e, st=st: e.reciprocal(st[:, 4:5], st[:, 3:4]), reads=[bst], writes=[bst])
                    xn, bxn = xn_r.next()
                    for nh in range(2):
                        ob, bob = obs[nh]
                        tm, btm = tm_r.next()
                        p.op("dve", lambda e, tm=tm, ob=ob, st=st, nh=nh: e.scalar_tensor_tensor(
                            out=tm[:], in0=ob[:], scalar=st[:, 4:5], in1=gpo[:, nh * 512:(nh + 1) * 512], op0=ALU.mult, op1=ALU.mult),
                            reads=[bob, bst, bP], writes=[btm])
                        p.op("pool", lambda e, xn=xn, tm=tm, xin=xin, nh=nh: e.tensor_tensor(xn[:, nh * 512:(nh + 1) * 512], tm[:], xin[:, nh * 512:(nh + 1) * 512], ALU.add),
                             reads=[btm, bxin], writes=[bxn])
                    p.dma(lambda e, xn=xn, r0=r0: e.dma_start(out=y[r0:r0 + 128, :], in_=xn[:]), reads=[bxn], queue="pool")
                    if dbg and l == 0:
                        p.dma(lambda e, xn=xn, r0=r0: e.dma_start(out=dbg_o["x1"][r0:r0 + 128, :], in_=xn[:]), reads=[bxn], queue="pool")
                    p.op("act", lambda e, xn=xn, st=st: e.activation(junk[:], xn[:], AF.Square, accum_out=st[:, 5:6]), reads=[bxn], writes=[bst, bjunk])
                    p.op("act", lambda e, st=st: e.activation(st[:, 6:7], st[:, 5:6], AF.Sqrt, bias=EPS, scale=1.0 / D), reads=[bst], writes=[bst])
                    p.op("dve", lambda e, st=st: e.reciprocal(st[:, 7:8], st[:, 6:7]), reads=[bst], writes=[bst])
                    hbf, bhbf = hbf_r.next()
                    p.op("dve", lambda e, hbf=hbf, xn=xn, st=st: e.tensor_scalar(hbf[:], xn[:], st[:, 7:8], None, op0=ALU.mult), reads=[bxn, bst], writes=[bhbf])
                    pT2, bpT2 = pT2_r.next()
                    for k in range(8):
                        p.op("pe", lambda e, pT2=pT2, hbf=hbf, k=k: e.transpose(pT2[:, k, :], hbf[:, k * 128:(k + 1) * 128], ident[:]),
                             reads=[bhbf, b_const], writes=[bpT2])
                    hTt, bhT = hT_r.next()
                    p.op("dve", lambda e, hTt=hTt, pT2=pT2: e.tensor_tensor(hTt[:], pT2[:], gT[:, 8:16].unsqueeze(2).to_broadcast([128, 8, 128]), ALU.mult),
                         reads=[bpT2, bP], writes=[bhT])
                    p.dma(lambda e, hTt=hTt, r0=r0: e.dma_start(out=hT_c[:, :, r0:r0 + 128], in_=hTt[:]), reads=[bhT], queue="pool")
            p.barrier()

        def phase3b(l):
            with ExitStack() as es:
                sb, ps = mk(es)
                Wu = sb([128, 8, 5632], BF16)
                Wd = sb([128, 22, D], BF16)
                bWu = [Buf() for _ in range(8)]
                bWd = Buf()
                for k in range(8):
                    p.dma(lambda e, k=k: e.dma_start(out=Wu[:, k, :], in_=w_up[l, :, k, :]), writes=[bWu[k]], queue="pool")
                p.dma(lambda e: e.dma_start(out=Wd[:], in_=w_dn[l]), writes=[bWd], queue="pool")
                cw = sb([128, 44, 4], F32)
                gpo = sb([128, D], F32)
                bP = Buf()
                p.dma(lambda e: e.dma_start(out=cw[:], in_=convp[l]), writes=[bP])
                p.dma(lambda e: e.dma_start(out=gpo[:], in_=g_post[l, 1:2, :].to_broadcast([128, D])), writes=[bP])
                hs_r = Ring([sb([128, 8, 258], BF16) for _ in range(2)])
                xin_r = Ring([sb([128, D], F32) for _ in range(2)])
                cg_r = Ring([sb([128, 256], F32) for _ in range(2)])
                cv_r = Ring([sb([128, 256], F32) for _ in range(2)])
                gg_r = Ring([sb([128, 256], F32) for _ in range(2)])
                aT_r = Ring([sb([128, 22, 256], BF16) for _ in range(2)])
                tm_r = Ring([sb([128, 512], F32) for _ in range(2)])
                yo_r = Ring([sb([128, D], F32) for _ in range(2)])
                st_r = Ring([sb([128, 8], F32) for _ in range(2)])
                junk = sb([128, 512], BF16)
                bjunk = Buf()
                u_r = Ring([ps([128, 512], F32) for _ in range(4)])
                d_r = Ring([ps([128, 512], F32) for _ in range(4)])
                for si, S in enumerate(seqs):
                    for b in range(S // 256):
                        t0 = seq_start[si] + b * 256
                        first, last = (b == 0), (b == S // 256 - 1)
                        hs, bhs = hs_r.next()
                        lo = 1 if first else 0
                        hi = 257 if last else 258
                        if first:
                            p.op("pool", lambda e, hs=hs: e.memset(hs[:, :, 0:1], 0.0), writes=[bhs])
                        if last:
                            p.op("pool", lambda e, hs=hs: e.memset(hs[:, :, 257:258], 0.0), writes=[bhs])
                        p.dma(lambda e, hs=hs, lo=lo, hi=hi, t0=t0: e.dma_start(out=hs[:, :, lo:hi], in_=hT_c[:, :, t0 - 1 + lo:t0 - 1 + hi]), writes=[bhs])
                        aT, baT = aT_r.next()
                        for fp in range(22):
                            us = []
                            for ch in (fp, fp + 22):
                                u, bu = u_r.next()
                                us.append((u, bu, ch))
                                for k in range(8):
                                    p.op("pe", lambda e, u=u, hs=hs, k=k, ch=ch: e.matmul(u[:, 0:258], lhsT=Wu[:, k, ch * 128:(ch + 1) * 128], rhs=hs[:, k, :],
                                                                                     start=(k == 0), stop=(k == 7)), reads=[bhs, bWu[k]], writes=[bu])
                            cg, bcg = cg_r.next()
                            cv, bcv = cv_r.next()
                            for (u, bu, ch), (c_, bc_) in zip(us, ((cg, bcg), (cv, bcv))):
                                p.op("act", lambda e, c_=c_, u=u, ch=ch: e.activation(c_[:], u[:, 0:256], AF.Identity, bias=cw[:, ch, 3:4], scale=cw[:, ch, 0:1]),
                                     reads=[bu, bP], writes=[bc_])
                                p.op("dve", lambda e, c_=c_, u=u, ch=ch: e.scalar_tensor_tensor(out=c_[:], in0=u[:, 1:257], scalar=cw[:, ch, 1:2], in1=c_[:], op0=ALU.mult, op1=ALU.add),
                                     reads=[bu, bP], writes=[bc_])
                                p.op("dve", lambda e, c_=c_, u=u, ch=ch: e.scalar_tensor_tensor(out=c_[:], in0=u[:, 2:258], scalar=cw[:, ch, 2:3], in1=c_[:], op0=ALU.mult, op1=ALU.add),
                                     reads=[bu, bP], writes=[bc_])
                            gg, bgg = gg_r.next()
                            p.op("act", lambda e, gg=gg, cg=cg: e.activation(gg[:], cg[:], AF.Gelu_apprx_tanh), reads=[bcg], writes=[bgg])
                            p.op("pool", lambda e, aT=aT, gg=gg, cv=cv, fp=fp: e.tensor_tensor(aT[:, fp, :], gg[:], cv[:], ALU.mult), reads=[bgg, bcv], writes=[baT])
                        for a in range(2):
                            r0 = t0 + a * 128
                            xin, bxin = xin_r.next()
                            p.dma(lambda e, xin=xin, r0=r0: e.dma_start(out=xin[:], in_=y[r0:r0 + 128, :]), writes=[bxin])
                            st, bst = st_r.next()
                            p.op("pool", lambda e, st=st: e.memset(st[:], 0.0), writes=[bst])
                            dbs = []
                            for nh in range(2):
                                dbk, bdb = d_r.next()
                                dbs.append((dbk, bdb))
                                for f in range(22):
                                    p.op("pe", lambda e, dbk=dbk, aT=aT, f=f, a=a, nh=nh: e.matmul(dbk[:], lhsT=aT[:, f, a * 128:(a + 1) * 128], rhs=Wd[:, f, nh * 512:(nh + 1) * 512],
                                                                                             start=(f == 0), stop=(f == 21)), reads=[baT, bWd], writes=[bdb])
                                p.op("act", lambda e, dbk=dbk, st=st, nh=nh: e.activation(junk[:], dbk[:], AF.Square, accum_out=st[:, nh:nh + 1]), reads=[bdb], writes=[bst, bjunk])
                            p.op("dve", lambda e, st=st: e.tensor_tensor(st[:, 2:3], st[:, 0:1], st[:, 1:2], ALU.add), reads=[bst], writes=[bst])
                            p.op("act", lambda e, st=st: e.activation(st[:, 3:4], st[:, 2:3], AF.Sqrt, bias=EPS, scale=1.0 / D), reads=[bst], writes=[bst])
                            p.op("dve", lambda e, st=st: e.reciprocal(st[:, 4:5], st[:, 3:4]), reads=[bst], writes=[bst])
                            yo, byo = yo_r.next()
                            for nh in range(2):
                                dbk, bdb = dbs[nh]
                                tm, btm = tm_r.next()
                                p.op("dve", lambda e, tm=tm, dbk=dbk, st=st, nh=nh: e.scalar_tensor_tensor(
                                    out=tm[:], in0=dbk[:], scalar=st[:, 4:5], in1=gpo[:, nh * 512:(nh + 1) * 512], op0=ALU.mult, op1=ALU.mult),
                                    reads=[bdb, bst, bP], writes=[btm])
                                p.op("pool", lambda e, yo=yo, tm=tm, xin=xin, nh=nh: e.tensor_tensor(yo[:, nh * 512:(nh + 1) * 512], tm[:], xin[:, nh * 512:(nh + 1) * 512], ALU.add),
                                     reads=[btm, bxin], writes=[byo])
                            p.dma(lambda e, yo=yo, r0=r0: e.dma_start(out=y[r0:r0 + 128, :], in_=yo[:]), reads=[byo], queue="pool")
            p.barrier()

        import os
        stop = int(os.environ.get("MK_STOP", "99"))
        for l in range(depth):
            phase1(l)
            if stop >= 2:
                phase2(l)
            if stop >= 3:
                phase3a(l)
            if stop >= 4:
                phase3b(l)
        p.emit()
    return nc, p


def prep_shared(inp, depth, smax):
    f = lambda a: np.ascontiguousarray(np.asarray(a, dtype=np.float32))
    w_in = f(inp["w_in"]).reshape(depth, 8, 128, 6656).transpose(0, 2, 1, 3)
    w_br = f(inp["w_branch"]).reshape(depth, 8, 128, D).transpose(0, 2, 1, 3)
    w_out = f(inp["w_out"]).reshape(depth, 8, 128, D).transpose(0, 2, 1, 3)
    w_up = f(inp["ffn_w_up"]).reshape(depth, 8, 128, 5632).transpose(0, 2, 1, 3)
    w_dn = f(inp["ffn_w_down"]).reshape(depth, 22, 128, D).transpose(0, 2, 1, 3)
    gT = np.concatenate([f(inp["norm_mix_pre"]).reshape(depth, 8, 128).transpose(0, 2, 1),
                         f(inp["norm_ffn_pre"]).reshape(depth, 8, 128).transpose(0, 2, 1)], axis=2)
    g_post = np.stack([f(inp["norm_mix_post"]), f(inp["norm_ffn_post"])], axis=1)
    cw = f(inp["ffn_conv_w"]).reshape(depth, 3, 44, 128)
    cb = f(inp["ffn_conv_b"]).reshape(depth, 1, 44, 128)
    convp = np.concatenate([cw, cb], axis=1).transpose(0, 3, 2, 1)
    cosf, sinf = rope_tables(smax)
    ropec = np.concatenate([cosf, sinf], axis=1)
    gqk = np.concatenate([np.tile(f(inp["gqa_q_norm"]), (1, 4)), np.tile(f(inp["gqa_k_norm"]), (1, 2))], axis=1)
    rpb = f(inp["na_rpb"])
    abias = np.stack([build_abias(rpb[l]) for l in range(depth)]).reshape(depth, 128, NA_SLABS * 512)
    dbias = build_dbias().reshape(128, 3 * 512)
    ii = np.arange(128, dtype=np.float32)[:, None]
    jj = np.arange(512, dtype=np.float32)[None, :]
    base = jj - ii
    cdist = np.concatenate([base] + [np.abs(base + d) for d in (0.0, -128.0, -256.0, -384.0)], axis=1).astype(np.float32)
    lamv = np.concatenate([f(inp["diff_lambda_q1"]), f(inp["diff_lambda_k1"]), f(inp["diff_lambda_q2"]), f(inp["diff_lambda_k2"])], axis=1)
    subln = f(inp["diff_subln"]).reshape(depth, 64, 1)
    sink = f(inp["swa_sink"])
    c = np.ascontiguousarray
    return dict(w_in=c(w_in), w_br=c(w_br), w_out=c(w_out), w_up=c(w_up), w_dn=c(w_dn), gT_pre=c(gT), g_post=c(g_post),
                convp=c(convp), ropec=c(ropec), gqk=c(gqk), abias=c(abias), dbias=c(dbias), cdist=c(cdist), lamv=c(lamv),
                subln=c(subln), sink=c(sink))


_CACHE = {}


def kernel(**inp):
    xp = np.asarray(inp["x_prompt"], dtype=np.float32)
    xs = np.asarray(inp["x_sample"], dtype=np.float32)
    depth = int(np.asarray(inp["w_in"]).shape[0])
    ncores = 8
    bp, sp_ = xp.shape[0], xp.shape[1]
    bs, ss_ = xs.shape[0], xs.shape[1]
    npc, nsc = bp // ncores, bs // ncores
    seqs = [sp_] * npc + [ss_] * nsc
    key = (tuple(seqs), depth)
    if key not in _CACHE:
        _CACHE[key] = build(seqs, depth)[0]
    nc = _CACHE[key]
    shared = prep_shared(inp, depth, max(seqs))
    in_maps = []
    for c in range(ncores):
        xc = np.concatenate([xp[c * npc:(c + 1) * npc].reshape(-1, D), xs[c * nsc:(c + 1) * nsc].reshape(-1, D)], axis=0)
        m = dict(shared)
        m["x"] = np.ascontiguousarray(xc)
        in_maps.append(m)
    res = run_bass_kernel_spmd(nc, in_maps, core_ids=list(range(ncores)))
    yp = np.empty_like(xp)
    ys = np.empty_like(xs)
    for c in range(ncores):
        yc = res.results[c]["y"]
        yp[c * npc:(c + 1) * npc] = yc[:npc * sp_].reshape(npc, sp_, D)
        ys[c * nsc:(c + 1) * nsc] = yc[npc * sp_:].reshape(nsc, ss_, D)
    return (yp, ys)
```

```python
import math
from contextlib import ExitStack
import numpy as np
import concourse.bass as bass
import concourse.mybir as mybir
from concourse.bass_utils import run_bass_kernel_spmd

F32 = mybir.dt.float32
BF16 = mybir.dt.bfloat16
AF = mybir.ActivationFunctionType
ALU = mybir.AluOpType
AX = mybir.AxisListType

COMPUTE = ("pe", "act", "dve", "pool")
ALLENG = COMPUTE + ("sp",)
NCHAINS = 48
EPS = 1e-6
NEG = -1e30


class Buf:
    __slots__ = ("w", "r")

    def __init__(self):
        self.w = None
        self.r = []


class Op:
    __slots__ = ("eng", "fn", "waits", "signal", "pos", "is_dma", "chain", "dma_val", "cnt")

    def __init__(self, eng, fn):
        self.eng = eng
        self.fn = fn
        self.waits = []
        self.signal = False
        self.pos = -1
        self.is_dma = False
        self.chain = None
        self.dma_val = 0
        self.cnt = 0


class Chain:
    __slots__ = ("sem", "count", "last")

    def __init__(self, sem):
        self.sem = sem
        self.count = 0
        self.last = None


class Prog:
    def __init__(self, nc):
        self.nc = nc
        self.ops = {e: [] for e in ALLENG}
        self.sem = {e: nc.alloc_semaphore(name="s_" + e) for e in ALLENG}
        self.known = {e: {p: -1 for p in ALLENG} for e in ALLENG}
        self.known_chain = {e: {} for e in ALLENG}
        self.chains = [Chain(nc.alloc_semaphore(name=f"c{i}")) for i in range(NCHAINS)]
        self.ci = 0
        self.nops = 0

    def _add_wait(self, op, d):
        E = op.eng
        if d.is_dma:
            kc = self.known_chain[E]
            if kc.get(id(d.chain), 0) >= d.dma_val:
                return
            kc[id(d.chain)] = d.dma_val
            op.waits.append(d)
        else:
            if d.eng == E and E == "pe":
                return
            kn = self.known[E]
            if kn[d.eng] >= d.pos:
                return
            kn[d.eng] = d.pos
            d.signal = True
            op.waits.append(d)

    def _deps(self, op, reads, writes):
        for b in reads:
            if b.w is not None:
                self._add_wait(op, b.w)
        for b in writes:
            if b.w is not None:
                self._add_wait(op, b.w)
            for r in b.r:
                self._add_wait(op, r)
        for b in reads:
            b.r.append(op)
        for b in writes:
            b.w = op
            b.r = []

    def op(self, eng, fn, reads=(), writes=()):
        o = Op(eng, fn)
        o.pos = len(self.ops[eng])
        self._deps(o, reads, writes)
        self.ops[eng].append(o)
        self.nops += 1
        return o

    def dma(self, fn, reads=(), writes=(), queue="sp"):
        ch = self.chains[self.ci]
        self.ci = (self.ci + 1) % NCHAINS
        o = Op(queue, fn)
        o.is_dma = True
        o.chain = ch
        o.pos = len(self.ops[queue])
        if ch.last is not None:
            self._add_wait(o, ch.last)
        self._deps(o, reads, writes)
        ch.count += 16
        o.dma_val = ch.count
        ch.last = o
        self.ops[queue].append(o)
        self.nops += 1
        return o

    def barrier(self):
        sp_wait = []
        for e in COMPUTE:
            if self.ops[e]:
                last = self.ops[e][-1]
                if last.is_dma or last.fn is None:
                    last = self.op(e, lambda en: en.nop())
                last.signal = True
                sp_wait.append(last)
        o = Op("sp", lambda en: en.nop())
        o.pos = len(self.ops["sp"])
        o.waits = sp_wait + [c.last for c in self.chains if c.last is not None]
        o.signal = True
        self.ops["sp"].append(o)
        for e in COMPUTE:
            w = Op(e, None)
            w.pos = len(self.ops[e])
            w.waits = [o]
            self.ops[e].append(w)
        for e in ALLENG:
            for q in ALLENG:
                self.known[e][q] = len(self.ops[q]) - 1
            for c in self.chains:
                self.known_chain[e][id(c)] = c.count

    def emit(self):
        nc = self.nc
        for e in ALLENG:
            c = 0
            for o in self.ops[e]:
                if o.signal and not o.is_dma:
                    c += 1
                o.cnt = c
        sem = self.sem

        def run(e, eng):
            for o in self.ops[e]:
                for d in o.waits:
                    if d.is_dma:
                        eng.wait_ge(d.chain.sem, d.dma_val)
                    else:
                        eng.wait_ge(sem[d.eng], d.cnt)
                if o.fn is None:
                    continue
                ins = o.fn(eng)
                if o.is_dma:
                    ins.then_inc(o.chain.sem, 16)
                elif o.signal:
                    ins.then_inc(sem[e], 1)

        with nc.Block() as block:
            @block.tensor
            def _(eng):
                run("pe", eng)

            @block.scalar
            def _(eng):
                run("act", eng)

            @block.vector
            def _(eng):
                run("dve", eng)

            @block.gpsimd
            def _(eng):
                run("pool", eng)

            @block.sync
            def _(eng):
                run("sp", eng)


class Ring:
    def __init__(self, tiles):
        self.t = tiles
        self.b = [Buf() for _ in tiles]
        self.i = -1

    def next(self):
        self.i = (self.i + 1) % len(self.t)
        return self.t[self.i], self.b[self.i]


D = 1024
NPJ = 2560
QK_SRC = [0, 128, 256, 384, 768, 896, 1024, 1280, 1408, 1536, 1664, 2048, 2176, 2304]
MIX = {
    "A": dict(qb=(0, 1), kb=(2, 3), vc=512, vw=256, nkv=4, idx=0),
    "B": dict(qb=(4, 5), kb=(6,), vc=1152, vw=128, nkv=2, idx=1),
    "C": dict(qb=(7, 8), kb=(9, 10), vc=1792, vw=256, nkv=4, idx=2),
    "D": dict(qb=(11, 12), kb=(13,), vc=2432, vw=128, nkv=2, idx=3),
}
_SL = 2.0 ** (-8.0 * np.arange(1, 9) / 8)
DIFF_SLOPES = [float(v) for v in _SL[0::2]]
SWA_SLOPES = [float(v) for v in _SL[1::2]]
NA_CLASSES = 5
NA_SLABS = 21


def na_tiles(m, T):
    if T <= 4:
        raise NotImplementedError
    if m == 0:
        return [(j, 5 + j) for j in range(4)]
    if m == 1:
        return [(j, 9 + j) for j in range(4)]
    if m == T - 2:
        return [(T - 4 + i, 13 + i) for i in range(4)]
    if m == T - 1:
        return [(T - 4 + i, 17 + i) for i in range(4)]
    return [(m - 2 + i, i) for i in range(5)]


def _na_slab_defs():
    T = 16
    reps = {0: 8, 1: 0, 2: 1, 3: T - 2, 4: T - 1}
    out = [None] * NA_SLABS
    for cls, m in reps.items():
        for (j, slab) in na_tiles(m, T):
            out[slab] = (m, j, T)
    return out


def build_abias(rpb):
    defs = _na_slab_defs()
    res = np.full((128, NA_SLABS, 4, 128), NEG, np.float32)
    kk = np.arange(128)
    qq = np.arange(128)
    for slab, (m, j, T) in enumerate(defs):
        rows = 2 * T
        qr = 2 * m + qq // 64
        qc = qq % 64
        kr = 2 * j + kk // 64
        kc = kk % 64
        r0 = np.clip(qr - 4, 0, rows - 8)
        c0 = np.clip(qc - 8, 0, 64 - 16)
        vr = (kr[:, None] >= r0[None, :]) & (kr[:, None] < r0[None, :] + 8)
        vc = (kc[:, None] >= c0[None, :]) & (kc[:, None] < c0[None, :] + 16)
        valid = vr & vc
        dr = np.clip(kr[:, None] - qr[None, :] + 7, 0, 14)
        dc = np.clip(kc[:, None] - qc[None, :] + 15, 0, 30)
        for hp, h in enumerate((0, 2, 1, 3)):
            g = rpb[h][dr, dc]
            res[:, slab, hp, :] = np.where(valid, g, np.float32(NEG))
    return res


def build_dbias():
    res = np.full((128, 3, 4, 128), NEG, np.float32)
    kk = np.arange(128)[:, None]
    qq = np.arange(128)[None, :]
    for di, dl in enumerate((-1, 0, 1)):
        rel = np.abs(dl * 128 + kk - qq)
        for hp, h in enumerate((0, 2, 1, 3)):
            res[:, di, hp, :] = np.where(rel <= 128, -np.float32(SWA_SLOPES[h]) * rel.astype(np.float32), np.float32(NEG))
    return res


def rope_tables(S):
    t = np.arange(S)
    row = (t // 64).astype(np.float32)
    col = (t % 64).astype(np.float32)
    inv = (10000.0 ** (-np.arange(0, 32, 2, dtype=np.float32) / 32)).astype(np.float32)
    ar = row[:, None] * inv
    ac = col[:, None] * inv
    cr, sr, cc, sc = np.cos(ar), np.sin(ar), np.cos(ac), np.sin(ac)
    cosf = np.concatenate([cr, cr, cc, cc], 1).astype(np.float32)
    sinf = np.concatenate([-sr, sr, -sc, sc], 1).astype(np.float32)
    return cosf, sinf


def build(seqs, depth, dbg=False):
    NT = sum(seqs)
    NTILES = NT // 128
    SMAX = max(seqs)
    TMAX = SMAX // 128
    seq_start = [sum(seqs[:i]) for i in range(len(seqs))]
    nc = bass.Bass("TRN2", target_bir_lowering=False)

    def din(name, shape, dt=F32):
        return nc.dram_tensor(name, list(shape), dt, kind="ExternalInput").ap()

    x_in = din("x", [NT, D])
    w_in = din("w_in", [depth, 128, 8, 6656])
    w_br = din("w_br", [depth, 128, 8, D])
    w_out = din("w_out", [depth, 128, 8, D])
    w_up = din("w_up", [depth, 128, 8, 5632])
    w_dn = din("w_dn", [depth, 128, 22, D])
    gT_pre = din("gT_pre", [depth, 128, 16])
    g_post = din("g_post", [depth, 2, D])
    convp = din("convp", [depth, 128, 44, 4])
    ropec = din("ropec", [SMAX, 128])
    gqk = din("gqk", [depth, 384])
    abias = din("abias", [depth, 128, NA_SLABS * 512])
    dbias = din("dbias", [128, 3 * 512])
    cdist = din("cdist", [128, 5 * 512])
    lamv = din("lamv", [depth, 128])
    subln = din("subln", [depth, 64, 1])
    sink = din("sink", [depth, 4])
    y = nc.dram_tensor("y", [NT, D], F32, kind="ExternalOutput").ap()
    pj_d = nc.dram_tensor("pj_s", [NT, NPJ], BF16).ap()
    qkT_d = nc.dram_tensor("qkT_s", [14 * 128, NT], BF16).ap()
    g_d = nc.dram_tensor("g_s", [NT, 4096], BF16).ap()
    oT_d = nc.dram_tensor("oT_s", [D, NT], BF16).ap()
    hT_d = nc.dram_tensor("hT_s", [D, NT], BF16).ap()
    dbg_o = {}
    if dbg:
        dbg_o["pj"] = nc.dram_tensor("dbg_pj", [NT, NPJ], BF16, kind="ExternalOutput").ap()
        dbg_o["oT"] = nc.dram_tensor("dbg_oT", [D, NT], BF16, kind="ExternalOutput").ap()
        dbg_o["x1"] = nc.dram_tensor("dbg_x1", [NT, D], F32, kind="ExternalOutput").ap()

    p = Prog(nc)
    qkT_r = qkT_d.rearrange("(b p) n -> p b n", p=128)
    oT_c = oT_d.rearrange("(c p) n -> p c n", p=128)
    oT_h = oT_d.rearrange("(h d) n -> d h n", d=64)
    hT_c = hT_d.rearrange("(c p) n -> p c n", p=128)

    with ExitStack() as top:
        uid = [0]

        def mk(es):
            def sb(shape, dt):
                uid[0] += 1
                return es.enter_context(nc.sbuf_tensor(f"sb{uid[0]}", list(shape), dt))

            def ps(shape, dt):
                uid[0] += 1
                return es.enter_context(nc.psum_tensor(f"ps{uid[0]}", list(shape), dt))
            return sb, ps

        sbT, _ = mk(top)
        ident = sbT([128, 128], BF16)
        identf = sbT([128, 128], F32)
        onesf = sbT([128, 64], F32)
        b_const = Buf()
        p.op("pool", lambda e: e.memset(identf[:], 0.0), writes=[b_const])
        p.op("pool", lambda e: e.affine_select(out=identf[:], in_=identf[:], compare_op=ALU.not_equal,
                                               fill=1.0, base=0, pattern=[[-1, 128]], channel_multiplier=1),
             writes=[b_const])
        p.op("pool", lambda e: e.tensor_copy(ident[:], identf[:]), writes=[b_const])
        p.op("pool", lambda e: e.memset(onesf[:], 1.0), writes=[b_const])

        def phase1(l):
            src = x_in if l == 0 else y
            with ExitStack() as es:
                sb, ps = mk(es)
                W = sb([128, 8, 6656], BF16)
                bW = [Buf() for _ in range(8)]
                for k in range(8):
                    p.dma(lambda e, k=k: e.dma_start(out=W[:, k, :], in_=w_in[l, :, k, :]), writes=[bW[k]], queue="pool")
                gT = sb([128, 16], F32)
                gqkt = sb([128, 384], F32)
                bP = Buf()
                p.dma(lambda e: e.dma_start(out=gT[:], in_=gT_pre[l]), writes=[bP])
                p.dma(lambda e: e.dma_start(out=gqkt[:], in_=gqk[l:l + 1, :].to_broadcast([128, 384])), writes=[bP])
                xin_r = Ring([sb([128, D], F32) for _ in range(3)])
                xbf_r = Ring([sb([128, D], BF16) for _ in range(2)])
                junk = sb([128, D], BF16)
                bjunk = Buf()
                st_r = Ring([sb([128, 8], F32) for _ in range(2)])
                xT_r = Ring([sb([128, 8, 128], BF16) for _ in range(2)])
                pj_r = Ring([sb([128, NPJ], BF16) for _ in range(3)])
                gt_r = Ring([sb([128, 4096], BF16) for _ in range(3)])
                bq_r = Ring([sb([128, 384], F32) for _ in range(2)])
                wk_r = Ring([sb([128, 5, 384], F32) for _ in range(2)])
                sb6_r = Ring([sb([128, 16], F32) for _ in range(2)])
                rp_r = Ring([sb([128, 128], F32) for _ in range(3)])
                qk_r = Ring([sb([128, 14, 128], BF16) for _ in range(2)])
                pT_r = Ring([ps([128, 8, 128], BF16) for _ in range(1)])
                po_r = Ring([ps([128, 512], F32) for _ in range(5)])
                pq_r = Ring([ps([128, 16, 128], BF16) for _ in range(1)])
                def loadA(t):
                    si = max(i for i in range(len(seqs)) if seq_start[i] <= t * 128)
                    pos = t * 128 - seq_start[si]
                    r0 = t * 128
                    xin, bxin = xin_r.next()
                    p.dma(lambda e, xin=xin, r0=r0: e.dma_start(out=xin[:], in_=src[r0:r0 + 128, :]), writes=[bxin])
                    rp, brp = rp_r.next()
                    p.dma(lambda e, rp=rp, pos=pos: e.dma_start(out=rp[:], in_=ropec[pos:pos + 128, :]), writes=[brp])
                    return (xin, bxin, rp, brp, r0)

                def partA(t, lctx):
                    xin, bxin, rp, brp, r0 = lctx
                    st, bst = st_r.next()
                    p.op("pool", lambda e, st=st: e.memset(st[:], 0.0), writes=[bst])
                    p.op("act", lambda e, xin=xin, st=st: e.activation(junk[:], xin[:], AF.Square, accum_out=st[:, 0:1]),
                         reads=[bxin], writes=[bst, bjunk])
                    p.op("act", lambda e, st=st: e.activation(st[:, 1:2], st[:, 0:1], AF.Sqrt, bias=EPS, scale=1.0 / D),
                         reads=[bst], writes=[bst])
                    p.op("dve", lambda e, st=st: e.reciprocal(st[:, 2:3], st[:, 1:2]), reads=[bst], writes=[bst])
                    xbf, bxbf = xbf_r.next()
                    p.op("dve", lambda e, xbf=xbf, xin=xin: e.tensor_copy(xbf[:], xin[:]), reads=[bxin], writes=[bxbf])
                    pT, bpT = pT_r.next()
                    for k in range(8):
                        p.op("pe", lambda e, pT=pT, xbf=xbf, k=k: e.transpose(pT[:, k, :], xbf[:, k * 128:(k + 1) * 128], ident[:]),
                             reads=[bxbf, b_const], writes=[bpT])
                    xT, bxT = xT_r.next()
                    p.op("dve", lambda e, xT=xT, pT=pT: e.tensor_tensor(xT[:], pT[:], gT[:, 0:8].unsqueeze(2).to_broadcast([128, 8, 128]), ALU.mult),
                         reads=[bpT, bP], writes=[bxT])
                    pj, bpj = pj_r.next()
                    gt, bgt = gt_r.next()
                    bq, bbq = bq_r.next()
                    rs = st[:, 2:3]
                    for n in range(13):
                        po, bpo = po_r.next()
                        for k in range(8):
                            p.op("pe", lambda e, po=po, xT=xT, k=k, n=n: e.matmul(po[:], lhsT=xT[:, k, :], rhs=W[:, k, n * 512:(n + 1) * 512],
                                                                             start=(k == 0), stop=(k == 7)),
                                 reads=[bxT, bW[k]], writes=[bpo])
                        if n >= 5:
                            p.op("act", lambda e, po=po, gt=gt, n=n, rs=rs: e.activation(gt[:, (n - 5) * 512:(n - 4) * 512], po[:], AF.Sigmoid, scale=rs),
                                 reads=[bpo, bst], writes=[bgt])
                        elif n == 1:
                            p.op("dve", lambda e, po=po, pj=pj, rs=rs: e.tensor_scalar(pj[:, 512:768], po[:, 0:256], rs, None, op0=ALU.mult),
                                 reads=[bpo, bst], writes=[bpj])
                            p.op("dve", lambda e, po=po, bq=bq, rs=rs: e.tensor_scalar(bq[:, 0:256], po[:, 256:512], rs, None, op0=ALU.mult),
                                 reads=[bpo, bst], writes=[bbq])
                        elif n == 2:
                            p.op("dve", lambda e, po=po, bq=bq, rs=rs: e.tensor_scalar(bq[:, 256:384], po[:, 0:128], rs, None, op0=ALU.mult),
                                 reads=[bpo, bst], writes=[bbq])
                            p.op("dve", lambda e, po=po, pj=pj, rs=rs: e.tensor_scalar(pj[:, 1152:1536], po[:, 128:512], rs, None, op0=ALU.mult),
                                 reads=[bpo, bst], writes=[bpj])
                        else:
                            p.op("dve", lambda e, po=po, pj=pj, rs=rs, n=n: e.tensor_scalar(pj[:, n * 512:(n + 1) * 512], po[:], rs, None, op0=ALU.mult),
                                 reads=[bpo, bst], writes=[bpj])
                    wk, bwk = wk_r.next()
                    s6, bs6 = sb6_r.next()
                    bq3 = bq[:].rearrange("p (h d) -> p h d", d=64)
                    sq3 = wk[:, 0, :].rearrange("p (h d) -> p h d", d=64)
                    qn = wk[:, 1, :]
                    qn3 = qn.rearrange("p (h d) -> p h d", d=64)
                    qn4 = qn.rearrange("p (a b c) -> p a b c", b=2, c=16)
                    rot4 = wk[:, 2, :].rearrange("p (a b c) -> p a b c", b=2, c=16)
                    rot3 = wk[:, 2, :].rearrange("p (h d) -> p h d", d=64)
                    t13 = wk[:, 3, :].rearrange("p (h d) -> p h d", d=64)
                    t23 = wk[:, 4, :].rearrange("p (h d) -> p h d", d=64)
                    p.op("pool", lambda e, wk=wk, bq=bq: e.tensor_tensor(wk[:, 0, :], bq[:], bq[:], ALU.mult), reads=[bbq], writes=[bwk])
                    p.op("dve", lambda e, s6=s6, sq3=sq3: e.tensor_reduce(s6[:, 0:6], sq3, axis=AX.X, op=ALU.add), reads=[bwk], writes=[bs6])
                    p.op("act", lambda e, s6=s6: e.activation(s6[:, 6:12], s6[:, 0:6], AF.Sqrt, bias=EPS, scale=1.0 / 64), reads=[bs6], writes=[bs6])
                    p.op("dve", lambda e, s6=s6: e.reciprocal(s6[:, 0:6], s6[:, 6:12]), reads=[bs6], writes=[bs6])
                    p.op("dve", lambda e, qn3=qn3, bq3=bq3, s6=s6: e.tensor_tensor(qn3, bq3, s6[:, 0:6].unsqueeze(2).to_broadcast([128, 6, 64]), ALU.mult),
                         reads=[bbq, bs6], writes=[bwk])
                    p.op("pool", lambda e, qn=qn: e.tensor_tensor(qn, qn, gqkt[:], ALU.mult), reads=[bwk, bP], writes=[bwk])
                    p.op("pool", lambda e, rot4=rot4, qn4=qn4: e.tensor_copy(rot4[:, :, 0, :], qn4[:, :, 1, :]), reads=[bwk], writes=[bwk])
                    p.op("pool", lambda e, rot4=rot4, qn4=qn4: e.tensor_copy(rot4[:, :, 1, :], qn4[:, :, 0, :]), reads=[bwk], writes=[bwk])
                    p.op("dve", lambda e, t13=t13, qn3=qn3, rp=rp: e.tensor_tensor(t13, qn3, rp[:, 0:64].unsqueeze(1).to_broadcast([128, 6, 64]), ALU.mult),
                         reads=[bwk, brp], writes=[bwk])
                    p.op("pool", lambda e, t23=t23, rot3=rot3, rp=rp: e.tensor_tensor(t23, rot3, rp[:, 64:128].unsqueeze(1).to_broadcast([128, 6, 64]), ALU.mult),
                         reads=[bwk, brp], writes=[bwk])
                    p.op("dve", lambda e, pj=pj, wk=wk: e.tensor_tensor(pj[:, 768:1152], wk[:, 3, :], wk[:, 4, :], ALU.add),
                         reads=[bwk], writes=[bpj])
                    return (pj, bpj, gt, bgt, r0)

                def partB(ctx):
                    pj, bpj, gt, bgt, r0 = ctx
                    pq, bpq = pq_r.next()
                    for bi, c0 in enumerate(QK_SRC):
                        p.op("pe", lambda e, pq=pq, pj=pj, bi=bi, c0=c0: e.transpose(pq[:, bi, :], pj[:, c0:c0 + 128], ident[:]),
                             reads=[bpj, b_const], writes=[bpq])
                    qk, bqk = qk_r.next()
                    p.op("act", lambda e, qk=qk, pq=pq: e.copy(qk[:], pq[:, 0:14, :]), reads=[bpq], writes=[bqk])
                    p.dma(lambda e, pj=pj, r0=r0: e.dma_start(out=pj_d[r0:r0 + 128, :], in_=pj[:]), reads=[bpj])
                    p.dma(lambda e, gt=gt, r0=r0: e.dma_start(out=g_d[r0:r0 + 128, :], in_=gt[:]), reads=[bgt])
                    p.dma(lambda e, qk=qk, r0=r0: e.dma_start(out=qkT_r[:, :, r0:r0 + 128], in_=qk[:]), reads=[bqk])
                    if dbg and l == 0:
                        p.dma(lambda e, pj=pj, r0=r0: e.dma_start(out=dbg_o["pj"][r0:r0 + 128, :], in_=pj[:]), reads=[bpj])

                prevctx = None
                lq = [loadA(0)]
                for t in range(NTILES):
                    if t + 1 < NTILES:
                        lq.append(loadA(t + 1))
                    ctx = partA(t, lq.pop(0))
                    if prevctx is not None:
                        partB(prevctx)
                    prevctx = ctx
                partB(prevctx)
            p.barrier()

        def phase2(l):
            lambda_init = 0.8 - 0.6 * math.exp(-0.3 * l)
            with ExitStack() as es:
                sb, ps = mk(es)
                Qt = [sb([128, SMAX], BF16) for _ in range(2)]
                Kt = [sb([128, SMAX], BF16) for _ in range(2)]
                bQ = [Buf(), Buf()]
                bK = [Buf(), Buf()]
                Vraw = sb([128, TMAX, 256], BF16)
                bVraw = Buf()
                Va = [sb([128, TMAX, 65], BF16) for _ in range(4)]
                bVa = [Buf() for _ in range(4)]
                ab = sb([128, NA_SLABS * 512], BF16)
                db = sb([128, 3 * 512], F32)
                cd = sb([128, 5 * 512], F32)
                sm = sb([128, 64], F32)
                gcol = sb([64, 2], F32)
                bpar = Buf()
                p.dma(lambda e: e.dma_start(out=ab[:], in_=abias[l]), writes=[bpar], queue="pool")
                p.dma(lambda e: e.dma_start(out=db[:], in_=dbias), writes=[bpar])
                p.dma(lambda e: e.dma_start(out=cd[:], in_=cdist), writes=[bpar])
                bsm = Buf()
                p.op("dve", lambda e: e.memset(sm[:], 0.0), writes=[bsm])
                p.dma(lambda e: e.dma_start(out=sm[64:65, 0:4], in_=sink[l:l + 1, :]), writes=[bsm])
                lam4 = sb([128, 128], F32)
                p.op("dve", lambda e: e.memset(lam4[:], 0.0), writes=[bsm])
                p.dma(lambda e: e.dma_start(out=lam4[64:65, :], in_=lamv[l:l + 1, :]), writes=[bsm])
                p.dma(lambda e: e.dma_start(out=gcol[:, 0:1], in_=subln[l]), writes=[bsm])
                p.op("act", lambda e: e.activation(sm[64:65, 4:8], sm[64:65, 0:4], AF.Exp), reads=[bsm], writes=[bsm])
                p.op("dve", lambda e: e.tensor_tensor(lam4[64:65, 0:32], lam4[64:65, 0:32], lam4[64:65, 32:64], ALU.mult), reads=[bsm], writes=[bsm])
                p.op("dve", lambda e: e.tensor_tensor(lam4[64:65, 64:96], lam4[64:65, 64:96], lam4[64:65, 96:128], ALU.mult), reads=[bsm], writes=[bsm])
                p.op("dve", lambda e: e.tensor_reduce(sm[64:65, 8:9], lam4[64:65, 0:32], axis=AX.X, op=ALU.add), reads=[bsm], writes=[bsm])
                p.op("dve", lambda e: e.tensor_reduce(sm[64:65, 9:10], lam4[64:65, 64:96], axis=AX.X, op=ALU.add), reads=[bsm], writes=[bsm])
                p.op("act", lambda e: e.activation(sm[64:65, 10:12], sm[64:65, 8:10], AF.Exp), reads=[bsm], writes=[bsm])
                p.op("dve", lambda e: e.tensor_tensor(sm[64:65, 12:13], sm[64:65, 11:12], sm[64:65, 10:11], ALU.subtract), reads=[bsm], writes=[bsm])
                p.op("dve", lambda e: e.tensor_scalar(sm[64:65, 13:14], sm[64:65, 12:13], -lambda_init, None, op0=ALU.add), reads=[bsm], writes=[bsm])
                nlam = sm[64:65, 13:14]
                p.op("act", lambda e: e.mul(gcol[:, 1:2], gcol[:, 0:1], 1.0 - lambda_init), reads=[bsm], writes=[bsm])

                cbt = sb([128, 512], F32)
                cbt_map = {}
                bcbt = Buf()
                for S_ in sorted(set(seqs)):
                    for qb_ in range(S_ // 512):
                        for kt_ in range(S_ // 128):
                            dl_ = qb_ * 512 - kt_ * 128
                            for h_ in range(4):
                                v_ = float(-DIFF_SLOPES[h_] * abs(dl_)) if (dl_ >= 128 or dl_ <= -512) else 0.0
                                if v_ not in cbt_map:
                                    cbt_map[v_] = len(cbt_map)
                assert len(cbt_map) <= 512
                for v_, idx_ in cbt_map.items():
                    p.op("pool", lambda e, idx_=idx_, v_=v_: e.memset(cbt[:, idx_:idx_ + 1], v_), writes=[bcbt])

                def bias_col(v):
                    i = cbt_map[float(v)]
                    return cbt[:, i:i + 1]

                sbank_r = Ring([ps([128, 512], F32) for _ in range(4)])
                acc_r = Ring([ps([128, 512], F32) for _ in range(4)])
                fin_r = sbank_r
                tmp_r = Ring([sb([128, 512], F32) for _ in range(4)])
                pt_r = Ring([sb([128, 512], BF16) for _ in range(12)])
                rl_r = Ring([sb([128, 512], F32) for _ in range(2)])
                bcs_r = Ring([sb([64, 512], F32) for _ in range(2)])
                ot_r = Ring([sb([64, 512], BF16) for _ in range(3)])
                f32w_r = Ring([sb([64, 512], F32) for _ in range(6)])

                def load_mixer(name, si):
                    S = seqs[si]
                    T = S // 128
                    s0 = seq_start[si]
                    mx = MIX[name]
                    for i, blk in enumerate(mx["qb"]):
                        p.dma(lambda e, i=i, blk=blk: e.dma_start(out=Qt[i][:, 0:S], in_=qkT_r[:, blk, s0:s0 + S]), writes=[bQ[i]])
                    if len(mx["kb"]) == 2:
                        for i, blk in enumerate(mx["kb"]):
                            p.dma(lambda e, i=i, blk=blk: e.dma_start(out=Kt[i][:, 0:S], in_=qkT_r[:, blk, s0:s0 + S]), writes=[bK[i]])
                    else:
                        blk = mx["kb"][0]
                        for j in range(2):
                            for half in range(2):
                                p.dma(lambda e, j=j, half=half: e.dma_start(out=Kt[j][half * 64:(half + 1) * 64, 0:S],
                                                                            in_=qkT_r[j * 64:(j + 1) * 64, blk, s0:s0 + S]), writes=[bK[j]])
                    vw = mx["vw"]
                    p.dma(lambda e: e.dma_start(out=Vraw[:, 0:T, 0:vw],
                                                in_=pj_d[s0:s0 + S, mx["vc"]:mx["vc"] + vw].rearrange("(t p) c -> p t c", p=128)),
                          writes=[bVraw])
                    for j in range(mx["nkv"]):
                        p.op("pool", lambda e, j=j: e.tensor_copy(Va[j][:, 0:T, 0:64], Vraw[:, 0:T, j * 64:(j + 1) * 64]),
                             reads=[bVraw], writes=[bVa[j]])
                        p.op("pool", lambda e, j=j: e.memset(Va[j][:, 0:T, 64:65], 1.0), writes=[bVa[j]])

                def finalize_simple(acc, bacc, ncols, dest_fn, add_sink=False, split4=False):
                    rl, brl = rl_r.next()
                    if add_sink:
                        for h in range(4):
                            p.op("dve", lambda e, h=h: e.tensor_scalar(rl[64:65, h * 128:(h + 1) * 128], acc[64:65, h * 128:(h + 1) * 128],
                                                                      sm[64:65, 4 + h:5 + h], None, op0=ALU.add), reads=[bacc, bsm], writes=[brl])
                        p.op("dve", lambda e: e.reciprocal(rl[64:65, 0:ncols], rl[64:65, 0:ncols]), reads=[brl], writes=[brl])
                    else:
                        p.op("dve", lambda e: e.reciprocal(rl[64:65, 0:ncols], acc[64:65, 0:ncols]), reads=[bacc], writes=[brl])
                    fin, bfin = fin_r.next()
                    p.op("pe", lambda e: e.matmul(fin[0:64, 0:ncols], lhsT=onesf[64:65, 0:64], rhs=rl[64:65, 0:ncols], start=True, stop=True),
                         reads=[brl, b_const], writes=[bfin])
                    bcs, bbcs = bcs_r.next()
                    p.op("act", lambda e: e.copy(bcs[:, 0:ncols], fin[0:64, 0:ncols]), reads=[bfin], writes=[bbcs])
                    ot, bot = ot_r.next()
                    p.op("dve", lambda e: e.tensor_tensor(ot[:, 0:ncols], acc[0:64, 0:ncols], bcs[:, 0:ncols], ALU.mult),
                         reads=[bacc, bbcs], writes=[bot])
                    if split4:
                        for h4 in range(4):
                            p.dma(lambda e, h4=h4: e.dma_start(out=dest_fn(h4), in_=ot[:, h4 * 128:(h4 + 1) * 128]), reads=[bot])
                    else:
                        p.dma(lambda e: e.dma_start(out=dest_fn(), in_=ot[:, 0:ncols]), reads=[bot])
                    return ot

                def local_attn(name, si):
                    import os
                    LSK = os.environ.get("MK_LSK", "")
                    S = seqs[si]
                    T = S // 128
                    s0 = seq_start[si]
                    mx = MIX[name]
                    isA = name == "A"
                    gqa = not isA

                    def keyt(m):
                        if isA:
                            return na_tiles(m, T)
                        return [(m + dl, dl + 1) for dl in (-1, 0, 1) if 0 <= m + dl < T]

                    def stage1(m):
                        pts = []
                        for (j, slab) in keyt(m):
                            tmp, btmp = tmp_r.next()
                            for g in range(2):
                                sbk, bsb = sbank_r.next()
                                pr = slice(64 * g, 64 * g + 64)
                                for i, h in enumerate((g, g + 2)):
                                    Kh, bKh = (Kt[h // 2], bK[h // 2])
                                    Qh, bQh = Qt[h // 2], bQ[h // 2]
                                    p.op("pe", lambda e, sbk=sbk, i=i, pr=pr, Kh=Kh, Qh=Qh, j=j, m=m: e.matmul(
                                        sbk[:, i * 128:(i + 1) * 128], lhsT=Kh[pr, j * 128:(j + 1) * 128], rhs=Qh[pr, m * 128:(m + 1) * 128],
                                        start=True, stop=True), reads=[bKh, bQh], writes=[bsb])
                                p.op("act", lambda e, tmp=tmp, sbk=sbk, g=g: e.mul(tmp[:, g * 256:(g + 1) * 256], sbk[:, 0:256], 0.125), reads=[bsb], writes=[btmp])
                            bias_ap = ab[:, slab * 512:(slab + 1) * 512] if isA else db[:, slab * 512:(slab + 1) * 512]
                            p.op("dve", lambda e, tmp=tmp, bias_ap=bias_ap: e.tensor_tensor(tmp[:], tmp[:], bias_ap, ALU.add),
                                 reads=[bpar], writes=[btmp])
                            pt, bpt = pt_r.next()
                            p.op("act", lambda e, pt=pt, tmp=tmp: e.activation(pt[:], tmp[:], AF.Exp), reads=[btmp], writes=[bpt])
                            pts.append((j, pt, bpt))
                        return pts

                    def stage2(m, pts):
                        if "2" in LSK:
                            return
                        acc, bacc = acc_r.next()
                        for h in range(4):
                            kv = h if isA else h // 2
                            for ii, (j, pt, bpt) in enumerate(pts):
                                hp = (0, 2, 1, 3)[h]
                                p.op("pe", lambda e, acc=acc, h=h, kv=kv, j=j, pt=pt, ii=ii, hp=hp: e.matmul(
                                    acc[0:65, h * 128:(h + 1) * 128], lhsT=Va[kv][:, j, 0:65], rhs=pt[:, hp * 128:(hp + 1) * 128],
                                    start=(ii == 0), stop=(ii == len(pts) - 1)), reads=[bVa[kv], bpt], writes=[bacc])
                        t0 = s0 + m * 128
                        if "f" in LSK:
                            return
                        finalize_simple(acc, bacc, 512,
                                        lambda h4: oT_h[:, mx["idx"] * 4 + h4, t0:t0 + 128],
                                        add_sink=(name == "D"), split4=True)

                    prev = stage1(0)
                    for m in range(T):
                        nxt = stage1(m + 1) if m + 1 < T else None
                        stage2(m, prev)
                        prev = nxt

                def finalize_c(qb, h, t0, make_stream, run_streams):
                    a1, ba1 = acc_r.next()
                    a2, ba2 = acc_r.next()
                    run_streams([make_stream(qb, h, 0, a1, ba1), make_stream(qb, h, 1, a2, ba2)])
                    rl, brl = rl_r.next()
                    rl2, brl2 = rl_r.next()
                    p.op("dve", lambda e: e.reciprocal(rl[64:65, :], a1[64:65, :]), reads=[ba1], writes=[brl])
                    p.op("dve", lambda e: e.reciprocal(rl2[64:65, :], a2[64:65, :]), reads=[ba2], writes=[brl2])
                    p.op("dve", lambda e: e.tensor_scalar(rl2[64:65, :], rl2[64:65, :], nlam, None, op0=ALU.mult), reads=[brl2, bsm], writes=[brl2])
                    o1, bo1 = f32w_r.next()
                    o2, bo2 = f32w_r.next()
                    for (rr, brr, aa, baa, oo, boo) in ((rl, brl, a1, ba1, o1, bo1), (rl2, brl2, a2, ba2, o2, bo2)):
                        fin, bfin = fin_r.next()
                        p.op("pe", lambda e, fin=fin, rr=rr: e.matmul(fin[0:64, :], lhsT=onesf[64:65, 0:64], rhs=rr[64:65, :], start=True, stop=True),
                             reads=[brr, b_const], writes=[bfin])
                        bcs, bbcs = bcs_r.next()
                        p.op("act", lambda e, bcs=bcs, fin=fin: e.copy(bcs[:], fin[0:64, :]), reads=[bfin], writes=[bbcs])
                        p.op("dve", lambda e, oo=oo, aa=aa, bcs=bcs: e.tensor_tensor(oo[:], aa[0:64, :], bcs[:], ALU.mult),
                             reads=[baa, bbcs], writes=[boo])
                    p.op("pool", lambda e: e.tensor_tensor(o1[:], o1[:], o2[:], ALU.add), reads=[bo2], writes=[bo1])
                    p.op("pool", lambda e: e.tensor_tensor(o2[:], o1[:], o1[:], ALU.mult), reads=[bo1], writes=[bo2])
                    fin2, bfin2 = fin_r.next()
                    p.op("pe", lambda e: e.matmul(fin2[0:64, :], lhsT=onesf[0:64, 0:64], rhs=o2[:], start=True, stop=True),
                         reads=[bo2, b_const], writes=[bfin2])
                    sd, bsd = f32w_r.next()
                    p.op("act", lambda e: e.activation(sd[:], fin2[0:64, :], AF.Sqrt, bias=EPS, scale=1.0 / 64), reads=[bfin2], writes=[bsd])
                    p.op("dve", lambda e: e.reciprocal(sd[:], sd[:]), reads=[bsd], writes=[bsd])
                    ot, bot = ot_r.next()
                    p.op("dve", lambda e: e.scalar_tensor_tensor(out=ot[:], in0=o1[:], scalar=gcol[:, 1:2], in1=sd[:], op0=ALU.mult, op1=ALU.mult),
                         reads=[bo1, bsd, bsm], writes=[bot])
                    p.dma(lambda e: e.dma_start(out=oT_h[:, 8 + h, t0:t0 + 512], in_=ot[:]), reads=[bot])

                def dense_attn(name, si):
                    S = seqs[si]
                    T = S // 128
                    s0 = seq_start[si]
                    mx = MIX[name]
                    isC = name == "C"
                    nq = S // 512
                    scale = (32 ** -0.5) if isC else 0.125

                    def make_stream(qb, h, mp, acc, bacc):
                        if isC:
                            base = 64 * (h % 2) + 32 * mp
                            pr = slice(base, base + 32)
                            Kh, bKh, Qh, bQh = Kt[h // 2], bK[h // 2], Qt[h // 2], bQ[h // 2]
                            kv = h
                            cs = DIFF_SLOPES[h] / scale
                        else:
                            base = 64 * (h % 2)
                            pr = slice(base, base + 64)
                            Kh, bKh, Qh, bQh = Kt[h // 2], bK[h // 2], Qt[h // 2], bQ[h // 2]
                            kv = h // 2
                        kw = dict(tile_position=(base, 0)) if base == 96 else {}

                        def qk(kt):
                            sbk, bsb = sbank_r.next()
                            p.op("pe", lambda e: e.matmul(sbk[:], lhsT=Kh[pr, kt * 128:(kt + 1) * 128], rhs=Qh[pr, qb * 512:(qb + 1) * 512],
                                                          start=True, stop=True, **kw), reads=[bKh, bQh], writes=[bsb])
                            pt, bpt = pt_r.next()
                            if not isC:
                                p.op("act", lambda e: e.activation(pt[:], sbk[:], AF.Exp, scale=scale), reads=[bsb], writes=[bpt])
                            else:
                                dl = qb * 512 - kt * 128
                                tmp, btmp = tmp_r.next()
                                if dl >= 128:
                                    src_ap, coef, bias = cd[:, 0:512], -cs, -DIFF_SLOPES[h] * dl
                                elif dl <= -512:
                                    src_ap, coef, bias = cd[:, 0:512], cs, DIFF_SLOPES[h] * dl
                                else:
                                    di = {0: 1, -128: 2, -256: 3, -384: 4}[dl]
                                    src_ap, coef, bias = cd[:, di * 512:(di + 1) * 512], -cs, 0.0
                                p.op("dve", lambda e: e.scalar_tensor_tensor(out=tmp[:], in0=src_ap, scalar=coef, in1=sbk[:],
                                                                             op0=ALU.mult, op1=ALU.add), reads=[bsb, bpar], writes=[btmp])
                                bc_ap = bias_col(bias)
                                p.op("act", lambda e: e.activation(pt[:], tmp[:], AF.Exp, bias=bc_ap, scale=scale), reads=[btmp, bcbt], writes=[bpt])
                            return pt, bpt

                        def pv(kt, pt, bpt):
                            p.op("pe", lambda e: e.matmul(acc[0:65, :], lhsT=Va[kv][:, kt, 0:65], rhs=pt[:], start=(kt == 0), stop=(kt == T - 1)),
                                 reads=[bVa[kv], bpt], writes=[bacc])

                        return qk, pv

                    PD = 2

                    def run_streams(sts):
                        q = [[qk(i) for (qk, pv) in sts] for i in range(min(PD, T))]
                        for kt in range(T):
                            if kt + PD < T:
                                q.append([qk(kt + PD) for (qk, pv) in sts])
                            cur = q.pop(0)
                            for (qk, pv), pr in zip(sts, cur):
                                pv(kt, *pr)

                    for qb in range(nq):
                        t0 = s0 + qb * 512
                        if not isC:
                            for j in range(2):
                                accs = [acc_r.next(), acc_r.next()]
                                run_streams([make_stream(qb, 2 * j + g, 0, accs[g][0], accs[g][1]) for g in range(2)])
                                for g in range(2):
                                    finalize_simple(accs[g][0], accs[g][1], 512, lambda h=2 * j + g, t0=t0: oT_h[:, 4 + h, t0:t0 + 512])
                        else:
                            for h in range(4):
                                finalize_c(qb, h, t0, make_stream, run_streams)
                import os
                for si in range(len(seqs)):
                    for name in os.environ.get("MK_MIX", "ABCD"):
                        load_mixer(name, si)
                        if name in ("A", "D"):
                            local_attn(name, si)
                        else:
                            dense_attn(name, si)
            p.barrier()
            import os
            if dbg and l == 0 and "d" not in os.environ.get("MK_SKIP", ""):
                with ExitStack() as es2:
                    sb2, _ = mk(es2)
                    dr = Ring([sb2([128, 8, 128], BF16) for _ in range(2)])
                    dbg_c = dbg_o["oT"].rearrange("(c p) n -> p c n", p=128)
                    for t in range(NTILES):
                        tt, btt = dr.next()
                        p.dma(lambda e, tt=tt, t=t: e.dma_start(out=tt[:], in_=oT_c[:, :, t * 128:(t + 1) * 128]), writes=[btt])
                        p.dma(lambda e, tt=tt, t=t: e.dma_start(out=dbg_c[:, :, t * 128:(t + 1) * 128], in_=tt[:]), reads=[btt])
                p.barrier()

        def phase3a(l):
            src = x_in if l == 0 else y
            with ExitStack() as es:
                sb, ps = mk(es)
                Wb = sb([128, 8, D], BF16)
                Wo = sb([128, 8, D], BF16)
                bW = Buf()
                bWo = Buf()
                p.dma(lambda e: e.dma_start(out=Wb[:], in_=w_br[l]), writes=[bW], queue="pool")
                p.dma(lambda e: e.dma_start(out=Wo[:], in_=w_out[l]), writes=[bWo], queue="pool")
                gT = sb([128, 16], F32)
                gpo = sb([128, D], F32)
                bP = Buf()
                p.dma(lambda e: e.dma_start(out=gT[:], in_=gT_pre[l]), writes=[bP])
                p.dma(lambda e: e.dma_start(out=gpo[:], in_=g_post[l, 0:1, :].to_broadcast([128, D])), writes=[bP])
                xin_r = Ring([sb([128, D], F32) for _ in range(4)])
                oTt_r = Ring([sb([128, 8, 128], BF16) for _ in range(3)])
                gt_r = Ring([sb([128, 4096], BF16) for _ in range(3)])
                mg_r = Ring([sb([128, D], F32) for _ in range(2)])
                tm_r = Ring([sb([128, 512], F32) for _ in range(3)])
                mbf_r = Ring([sb([128, D], BF16) for _ in range(2)])
                mT_r = Ring([sb([128, 8, 128], BF16) for _ in range(2)])
                st_r = Ring([sb([128, 16], F32) for _ in range(2)])
                xn_r = Ring([sb([128, D], F32) for _ in range(2)])
                hbf_r = Ring([sb([128, D], BF16) for _ in range(2)])
                hT_r = Ring([sb([128, 8, 128], BF16) for _ in range(2)])
                junk = sb([128, D], BF16)
                bjunk = Buf()
                br_r = Ring([ps([128, 512], F32) for _ in range(2)])
                pT_r = Ring([ps([128, 8, 128], BF16) for _ in range(1)])
                ob_r = Ring([ps([128, 512], F32) for _ in range(4)])
                pT2_r = Ring([ps([128, 8, 128], BF16) for _ in range(1)])
                def L3(t):
                    r0 = t * 128
                    xin, bxin = xin_r.next()
                    p.dma(lambda e, xin=xin, r0=r0: e.dma_start(out=xin[:], in_=src[r0:r0 + 128, :]), writes=[bxin])
                    oTt, boT = oTt_r.next()
                    p.dma(lambda e, oTt=oTt, r0=r0: e.dma_start(out=oTt[:], in_=oT_c[:, :, r0:r0 + 128]), writes=[boT])
                    gt, bgt = gt_r.next()
                    p.dma(lambda e, gt=gt, r0=r0: e.dma_start(out=gt[:], in_=g_d[r0:r0 + 128, :]), writes=[bgt])
                    return (xin, bxin, oTt, boT, gt, bgt, r0)

                def S1(lctx):
                    xin, bxin, oTt, boT, gt, bgt, r0 = lctx
                    mg, bmg = mg_r.next()
                    for nh in range(2):
                        for i in range(4):
                            br, bbr = br_r.next()
                            for c in range(2):
                                p.op("pe", lambda e, br=br, oTt=oTt, i=i, c=c, nh=nh: e.matmul(
                                    br[:], lhsT=oTt[:, 2 * i + c, :], rhs=Wb[:, 2 * i + c, nh * 512:(nh + 1) * 512], start=(c == 0), stop=(c == 1)),
                                    reads=[boT, bW], writes=[bbr])
                            gsl = gt[:, i * 1024 + nh * 512:i * 1024 + (nh + 1) * 512]
                            if i == 0:
                                p.op("dve", lambda e, mg=mg, br=br, gsl=gsl, nh=nh: e.tensor_tensor(mg[:, nh * 512:(nh + 1) * 512], br[:], gsl, ALU.mult),
                                     reads=[bbr, bgt], writes=[bmg])
                            else:
                                tm, btm = tm_r.next()
                                p.op("dve", lambda e, tm=tm, br=br, gsl=gsl: e.tensor_tensor(tm[:], br[:], gsl, ALU.mult), reads=[bbr, bgt], writes=[btm])
                                p.op("pool", lambda e, mg=mg, tm=tm, nh=nh: e.tensor_tensor(mg[:, nh * 512:(nh + 1) * 512], mg[:, nh * 512:(nh + 1) * 512], tm[:], ALU.add),
                                     reads=[btm], writes=[bmg])
                    mbf, bmbf = mbf_r.next()
                    p.op("act", lambda e, mbf=mbf, mg=mg: e.copy(mbf[:], mg[:]), reads=[bmg], writes=[bmbf])
                    return (mbf, bmbf, xin, bxin, r0)

                def S2(c1):
                    mbf, bmbf, xin, bxin, r0 = c1
                    pT, bpT = pT_r.next()
                    for k in range(8):
                        p.op("pe", lambda e, pT=pT, mbf=mbf, k=k: e.transpose(pT[:, k, :], mbf[:, k * 128:(k + 1) * 128], ident[:]),
                             reads=[bmbf, b_const], writes=[bpT])
                    mT, bmT = mT_r.next()
                    p.op("dve", lambda e, mT=mT, pT=pT: e.tensor_copy(mT[:], pT[:]), reads=[bpT], writes=[bmT])
                    st, bst = st_r.next()
                    p.op("pool", lambda e, st=st: e.memset(st[:], 0.0), writes=[bst])
                    obs = []
                    for nh in range(2):
                        ob, bob = ob_r.next()
                        obs.append((ob, bob))
                        for k in range(8):
                            p.op("pe", lambda e, ob=ob, mT=mT, k=k, nh=nh: e.matmul(ob[:], lhsT=mT[:, k, :], rhs=Wo[:, k, nh * 512:(nh + 1) * 512],
                                                                               start=(k == 0), stop=(k == 7)), reads=[bmT, bWo], writes=[bob])
                        p.op("act", lambda e, ob=ob, st=st, nh=nh: e.activation(junk[:, 0:512], ob[:], AF.Square, accum_out=st[:, nh:nh + 1]),
                             reads=[bob], writes=[bst, bjunk])
                    p.op("dve", lambda e, st=st: e.tensor_tensor(st[:, 2:3], st[:, 0:1], st[:, 1:2], ALU.add), reads=[bst], writes=[bst])
                    p.op("act", lambda e, st=st: e.activation(st[:, 3:4], st[:, 2:3], AF.Sqrt, bias=EPS, scale=1.0 / D), reads=[bst], writes=[bst])
                    p.op("dve", lambda e, st=st: e.reciprocal(st[:, 4:5], st[:, 3:4]), reads=[bst], writes=[bst])
                    xn, bxn = xn_r.next()
                    for nh in range(2):
                        ob, bob = obs[nh]
                        tm, btm = tm_r.next()
                        p.op("dve", lambda e, tm=tm, ob=ob, st=st, nh=nh: e.scalar_tensor_tensor(
                            out=tm[:], in0=ob[:], scalar=st[:, 4:5], in1=gpo[:, nh * 512:(nh + 1) * 512], op0=ALU.mult, op1=ALU.mult),
                            reads=[bob, bst, bP], writes=[btm])
                        p.op("pool", lambda e, xn=xn, tm=tm, xin=xin, nh=nh: e.tensor_tensor(xn[:, nh * 512:(nh + 1) * 512], tm[:], xin[:, nh * 512:(nh + 1) * 512], ALU.add),
                             reads=[btm, bxin], writes=[bxn])
                    p.dma(lambda e, xn=xn, r0=r0: e.dma_start(out=y[r0:r0 + 128, :], in_=xn[:]), reads=[bxn])
                    if dbg and l == 0:
                        p.dma(lambda e, xn=xn, r0=r0: e.dma_start(out=dbg_o["x1"][r0:r0 + 128, :], in_=xn[:]), reads=[bxn])
                    p.op("act", lambda e, xn=xn, st=st: e.activation(junk[:], xn[:], AF.Square, accum_out=st[:, 5:6]), reads=[bxn], writes=[bst, bjunk])
                    p.op("act", lambda e, st=st: e.activation(st[:, 6:7], st[:, 5:6], AF.Sqrt, bias=EPS, scale=1.0 / D), reads=[bst], writes=[bst])
                    p.op("dve", lambda e, st=st: e.reciprocal(st[:, 7:8], st[:, 6:7]), reads=[bst], writes=[bst])
                    hbf, bhbf = hbf_r.next()
                    p.op("dve", lambda e, hbf=hbf, xn=xn, st=st: e.tensor_scalar(hbf[:], xn[:], st[:, 7:8], None, op0=ALU.mult), reads=[bxn, bst], writes=[bhbf])
                    return (hbf, bhbf, r0)

                def S3(c2):
                    hbf, bhbf, r0 = c2
                    pT2, bpT2 = pT2_r.next()
                    for k in range(8):
                        p.op("pe", lambda e, pT2=pT2, hbf=hbf, k=k: e.transpose(pT2[:, k, :], hbf[:, k * 128:(k + 1) * 128], ident[:]),
                             reads=[bhbf, b_const], writes=[bpT2])
                    hTt, bhT = hT_r.next()
                    p.op("dve", lambda e, hTt=hTt, pT2=pT2: e.tensor_tensor(hTt[:], pT2[:], gT[:, 8:16].unsqueeze(2).to_broadcast([128, 8, 128]), ALU.mult),
                         reads=[bpT2, bP], writes=[bhT])
                    p.dma(lambda e, hTt=hTt, r0=r0: e.dma_start(out=hT_c[:, :, r0:r0 + 128], in_=hTt[:]), reads=[bhT])

                lq = [L3(i) for i in range(min(2, NTILES))]
                q1, q2 = [], []
                for t in range(NTILES + 2):
                    if t < NTILES:
                        if t + 2 < NTILES:
                            lq.append(L3(t + 2))
                        q1.append(S1(lq.pop(0)))
                    if 0 <= t - 1 < NTILES:
                        q2.append(S2(q1.pop(0)))
                    if t - 2 >= 0:
                        S3(q2.pop(0))
            p.barrier()

        def phase3b(l):
            with ExitStack() as es:
                sb, ps = mk(es)
                Wu = sb([128, 8, 5632], BF16)
                Wd = sb([128, 22, D], BF16)
                bWu = [Buf() for _ in range(8)]
                bWd = Buf()
                for k in range(8):
                    p.dma(lambda e, k=k: e.dma_start(out=Wu[:, k, :], in_=w_up[l, :, k, :]), writes=[bWu[k]], queue="pool")
                p.dma(lambda e: e.dma_start(out=Wd[:], in_=w_dn[l]), writes=[bWd], queue="pool")
                cw = sb([128, 44, 4], F32)
                gpo = sb([128, D], F32)
                bP = Buf()
                p.dma(lambda e: e.dma_start(out=cw[:], in_=convp[l]), writes=[bP])
                p.dma(lambda e: e.dma_start(out=gpo[:], in_=g_post[l, 1:2, :].to_broadcast([128, D])), writes=[bP])
                hs_r = Ring([sb([128, 8, 258], BF16) for _ in range(2)])
                xin_r = Ring([sb([128, D], F32) for _ in range(4)])
                cg_r = Ring([sb([128, 256], F32) for _ in range(2)])
                cv_r = Ring([sb([128, 256], F32) for _ in range(2)])
                gg_r = Ring([sb([128, 256], F32) for _ in range(2)])
                aT_r = Ring([sb([128, 22, 256], BF16) for _ in range(2)])
                tm_r = Ring([sb([128, 512], F32) for _ in range(2)])
                st_r = Ring([sb([128, 8], F32) for _ in range(2)])
                junk = sb([128, 512], BF16)
                bjunk = Buf()
                u_r = Ring([ps([128, 512], F32) for _ in range(4)])
                d_r = Ring([ps([128, 512], F32) for _ in range(4)])
                blocks = []
                for si, S in enumerate(seqs):
                    for b in range(S // 256):
                        blocks.append((seq_start[si] + b * 256, b == 0, b == S // 256 - 1))

                def LB(bi):
                    t0, first, last = blocks[bi]
                    hs, bhs = hs_r.next()
                    lo = 1 if first else 0
                    hi = 257 if last else 258
                    if first:
                        p.op("pool", lambda e: e.memset(hs[:, :, 0:1], 0.0), writes=[bhs])
                    if last:
                        p.op("pool", lambda e: e.memset(hs[:, :, 257:258], 0.0), writes=[bhs])
                    p.dma(lambda e: e.dma_start(out=hs[:, :, lo:hi], in_=hT_c[:, :, t0 - 1 + lo:t0 - 1 + hi]), writes=[bhs])
                    return (hs, bhs, t0)

                def UP(lctx):
                    hs, bhs, t0 = lctx
                    xs = []
                    for a in range(2):
                        r0 = t0 + a * 128
                        xin, bxin = xin_r.next()
                        p.dma(lambda e, xin=xin, r0=r0: e.dma_start(out=xin[:], in_=y[r0:r0 + 128, :]), writes=[bxin])
                        xs.append((xin, bxin, r0))
                    aT, baT = aT_r.next()
                    for fp in range(22):
                        us = []
                        for ch in (fp, fp + 22):
                            u, bu = u_r.next()
                            us.append((u, bu, ch))
                            for k in range(8):
                                p.op("pe", lambda e, u=u, k=k, ch=ch: e.matmul(u[:, 0:258], lhsT=Wu[:, k, ch * 128:(ch + 1) * 128], rhs=hs[:, k, :],
                                                                          start=(k == 0), stop=(k == 7)), reads=[bhs, bWu[k]], writes=[bu])
                        cg, bcg = cg_r.next()
                        cv, bcv = cv_r.next()
                        for (u, bu, ch), (c_, bc_) in zip(us, ((cg, bcg), (cv, bcv))):
                            p.op("act", lambda e, c_=c_, u=u, ch=ch: e.activation(c_[:], u[:, 0:256], AF.Identity, bias=cw[:, ch, 3:4], scale=cw[:, ch, 0:1]),
                                 reads=[bu, bP], writes=[bc_])
                            p.op("dve", lambda e, c_=c_, u=u, ch=ch: e.scalar_tensor_tensor(out=c_[:], in0=u[:, 1:257], scalar=cw[:, ch, 1:2], in1=c_[:], op0=ALU.mult, op1=ALU.add),
                                 reads=[bu, bP], writes=[bc_])
                            p.op("dve", lambda e, c_=c_, u=u, ch=ch: e.scalar_tensor_tensor(out=c_[:], in0=u[:, 2:258], scalar=cw[:, ch, 2:3], in1=c_[:], op0=ALU.mult, op1=ALU.add),
                                 reads=[bu, bP], writes=[bc_])
                        gg, bgg = gg_r.next()
                        p.op("act", lambda e, gg=gg, cg=cg: e.activation(gg[:], cg[:], AF.Gelu_apprx_tanh), reads=[bcg], writes=[bgg])
                        p.op("pool", lambda e, gg=gg, cv=cv, fp=fp: e.tensor_tensor(aT[:, fp, :], gg[:], cv[:], ALU.mult), reads=[bgg, bcv], writes=[baT])
                    return (aT, baT, xs)

                def DOWN(uctx):
                    aT, baT, xs = uctx
                    for a in range(2):
                        xin, bxin, r0 = xs[a]
                        st, bst = st_r.next()
                        p.op("pool", lambda e, st=st: e.memset(st[:], 0.0), writes=[bst])
                        dbs = []
                        for nh in range(2):
                            dbk, bdb = d_r.next()
                            dbs.append((dbk, bdb))
                            for f in range(22):
                                p.op("pe", lambda e, dbk=dbk, f=f, a=a, nh=nh: e.matmul(dbk[:], lhsT=aT[:, f, a * 128:(a + 1) * 128], rhs=Wd[:, f, nh * 512:(nh + 1) * 512],
                                                                                   start=(f == 0), stop=(f == 21)), reads=[baT, bWd], writes=[bdb])
                            p.op("act", lambda e, dbk=dbk, st=st, nh=nh: e.activation(junk[:], dbk[:], AF.Square, accum_out=st[:, nh:nh + 1]), reads=[bdb], writes=[bst, bjunk])
                        p.op("dve", lambda e, st=st: e.tensor_tensor(st[:, 2:3], st[:, 0:1], st[:, 1:2], ALU.add), reads=[bst], writes=[bst])
                        p.op("act", lambda e, st=st: e.activation(st[:, 3:4], st[:, 2:3], AF.Sqrt, bias=EPS, scale=1.0 / D), reads=[bst], writes=[bst])
                        p.op("dve", lambda e, st=st: e.reciprocal(st[:, 4:5], st[:, 3:4]), reads=[bst], writes=[bst])
                        for nh in range(2):
                            dbk, bdb = dbs[nh]
                            tm, btm = tm_r.next()
                            p.op("dve", lambda e, tm=tm, dbk=dbk, st=st, nh=nh: e.scalar_tensor_tensor(
                                out=tm[:], in0=dbk[:], scalar=st[:, 4:5], in1=gpo[:, nh * 512:(nh + 1) * 512], op0=ALU.mult, op1=ALU.mult),
                                reads=[bdb, bst, bP], writes=[btm])
                            p.op("pool", lambda e, tm=tm, xin=xin, nh=nh: e.tensor_tensor(xin[:, nh * 512:(nh + 1) * 512], tm[:], xin[:, nh * 512:(nh + 1) * 512], ALU.add),
                                 reads=[btm], writes=[bxin])
                        p.dma(lambda e, xin=xin, r0=r0: e.dma_start(out=y[r0:r0 + 128, :], in_=xin[:]), reads=[bxin])

                NB = len(blocks)
                lq = [LB(0)]
                prevu = None
                for bi in range(NB):
                    if bi + 1 < NB:
                        lq.append(LB(bi + 1))
                    uctx = UP(lq.pop(0))
                    if prevu is not None:
                        DOWN(prevu)
                    prevu = uctx
                DOWN(prevu)
            p.barrier()

        import os
        stop = int(os.environ.get("MK_STOP", "99"))
        for l in range(depth):
            phase1(l)
            if stop >= 2:
                phase2(l)
            if stop >= 3:
                phase3a(l)
            if stop >= 4:
                phase3b(l)
        p.emit()
    return nc, p


def prep_shared(inp, depth, smax):
    f = lambda a: np.ascontiguousarray(np.asarray(a, dtype=np.float32))
    w_in = f(inp["w_in"]).reshape(depth, 8, 128, 6656).transpose(0, 2, 1, 3)
    w_br = f(inp["w_branch"]).reshape(depth, 8, 128, D).transpose(0, 2, 1, 3)
    w_out = f(inp["w_out"]).reshape(depth, 8, 128, D).transpose(0, 2, 1, 3)
    w_up = f(inp["ffn_w_up"]).reshape(depth, 8, 128, 5632).transpose(0, 2, 1, 3)
    w_dn = f(inp["ffn_w_down"]).reshape(depth, 22, 128, D).transpose(0, 2, 1, 3)
    gT = np.concatenate([f(inp["norm_mix_pre"]).reshape(depth, 8, 128).transpose(0, 2, 1),
                         f(inp["norm_ffn_pre"]).reshape(depth, 8, 128).transpose(0, 2, 1)], axis=2)
    g_post = np.stack([f(inp["norm_mix_post"]), f(inp["norm_ffn_post"])], axis=1)
    cw = f(inp["ffn_conv_w"]).reshape(depth, 3, 44, 128)
    cb = f(inp["ffn_conv_b"]).reshape(depth, 1, 44, 128)
    convp = np.concatenate([cw, cb], axis=1).transpose(0, 3, 2, 1)
    cosf, sinf = rope_tables(smax)
    ropec = np.concatenate([cosf, sinf], axis=1)
    gqk = np.concatenate([np.tile(f(inp["gqa_q_norm"]), (1, 4)), np.tile(f(inp["gqa_k_norm"]), (1, 2))], axis=1)
    rpb = f(inp["na_rpb"])
    abias = np.stack([build_abias(rpb[l]) for l in range(depth)]).reshape(depth, 128, NA_SLABS * 512)
    dbias = build_dbias().reshape(128, 3 * 512)
    ii = np.arange(128, dtype=np.float32)[:, None]
    jj = np.arange(512, dtype=np.float32)[None, :]
    base = jj - ii
    cdist = np.concatenate([base] + [np.abs(base + d) for d in (0.0, -128.0, -256.0, -384.0)], axis=1).astype(np.float32)
    lamv = np.concatenate([f(inp["diff_lambda_q1"]), f(inp["diff_lambda_k1"]), f(inp["diff_lambda_q2"]), f(inp["diff_lambda_k2"])], axis=1)
    subln = f(inp["diff_subln"]).reshape(depth, 64, 1)
    sink = f(inp["swa_sink"])
    c = np.ascontiguousarray
    return dict(w_in=c(w_in), w_br=c(w_br), w_out=c(w_out), w_up=c(w_up), w_dn=c(w_dn), gT_pre=c(gT), g_post=c(g_post),
                convp=c(convp), ropec=c(ropec), gqk=c(gqk), abias=c(abias), dbias=c(dbias), cdist=c(cdist), lamv=c(lamv),
                subln=c(subln), sink=c(sink))


_CACHE = {}


def kernel(**inp):
    xp = np.asarray(inp["x_prompt"], dtype=np.float32)
    xs = np.asarray(inp["x_sample"], dtype=np.float32)
    depth = int(np.asarray(inp["w_in"]).shape[0])
    ncores = 8
    bp, sp_ = xp.shape[0], xp.shape[1]
    bs, ss_ = xs.shape[0], xs.shape[1]
    npc, nsc = bp // ncores, bs // ncores
    seqs = [sp_] * npc + [ss_] * nsc
    key = (tuple(seqs), depth)
    if key not in _CACHE:
        _CACHE[key] = build(seqs, depth)[0]
    nc = _CACHE[key]
    shared = prep_shared(inp, depth, max(seqs))
    in_maps = []
    for c in range(ncores):
        xc = np.concatenate([xp[c * npc:(c + 1) * npc].reshape(-1, D), xs[c * nsc:(c + 1) * nsc].reshape(-1, D)], axis=0)
        m = dict(shared)
        m["x"] = np.ascontiguousarray(xc)
        in_maps.append(m)
    res = run_bass_kernel_spmd(nc, in_maps, core_ids=list(range(ncores)))
    yp = np.empty_like(xp)
    ys = np.empty_like(xs)
    for c in range(ncores):
        yc = res.results[c]["y"]
        yp[c * npc:(c + 1) * npc] = yc[:npc * sp_].reshape(npc, sp_, D)
        ys[c * nsc:(c + 1) * nsc] = yc[npc * sp_:].reshape(nsc, ss_, D)
    return (yp, ys)
```

```python
import math
from contextlib import ExitStack
import numpy as np
import concourse.bass as bass
import concourse.mybir as mybir
from concourse.bass_utils import run_bass_kernel_spmd

F32 = mybir.dt.float32
BF16 = mybir.dt.bfloat16
AF = mybir.ActivationFunctionType
ALU = mybir.AluOpType
AX = mybir.AxisListType

COMPUTE = ("pe", "act", "dve", "pool")
ALLENG = COMPUTE + ("sp",)
NCHAINS = 48
EPS = 1e-6
NEG = -1e30


class Buf:
    __slots__ = ("w", "r")

    def __init__(self):
        self.w = None
        self.r = []


class Op:
    __slots__ = ("eng", "fn", "waits", "signal", "pos", "is_dma", "chain", "dma_val", "cnt")

    def __init__(self, eng, fn):
        self.eng = eng
        self.fn = fn
        self.waits = []
        self.signal = False
        self.pos = -1
        self.is_dma = False
        self.chain = None
        self.dma_val = 0
        self.cnt = 0


class Chain:
    __slots__ = ("sem", "count", "last")

    def __init__(self, sem):
        self.sem = sem
        self.count = 0
        self.last = None


class Prog:
    def __init__(self, nc):
        self.nc = nc
        self.ops = {e: [] for e in ALLENG}
        self.sem = {e: nc.alloc_semaphore(name="s_" + e) for e in ALLENG}
        self.known = {e: {p: -1 for p in ALLENG} for e in ALLENG}
        self.known_chain = {e: {} for e in ALLENG}
        self.chains = [Chain(nc.alloc_semaphore(name=f"c{i}")) for i in range(NCHAINS)]
        self.ci = 0
        self.nops = 0

    def _add_wait(self, op, d):
        E = op.eng
        if d.is_dma:
            kc = self.known_chain[E]
            if kc.get(id(d.chain), 0) >= d.dma_val:
                return
            kc[id(d.chain)] = d.dma_val
            op.waits.append(d)
        else:
            if d.eng == E and E == "pe":
                return
            kn = self.known[E]
            if kn[d.eng] >= d.pos:
                return
            kn[d.eng] = d.pos
            d.signal = True
            op.waits.append(d)

    def _deps(self, op, reads, writes):
        for b in reads:
            if b.w is not None:
                self._add_wait(op, b.w)
        for b in writes:
            if b.w is not None:
                self._add_wait(op, b.w)
            for r in b.r:
                self._add_wait(op, r)
        for b in reads:
            b.r.append(op)
        for b in writes:
            b.w = op
            b.r = []

    def op(self, eng, fn, reads=(), writes=()):
        o = Op(eng, fn)
        o.pos = len(self.ops[eng])
        self._deps(o, reads, writes)
        self.ops[eng].append(o)
        self.nops += 1
        return o

    def dma(self, fn, reads=(), writes=(), queue="sp"):
        ch = self.chains[self.ci]
        self.ci = (self.ci + 1) % NCHAINS
        o = Op(queue, fn)
        o.is_dma = True
        o.chain = ch
        o.pos = len(self.ops[queue])
        if ch.last is not None:
            self._add_wait(o, ch.last)
        self._deps(o, reads, writes)
        ch.count += 16
        o.dma_val = ch.count
        ch.last = o
        self.ops[queue].append(o)
        self.nops += 1
        return o

    def barrier(self):
        sp_wait = []
        for e in COMPUTE:
            if self.ops[e]:
                last = self.ops[e][-1]
                if last.is_dma or last.fn is None:
                    last = self.op(e, lambda en: en.nop())
                last.signal = True
                sp_wait.append(last)
        o = Op("sp", lambda en: en.nop())
        o.pos = len(self.ops["sp"])
        o.waits = sp_wait + [c.last for c in self.chains if c.last is not None]
        o.signal = True
        self.ops["sp"].append(o)
        for e in COMPUTE:
            w = Op(e, None)
            w.pos = len(self.ops[e])
            w.waits = [o]
            self.ops[e].append(w)
        for e in ALLENG:
            for q in ALLENG:
                self.known[e][q] = len(self.ops[q]) - 1
            for c in self.chains:
                self.known_chain[e][id(c)] = c.count

    def emit(self):
        nc = self.nc
        for e in ALLENG:
            c = 0
            for o in self.ops[e]:
                if o.signal and not o.is_dma:
                    c += 1
                o.cnt = c
        sem = self.sem

        def run(e, eng):
            for o in self.ops[e]:
                for d in o.waits:
                    if d.is_dma:
                        eng.wait_ge(d.chain.sem, d.dma_val)
                    else:
                        eng.wait_ge(sem[d.eng], d.cnt)
                if o.fn is None:
                    continue
                ins = o.fn(eng)
                if o.is_dma:
                    ins.then_inc(o.chain.sem, 16)
                elif o.signal:
                    ins.then_inc(sem[e], 1)

        with nc.Block() as block:
            @block.tensor
            def _(eng):
                run("pe", eng)

            @block.scalar
            def _(eng):
                run("act", eng)

            @block.vector
            def _(eng):
                run("dve", eng)

            @block.gpsimd
            def _(eng):
                run("pool", eng)

            @block.sync
            def _(eng):
                run("sp", eng)


class Ring:
    def __init__(self, tiles):
        self.t = tiles
        self.b = [Buf() for _ in tiles]
        self.i = -1

    def next(self):
        self.i = (self.i + 1) % len(self.t)
        return self.t[self.i], self.b[self.i]


D = 1024
NPJ = 2560
QK_SRC = [0, 128, 256, 384, 768, 896, 1024, 1280, 1408, 1536, 1664, 2048, 2176, 2304]
MIX = {
    "A": dict(qb=(0, 1), kb=(2, 3), vc=512, vw=256, nkv=4, idx=0),
    "B": dict(qb=(4, 5), kb=(6,), vc=1152, vw=128, nkv=2, idx=1),
    "C": dict(qb=(7, 8), kb=(9, 10), vc=1792, vw=256, nkv=4, idx=2),
    "D": dict(qb=(11, 12), kb=(13,), vc=2432, vw=128, nkv=2, idx=3),
}
_SL = 2.0 ** (-8.0 * np.arange(1, 9) / 8)
DIFF_SLOPES = [float(v) for v in _SL[0::2]]
SWA_SLOPES = [float(v) for v in _SL[1::2]]
NA_CLASSES = 5
NA_SLABS = 21


def na_tiles(m, T):
    if T <= 4:
        raise NotImplementedError
    if m == 0:
        return [(j, 5 + j) for j in range(4)]
    if m == 1:
        return [(j, 9 + j) for j in range(4)]
    if m == T - 2:
        return [(T - 4 + i, 13 + i) for i in range(4)]
    if m == T - 1:
        return [(T - 4 + i, 17 + i) for i in range(4)]
    return [(m - 2 + i, i) for i in range(5)]


def _na_slab_defs():
    T = 16
    reps = {0: 8, 1: 0, 2: 1, 3: T - 2, 4: T - 1}
    out = [None] * NA_SLABS
    for cls, m in reps.items():
        for (j, slab) in na_tiles(m, T):
            out[slab] = (m, j, T)
    return out


def build_abias(rpb):
    defs = _na_slab_defs()
    res = np.full((128, NA_SLABS, 4, 128), NEG, np.float32)
    kk = np.arange(128)
    qq = np.arange(128)
    for slab, (m, j, T) in enumerate(defs):
        rows = 2 * T
        qr = 2 * m + qq // 64
        qc = qq % 64
        kr = 2 * j + kk // 64
        kc = kk % 64
        r0 = np.clip(qr - 4, 0, rows - 8)
        c0 = np.clip(qc - 8, 0, 64 - 16)
        vr = (kr[:, None] >= r0[None, :]) & (kr[:, None] < r0[None, :] + 8)
        vc = (kc[:, None] >= c0[None, :]) & (kc[:, None] < c0[None, :] + 16)
        valid = vr & vc
        dr = np.clip(kr[:, None] - qr[None, :] + 7, 0, 14)
        dc = np.clip(kc[:, None] - qc[None, :] + 15, 0, 30)
        for hp, h in enumerate((0, 2, 1, 3)):
            g = rpb[h][dr, dc]
            res[:, slab, hp, :] = np.where(valid, g, np.float32(NEG))
    return res


def build_dbias():
    res = np.full((128, 3, 4, 128), NEG, np.float32)
    kk = np.arange(128)[:, None]
    qq = np.arange(128)[None, :]
    for di, dl in enumerate((-1, 0, 1)):
        rel = np.abs(dl * 128 + kk - qq)
        for hp, h in enumerate((0, 2, 1, 3)):
            res[:, di, hp, :] = np.where(rel <= 128, -np.float32(SWA_SLOPES[h]) * rel.astype(np.float32), np.float32(NEG))
    return res


def rope_tables(S):
    t = np.arange(S)
    row = (t // 64).astype(np.float32)
    col = (t % 64).astype(np.float32)
    inv = (10000.0 ** (-np.arange(0, 32, 2, dtype=np.float32) / 32)).astype(np.float32)
    ar = row[:, None] * inv
    ac = col[:, None] * inv
    cr, sr, cc, sc = np.cos(ar), np.sin(ar), np.cos(ac), np.sin(ac)
    cosf = np.concatenate([cr, cr, cc, cc], 1).astype(np.float32)
    sinf = np.concatenate([-sr, sr, -sc, sc], 1).astype(np.float32)
    return cosf, sinf


def build(seqs, depth, dbg=False):
    NT = sum(seqs)
    NTILES = NT // 128
    SMAX = max(seqs)
    TMAX = SMAX // 128
    seq_start = [sum(seqs[:i]) for i in range(len(seqs))]
    nc = bass.Bass("TRN2", target_bir_lowering=False)

    def din(name, shape, dt=F32):
        return nc.dram_tensor(name, list(shape), dt, kind="ExternalInput").ap()

    x_in = din("x", [NT, D])
    w_in = din("w_in", [depth, 128, 8, 6656])
    w_br = din("w_br", [depth, 128, 8, D])
    w_out = din("w_out", [depth, 128, 8, D])
    w_up = din("w_up", [depth, 128, 8, 5632])
    w_dn = din("w_dn", [depth, 128, 22, D])
    gT_pre = din("gT_pre", [depth, 128, 16])
    g_post = din("g_post", [depth, 2, D])
    convp = din("convp", [depth, 128, 44, 4])
    ropec = din("ropec", [SMAX, 128])
    gqk = din("gqk", [depth, 384])
    abias = din("abias", [depth, 128, NA_SLABS * 512])
    dbias = din("dbias", [128, 3 * 512])
    cdist = din("cdist", [128, 5 * 512])
    cbias_d = din("cbias", [128, 512])
    cF_d = din("cF", [8, 512])
    lamv = din("lamv", [depth, 128])
    subln = din("subln", [depth, 64, 1])
    sink = din("sink", [depth, 4])
    y = nc.dram_tensor("y", [NT, D], F32, kind="ExternalOutput").ap()
    pj_d = nc.dram_tensor("pj_s", [NT, NPJ], BF16).ap()
    qkT_d = nc.dram_tensor("qkT_s", [14 * 128, NT], BF16).ap()
    g_d = nc.dram_tensor("g_s", [NT, 4096], BF16).ap()
    oT_d = nc.dram_tensor("oT_s", [D, NT], BF16).ap()
    hT_d = nc.dram_tensor("hT_s", [D, NT], BF16).ap()
    dbg_o = {}
    if dbg:
        dbg_o["pj"] = nc.dram_tensor("dbg_pj", [NT, NPJ], BF16, kind="ExternalOutput").ap()
        dbg_o["oT"] = nc.dram_tensor("dbg_oT", [D, NT], BF16, kind="ExternalOutput").ap()
        dbg_o["x1"] = nc.dram_tensor("dbg_x1", [NT, D], F32, kind="ExternalOutput").ap()

    p = Prog(nc)
    qkT_r = qkT_d.rearrange("(b p) n -> p b n", p=128)
    oT_c = oT_d.rearrange("(c p) n -> p c n", p=128)
    oT_h = oT_d.rearrange("(h d) n -> d h n", d=64)
    hT_c = hT_d.rearrange("(c p) n -> p c n", p=128)

    with ExitStack() as top:
        uid = [0]

        def mk(es):
            def sb(shape, dt):
                uid[0] += 1
                return es.enter_context(nc.sbuf_tensor(f"sb{uid[0]}", list(shape), dt))

            def ps(shape, dt):
                uid[0] += 1
                return es.enter_context(nc.psum_tensor(f"ps{uid[0]}", list(shape), dt))
            return sb, ps

        sbT, _ = mk(top)
        ident = sbT([128, 128], BF16)
        identf = sbT([128, 128], F32)
        onesf = sbT([128, 64], F32)
        b_const = Buf()
        p.op("pool", lambda e: e.memset(identf[:], 0.0), writes=[b_const])
        p.op("pool", lambda e: e.affine_select(out=identf[:], in_=identf[:], compare_op=ALU.not_equal,
                                               fill=1.0, base=0, pattern=[[-1, 128]], channel_multiplier=1),
             writes=[b_const])
        p.op("pool", lambda e: e.tensor_copy(ident[:], identf[:]), writes=[b_const])
        p.op("pool", lambda e: e.memset(onesf[:], 1.0), writes=[b_const])

        def phase1(l):
            src = x_in if l == 0 else y
            with ExitStack() as es:
                sb, ps = mk(es)
                W = sb([128, 8, 6656], BF16)
                bW = [Buf() for _ in range(8)]
                for k in range(8):
                    p.dma(lambda e, k=k: e.dma_start(out=W[:, k, :], in_=w_in[l, :, k, :]), writes=[bW[k]], queue="pool")
                gT = sb([128, 16], F32)
                gqkt = sb([128, 384], F32)
                bP = Buf()
                p.dma(lambda e: e.dma_start(out=gT[:], in_=gT_pre[l]), writes=[bP])
                p.dma(lambda e: e.dma_start(out=gqkt[:], in_=gqk[l:l + 1, :].to_broadcast([128, 384])), writes=[bP])
                xin_r = Ring([sb([128, D], F32) for _ in range(3)])
                xbf_r = Ring([sb([128, D], BF16) for _ in range(2)])
                junk = sb([128, D], BF16)
                bjunk = Buf()
                st_r = Ring([sb([128, 8], F32) for _ in range(2)])
                xT_r = Ring([sb([128, 8, 128], BF16) for _ in range(2)])
                pj_r = Ring([sb([128, NPJ], BF16) for _ in range(3)])
                gt_r = Ring([sb([128, 4096], BF16) for _ in range(3)])
                bq_r = Ring([sb([128, 384], F32) for _ in range(2)])
                wk_r = Ring([sb([128, 5, 384], F32) for _ in range(2)])
                sb6_r = Ring([sb([128, 16], F32) for _ in range(2)])
                rp_r = Ring([sb([128, 128], F32) for _ in range(3)])
                qk_r = Ring([sb([128, 14, 128], BF16) for _ in range(2)])
                pT_r = Ring([ps([128, 8, 128], BF16) for _ in range(1)])
                po_r = Ring([ps([128, 512], F32) for _ in range(5)])
                pq_r = Ring([ps([128, 16, 128], BF16) for _ in range(1)])
                def loadA(t):
                    si = max(i for i in range(len(seqs)) if seq_start[i] <= t * 128)
                    pos = t * 128 - seq_start[si]
                    r0 = t * 128
                    xin, bxin = xin_r.next()
                    p.dma(lambda e, xin=xin, r0=r0: e.dma_start(out=xin[:], in_=src[r0:r0 + 128, :]), writes=[bxin])
                    rp, brp = rp_r.next()
                    p.dma(lambda e, rp=rp, pos=pos: e.dma_start(out=rp[:], in_=ropec[pos:pos + 128, :]), writes=[brp])
                    return (xin, bxin, rp, brp, r0)

                def partA(t, lctx):
                    xin, bxin, rp, brp, r0 = lctx
                    st, bst = st_r.next()
                    p.op("pool", lambda e, st=st: e.memset(st[:], 0.0), writes=[bst])
                    p.op("act", lambda e, xin=xin, st=st: e.activation(junk[:], xin[:], AF.Square, accum_out=st[:, 0:1]),
                         reads=[bxin], writes=[bst, bjunk])
                    p.op("act", lambda e, st=st: e.activation(st[:, 1:2], st[:, 0:1], AF.Sqrt, bias=EPS, scale=1.0 / D),
                         reads=[bst], writes=[bst])
                    p.op("dve", lambda e, st=st: e.reciprocal(st[:, 2:3], st[:, 1:2]), reads=[bst], writes=[bst])
                    xbf, bxbf = xbf_r.next()
                    p.op("dve", lambda e, xbf=xbf, xin=xin: e.tensor_copy(xbf[:], xin[:]), reads=[bxin], writes=[bxbf])
                    pT, bpT = pT_r.next()
                    for k in range(8):
                        p.op("pe", lambda e, pT=pT, xbf=xbf, k=k: e.transpose(pT[:, k, :], xbf[:, k * 128:(k + 1) * 128], ident[:]),
                             reads=[bxbf, b_const], writes=[bpT])
                    xT, bxT = xT_r.next()
                    p.op("dve", lambda e, xT=xT, pT=pT: e.tensor_tensor(xT[:], pT[:], gT[:, 0:8].unsqueeze(2).to_broadcast([128, 8, 128]), ALU.mult),
                         reads=[bpT, bP], writes=[bxT])
                    pj, bpj = pj_r.next()
                    gt, bgt = gt_r.next()
                    bq, bbq = bq_r.next()
                    rs = st[:, 2:3]
                    for n in range(13):
                        po, bpo = po_r.next()
                        for k in range(8):
                            p.op("pe", lambda e, po=po, xT=xT, k=k, n=n: e.matmul(po[:], lhsT=xT[:, k, :], rhs=W[:, k, n * 512:(n + 1) * 512],
                                                                             start=(k == 0), stop=(k == 7)),
                                 reads=[bxT, bW[k]], writes=[bpo])
                        if n >= 5:
                            p.op("act", lambda e, po=po, gt=gt, n=n, rs=rs: e.activation(gt[:, (n - 5) * 512:(n - 4) * 512], po[:], AF.Sigmoid, scale=rs),
                                 reads=[bpo, bst], writes=[bgt])
                        elif n == 1:
                            p.op("dve", lambda e, po=po, pj=pj, rs=rs: e.tensor_scalar(pj[:, 512:768], po[:, 0:256], rs, None, op0=ALU.mult),
                                 reads=[bpo, bst], writes=[bpj])
                            p.op("dve", lambda e, po=po, bq=bq, rs=rs: e.tensor_scalar(bq[:, 0:256], po[:, 256:512], rs, None, op0=ALU.mult),
                                 reads=[bpo, bst], writes=[bbq])
                        elif n == 2:
                            p.op("dve", lambda e, po=po, bq=bq, rs=rs: e.tensor_scalar(bq[:, 256:384], po[:, 0:128], rs, None, op0=ALU.mult),
                                 reads=[bpo, bst], writes=[bbq])
                            p.op("dve", lambda e, po=po, pj=pj, rs=rs: e.tensor_scalar(pj[:, 1152:1536], po[:, 128:512], rs, None, op0=ALU.mult),
                                 reads=[bpo, bst], writes=[bpj])
                        else:
                            p.op("dve", lambda e, po=po, pj=pj, rs=rs, n=n: e.tensor_scalar(pj[:, n * 512:(n + 1) * 512], po[:], rs, None, op0=ALU.mult),
                                 reads=[bpo, bst], writes=[bpj])
                    wk, bwk = wk_r.next()
                    s6, bs6 = sb6_r.next()
                    bq3 = bq[:].rearrange("p (h d) -> p h d", d=64)
                    sq3 = wk[:, 0, :].rearrange("p (h d) -> p h d", d=64)
                    qn = wk[:, 1, :]
                    qn3 = qn.rearrange("p (h d) -> p h d", d=64)
                    qn4 = qn.rearrange("p (a b c) -> p a b c", b=2, c=16)
                    rot4 = wk[:, 2, :].rearrange("p (a b c) -> p a b c", b=2, c=16)
                    rot3 = wk[:, 2, :].rearrange("p (h d) -> p h d", d=64)
                    t13 = wk[:, 3, :].rearrange("p (h d) -> p h d", d=64)
                    t23 = wk[:, 4, :].rearrange("p (h d) -> p h d", d=64)
                    p.op("pool", lambda e, wk=wk, bq=bq: e.tensor_tensor(wk[:, 0, :], bq[:], bq[:], ALU.mult), reads=[bbq], writes=[bwk])
                    p.op("dve", lambda e, s6=s6, sq3=sq3: e.tensor_reduce(s6[:, 0:6], sq3, axis=AX.X, op=ALU.add), reads=[bwk], writes=[bs6])
                    p.op("act", lambda e, s6=s6: e.activation(s6[:, 6:12], s6[:, 0:6], AF.Sqrt, bias=EPS, scale=1.0 / 64), reads=[bs6], writes=[bs6])
                    p.op("dve", lambda e, s6=s6: e.reciprocal(s6[:, 0:6], s6[:, 6:12]), reads=[bs6], writes=[bs6])
                    p.op("dve", lambda e, qn3=qn3, bq3=bq3, s6=s6: e.tensor_tensor(qn3, bq3, s6[:, 0:6].unsqueeze(2).to_broadcast([128, 6, 64]), ALU.mult),
                         reads=[bbq, bs6], writes=[bwk])
                    p.op("pool", lambda e, qn=qn: e.tensor_tensor(qn, qn, gqkt[:], ALU.mult), reads=[bwk, bP], writes=[bwk])
                    p.op("pool", lambda e, rot4=rot4, qn4=qn4: e.tensor_copy(rot4[:, :, 0, :], qn4[:, :, 1, :]), reads=[bwk], writes=[bwk])
                    p.op("pool", lambda e, rot4=rot4, qn4=qn4: e.tensor_copy(rot4[:, :, 1, :], qn4[:, :, 0, :]), reads=[bwk], writes=[bwk])
                    p.op("dve", lambda e, t13=t13, qn3=qn3, rp=rp: e.tensor_tensor(t13, qn3, rp[:, 0:64].unsqueeze(1).to_broadcast([128, 6, 64]), ALU.mult),
                         reads=[bwk, brp], writes=[bwk])
                    p.op("pool", lambda e, t23=t23, rot3=rot3, rp=rp: e.tensor_tensor(t23, rot3, rp[:, 64:128].unsqueeze(1).to_broadcast([128, 6, 64]), ALU.mult),
                         reads=[bwk, brp], writes=[bwk])
                    p.op("dve", lambda e, pj=pj, wk=wk: e.tensor_tensor(pj[:, 768:1152], wk[:, 3, :], wk[:, 4, :], ALU.add),
                         reads=[bwk], writes=[bpj])
                    return (pj, bpj, gt, bgt, r0)

                def partB(ctx):
                    pj, bpj, gt, bgt, r0 = ctx
                    pq, bpq = pq_r.next()
                    for bi, c0 in enumerate(QK_SRC):
                        p.op("pe", lambda e, pq=pq, pj=pj, bi=bi, c0=c0: e.transpose(pq[:, bi, :], pj[:, c0:c0 + 128], ident[:]),
                             reads=[bpj, b_const], writes=[bpq])
                    qk, bqk = qk_r.next()
                    p.op("act", lambda e, qk=qk, pq=pq: e.copy(qk[:], pq[:, 0:14, :]), reads=[bpq], writes=[bqk])
                    p.dma(lambda e, pj=pj, r0=r0: e.dma_start(out=pj_d[r0:r0 + 128, :], in_=pj[:]), reads=[bpj])
                    p.dma(lambda e, gt=gt, r0=r0: e.dma_start(out=g_d[r0:r0 + 128, :], in_=gt[:]), reads=[bgt])
                    p.dma(lambda e, qk=qk, r0=r0: e.dma_start(out=qkT_r[:, :, r0:r0 + 128], in_=qk[:]), reads=[bqk])
                    if dbg and l == 0:
                        p.dma(lambda e, pj=pj, r0=r0: e.dma_start(out=dbg_o["pj"][r0:r0 + 128, :], in_=pj[:]), reads=[bpj])

                prevctx = None
                lq = [loadA(0)]
                for t in range(NTILES):
                    if t + 1 < NTILES:
                        lq.append(loadA(t + 1))
                    ctx = partA(t, lq.pop(0))
                    if prevctx is not None:
                        partB(prevctx)
                    prevctx = ctx
                partB(prevctx)
            p.barrier()

        def phase2(l):
            lambda_init = 0.8 - 0.6 * math.exp(-0.3 * l)
            with ExitStack() as es:
                sb, ps = mk(es)
                Qt = [sb([128, SMAX], BF16) for _ in range(2)]
                Kt = [sb([128, SMAX], BF16) for _ in range(2)]
                bQ = [Buf(), Buf()]
                bK = [Buf(), Buf()]
                Vraw = sb([128, TMAX, 256], BF16)
                bVraw = Buf()
                Va = [sb([128, TMAX, 65], BF16) for _ in range(4)]
                bVa = [Buf() for _ in range(4)]
                ab = sb([128, NA_SLABS * 512], BF16)
                db = sb([128, 3 * 512], F32)
                cd = sb([128, 5 * 512], F32)
                sm = sb([128, 64], F32)
                gcol = sb([64, 2], F32)
                bpar = Buf()
                p.dma(lambda e: e.dma_start(out=ab[:], in_=abias[l]), writes=[bpar], queue="pool")
                p.dma(lambda e: e.dma_start(out=db[:], in_=dbias), writes=[bpar])
                p.dma(lambda e: e.dma_start(out=cd[:], in_=cdist), writes=[bpar])
                bsm = Buf()
                p.op("dve", lambda e: e.memset(sm[:], 0.0), writes=[bsm])
                p.dma(lambda e: e.dma_start(out=sm[64:65, 0:4], in_=sink[l:l + 1, :]), writes=[bsm])
                lam4 = sb([128, 128], F32)
                p.op("dve", lambda e: e.memset(lam4[:], 0.0), writes=[bsm])
                p.dma(lambda e: e.dma_start(out=lam4[64:65, :], in_=lamv[l:l + 1, :]), writes=[bsm])
                p.dma(lambda e: e.dma_start(out=gcol[:, 0:1], in_=subln[l]), writes=[bsm])
                p.op("act", lambda e: e.activation(sm[64:65, 4:8], sm[64:65, 0:4], AF.Exp), reads=[bsm], writes=[bsm])
                p.op("dve", lambda e: e.tensor_tensor(lam4[64:65, 0:32], lam4[64:65, 0:32], lam4[64:65, 32:64], ALU.mult), reads=[bsm], writes=[bsm])
                p.op("dve", lambda e: e.tensor_tensor(lam4[64:65, 64:96], lam4[64:65, 64:96], lam4[64:65, 96:128], ALU.mult), reads=[bsm], writes=[bsm])
                p.op("dve", lambda e: e.tensor_reduce(sm[64:65, 8:9], lam4[64:65, 0:32], axis=AX.X, op=ALU.add), reads=[bsm], writes=[bsm])
                p.op("dve", lambda e: e.tensor_reduce(sm[64:65, 9:10], lam4[64:65, 64:96], axis=AX.X, op=ALU.add), reads=[bsm], writes=[bsm])
                p.op("act", lambda e: e.activation(sm[64:65, 10:12], sm[64:65, 8:10], AF.Exp), reads=[bsm], writes=[bsm])
                p.op("dve", lambda e: e.tensor_tensor(sm[64:65, 12:13], sm[64:65, 11:12], sm[64:65, 10:11], ALU.subtract), reads=[bsm], writes=[bsm])
                p.op("dve", lambda e: e.tensor_scalar(sm[64:65, 13:14], sm[64:65, 12:13], -lambda_init, None, op0=ALU.add), reads=[bsm], writes=[bsm])
                nlam = sm[64:65, 13:14]
                p.op("act", lambda e: e.mul(gcol[:, 1:2], gcol[:, 0:1], 1.0 - lambda_init), reads=[bsm], writes=[bsm])

                cbt = sb([128, 512], F32)
                Ft = sb([128, 8, 512], F32)
                bcbt = Buf()
                p.dma(lambda e: e.dma_start(out=cbt[:], in_=cbias_d), writes=[bcbt])
                for r_ in range(8):
                    p.dma(lambda e, r_=r_: e.dma_start(out=Ft[:, r_, :], in_=cF_d[r_:r_ + 1, :].to_broadcast([128, 512])), writes=[bcbt])
                osum_r = Ring([sb([128, 512], F32) for _ in range(4)])
                tmpo_r = Ring([sb([128, 512], F32) for _ in range(2)])

                sbank_r = Ring([ps([128, 512], F32) for _ in range(4)])
                acc_r = Ring([ps([128, 512], F32) for _ in range(4)])
                fin_r = sbank_r
                tmp_r = Ring([sb([128, 512], F32) for _ in range(4)])
                pt_r = Ring([sb([128, 512], BF16) for _ in range(16)])
                rl_r = Ring([sb([128, 512], F32) for _ in range(2)])
                bcs_r = Ring([sb([64, 512], F32) for _ in range(2)])
                ot_r = Ring([sb([64, 512], BF16) for _ in range(3)])
                f32w_r = Ring([sb([64, 512], F32) for _ in range(6)])

                def load_mixer(name, si):
                    S = seqs[si]
                    T = S // 128
                    s0 = seq_start[si]
                    mx = MIX[name]
                    for i, blk in enumerate(mx["qb"]):
                        p.dma(lambda e, i=i, blk=blk: e.dma_start(out=Qt[i][:, 0:S], in_=qkT_r[:, blk, s0:s0 + S]), writes=[bQ[i]])
                    if len(mx["kb"]) == 2:
                        for i, blk in enumerate(mx["kb"]):
                            p.dma(lambda e, i=i, blk=blk: e.dma_start(out=Kt[i][:, 0:S], in_=qkT_r[:, blk, s0:s0 + S]), writes=[bK[i]])
                    else:
                        blk = mx["kb"][0]
                        for j in range(2):
                            for half in range(2):
                                p.dma(lambda e, j=j, half=half: e.dma_start(out=Kt[j][half * 64:(half + 1) * 64, 0:S],
                                                                            in_=qkT_r[j * 64:(j + 1) * 64, blk, s0:s0 + S]), writes=[bK[j]])
                    vw = mx["vw"]
                    p.dma(lambda e: e.dma_start(out=Vraw[:, 0:T, 0:vw],
                                                in_=pj_d[s0:s0 + S, mx["vc"]:mx["vc"] + vw].rearrange("(t p) c -> p t c", p=128)),
                          writes=[bVraw])
                    for j in range(mx["nkv"]):
                        p.op("pool", lambda e, j=j: e.tensor_copy(Va[j][:, 0:T, 0:64], Vraw[:, 0:T, j * 64:(j + 1) * 64]),
                             reads=[bVraw], writes=[bVa[j]])
                        p.op("pool", lambda e, j=j: e.memset(Va[j][:, 0:T, 64:65], 1.0), writes=[bVa[j]])

                def finalize_simple(acc, bacc, ncols, dest_fn, add_sink=False, split4=False):
                    rl, brl = rl_r.next()
                    if add_sink:
                        for h in range(4):
                            p.op("dve", lambda e, h=h: e.tensor_scalar(rl[64:65, h * 128:(h + 1) * 128], acc[64:65, h * 128:(h + 1) * 128],
                                                                      sm[64:65, 4 + h:5 + h], None, op0=ALU.add), reads=[bacc, bsm], writes=[brl])
                        p.op("dve", lambda e: e.reciprocal(rl[64:65, 0:ncols], rl[64:65, 0:ncols]), reads=[brl], writes=[brl])
                    else:
                        p.op("dve", lambda e: e.reciprocal(rl[64:65, 0:ncols], acc[64:65, 0:ncols]), reads=[bacc], writes=[brl])
                    fin, bfin = fin_r.next()
                    p.op("pe", lambda e: e.matmul(fin[0:64, 0:ncols], lhsT=onesf[64:65, 0:64], rhs=rl[64:65, 0:ncols], start=True, stop=True),
                         reads=[brl, b_const], writes=[bfin])
                    bcs, bbcs = bcs_r.next()
                    p.op("act", lambda e: e.copy(bcs[:, 0:ncols], fin[0:64, 0:ncols]), reads=[bfin], writes=[bbcs])
                    ot, bot = ot_r.next()
                    p.op("dve", lambda e: e.tensor_tensor(ot[:, 0:ncols], acc[0:64, 0:ncols], bcs[:, 0:ncols], ALU.mult),
                         reads=[bacc, bbcs], writes=[bot])
                    if split4:
                        for h4 in range(4):
                            p.dma(lambda e, h4=h4: e.dma_start(out=dest_fn(h4), in_=ot[:, h4 * 128:(h4 + 1) * 128]), reads=[bot])
                    else:
                        p.dma(lambda e: e.dma_start(out=dest_fn(), in_=ot[:, 0:ncols]), reads=[bot])
                    return ot

                def local_attn(name, si):
                    import os
                    LSK = os.environ.get("MK_LSK", "")
                    S = seqs[si]
                    T = S // 128
                    s0 = seq_start[si]
                    mx = MIX[name]
                    isA = name == "A"
                    gqa = not isA

                    def keyt(m):
                        if isA:
                            return na_tiles(m, T)
                        return [(m + dl, dl + 1) for dl in (-1, 0, 1) if 0 <= m + dl < T]

                    def stage1(m):
                        pts = []
                        for (j, slab) in keyt(m):
                            tmp, btmp = tmp_r.next()
                            for g in range(2):
                                sbk, bsb = sbank_r.next()
                                pr = slice(64 * g, 64 * g + 64)
                                for i, h in enumerate((g, g + 2)):
                                    Kh, bKh = (Kt[h // 2], bK[h // 2])
                                    Qh, bQh = Qt[h // 2], bQ[h // 2]
                                    p.op("pe", lambda e, sbk=sbk, i=i, pr=pr, Kh=Kh, Qh=Qh, j=j, m=m: e.matmul(
                                        sbk[:, i * 128:(i + 1) * 128], lhsT=Kh[pr, j * 128:(j + 1) * 128], rhs=Qh[pr, m * 128:(m + 1) * 128],
                                        start=True, stop=True), reads=[bKh, bQh], writes=[bsb])
                                bias_g = (ab if isA else db)[:, slab * 512 + g * 256:slab * 512 + (g + 1) * 256]
                                p.op("dve", lambda e, tmp=tmp, sbk=sbk, g=g, bias_g=bias_g: e.scalar_tensor_tensor(
                                    out=tmp[:, g * 256:(g + 1) * 256], in0=sbk[:, 0:256], scalar=0.125, in1=bias_g, op0=ALU.mult, op1=ALU.add),
                                    reads=[bsb, bpar], writes=[btmp])
                            pt, bpt = pt_r.next()
                            p.op("act", lambda e, pt=pt, tmp=tmp: e.activation(pt[:], tmp[:], AF.Exp), reads=[btmp], writes=[bpt])
                            pts.append((j, pt, bpt))
                        return pts

                    def stage2(m, pts):
                        if "2" in LSK:
                            return
                        acc, bacc = acc_r.next()
                        for h in range(4):
                            kv = h if isA else h // 2
                            for ii, (j, pt, bpt) in enumerate(pts):
                                hp = (0, 2, 1, 3)[h]
                                p.op("pe", lambda e, acc=acc, h=h, kv=kv, j=j, pt=pt, ii=ii, hp=hp: e.matmul(
                                    acc[0:65, h * 128:(h + 1) * 128], lhsT=Va[kv][:, j, 0:65], rhs=pt[:, hp * 128:(hp + 1) * 128],
                                    start=(ii == 0), stop=(ii == len(pts) - 1)), reads=[bVa[kv], bpt], writes=[bacc])
                        t0 = s0 + m * 128
                        if "f" in LSK:
                            return
                        finalize_simple(acc, bacc, 512,
                                        lambda h4: oT_h[:, mx["idx"] * 4 + h4, t0:t0 + 128],
                                        add_sink=(name == "D"), split4=True)

                    prev = stage1(0)
                    for m in range(T):
                        nxt = stage1(m + 1) if m + 1 < T else None
                        stage2(m, prev)
                        prev = nxt

                def finalize_c(h, t0, a1, ba1, a2, ba2):
                    rl, brl = rl_r.next()
                    rl2, brl2 = rl_r.next()
                    p.op("dve", lambda e: e.reciprocal(rl[64:65, :], a1[64:65, :]), reads=[ba1], writes=[brl])
                    p.op("dve", lambda e: e.reciprocal(rl2[64:65, :], a2[64:65, :]), reads=[ba2], writes=[brl2])
                    p.op("dve", lambda e: e.tensor_scalar(rl2[64:65, :], rl2[64:65, :], nlam, None, op0=ALU.mult), reads=[brl2, bsm], writes=[brl2])
                    o1, bo1 = f32w_r.next()
                    o2, bo2 = f32w_r.next()
                    for (rr, brr, aa, baa, oo, boo) in ((rl, brl, a1, ba1, o1, bo1), (rl2, brl2, a2, ba2, o2, bo2)):
                        fin, bfin = fin_r.next()
                        p.op("pe", lambda e, fin=fin, rr=rr: e.matmul(fin[0:64, :], lhsT=onesf[64:65, 0:64], rhs=rr[64:65, :], start=True, stop=True),
                             reads=[brr, b_const], writes=[bfin])
                        bcs, bbcs = bcs_r.next()
                        p.op("act", lambda e, bcs=bcs, fin=fin: e.copy(bcs[:], fin[0:64, :]), reads=[bfin], writes=[bbcs])
                        p.op("dve", lambda e, oo=oo, aa=aa, bcs=bcs: e.tensor_tensor(oo[:], aa[0:64, :], bcs[:], ALU.mult),
                             reads=[baa, bbcs], writes=[boo])
                    p.op("pool", lambda e: e.tensor_tensor(o1[:], o1[:], o2[:], ALU.add), reads=[bo2], writes=[bo1])
                    p.op("pool", lambda e: e.tensor_tensor(o2[:], o1[:], o1[:], ALU.mult), reads=[bo1], writes=[bo2])
                    fin2, bfin2 = fin_r.next()
                    p.op("pe", lambda e: e.matmul(fin2[0:64, :], lhsT=onesf[0:64, 0:64], rhs=o2[:], start=True, stop=True),
                         reads=[bo2, b_const], writes=[bfin2])
                    sd, bsd = f32w_r.next()
                    p.op("act", lambda e: e.activation(sd[:], fin2[0:64, :], AF.Sqrt, bias=EPS, scale=1.0 / 64), reads=[bfin2], writes=[bsd])
                    p.op("dve", lambda e: e.reciprocal(sd[:], sd[:]), reads=[bsd], writes=[bsd])
                    ot, bot = ot_r.next()
                    p.op("dve", lambda e: e.scalar_tensor_tensor(out=ot[:], in0=o1[:], scalar=gcol[:, 1:2], in1=sd[:], op0=ALU.mult, op1=ALU.mult),
                         reads=[bo1, bsd, bsm], writes=[bot])
                    p.dma(lambda e: e.dma_start(out=oT_h[:, 8 + h, t0:t0 + 512], in_=ot[:]), reads=[bot])

                def dense_attn(name, si):
                    S = seqs[si]
                    T = S // 128
                    s0 = seq_start[si]
                    mx = MIX[name]
                    isC = name == "C"
                    nq = S // 512
                    scale = (32 ** -0.5) if isC else 0.125

                    def make_stream(qb, h, mp, acc, bacc):
                        if isC:
                            base = 64 * (h % 2) + 32 * mp
                            pr = slice(base, base + 32)
                            Kh, bKh, Qh, bQh = Kt[h // 2], bK[h // 2], Qt[h // 2], bQ[h // 2]
                            kv = h
                            cs = DIFF_SLOPES[h] / scale
                        else:
                            base = 64 * (h % 2)
                            pr = slice(base, base + 64)
                            Kh, bKh, Qh, bQh = Kt[h // 2], bK[h // 2], Qt[h // 2], bQ[h // 2]
                            kv = h // 2
                        kw = dict(tile_position=(base, 0)) if base == 96 else {}

                        def grp(kt):
                            return 0 if kt < 4 * qb else (1 if kt < 4 * qb + 4 else 2)

                        def qk(kt):
                            sbk, bsb = sbank_r.next()
                            p.op("pe", lambda e: e.matmul(sbk[:], lhsT=Kh[pr, kt * 128:(kt + 1) * 128], rhs=Qh[pr, qb * 512:(qb + 1) * 512],
                                                          start=True, stop=True, **kw), reads=[bKh, bQh], writes=[bsb])
                            pt, bpt = pt_r.next()
                            if not isC:
                                p.op("act", lambda e: e.activation(pt[:], sbk[:], AF.Exp, scale=scale), reads=[bsb], writes=[bpt])
                            else:
                                dl = qb * 512 - kt * 128
                                g = grp(kt)
                                if g != 1:
                                    n = abs(dl) // 128
                                    ci = h * 64 + (0 if g == 0 else 32) + n
                                    p.op("act", lambda e: e.activation(pt[:], sbk[:], AF.Exp, bias=cbt[:, ci:ci + 1], scale=scale),
                                         reads=[bsb, bcbt], writes=[bpt])
                                else:
                                    tmp, btmp = tmp_r.next()
                                    di = {0: 1, -128: 2, -256: 3, -384: 4}[dl]
                                    p.op("dve", lambda e: e.scalar_tensor_tensor(out=tmp[:], in0=cd[:, di * 512:(di + 1) * 512], scalar=-cs, in1=sbk[:],
                                                                                 op0=ALU.mult, op1=ALU.add), reads=[bsb, bpar], writes=[btmp])
                                    p.op("act", lambda e: e.activation(pt[:], tmp[:], AF.Exp, scale=scale), reads=[btmp], writes=[bpt])
                            return pt, bpt

                        if not isC:
                            def pv(kt, pt, bpt):
                                p.op("pe", lambda e: e.matmul(acc[0:65, :], lhsT=Va[kv][:, kt, 0:65], rhs=pt[:], start=(kt == 0), stop=(kt == T - 1)),
                                     reads=[bVa[kv], bpt], writes=[bacc])
                            return qk, pv

                        stt = {"acc": None, "first": True}

                        def pv(kt, pt, bpt):
                            g = grp(kt)
                            gfirst = (kt == 0) or grp(kt - 1) != g
                            glast = (kt == T - 1) or grp(kt + 1) != g
                            if gfirst:
                                stt["acc"] = acc_r.next()
                            pa, bpa = stt["acc"]
                            p.op("pe", lambda e: e.matmul(pa[0:65, :], lhsT=Va[kv][:, kt, 0:65], rhs=pt[:], start=gfirst, stop=glast),
                                 reads=[bVa[kv], bpt], writes=[bpa])
                            if not glast:
                                return
                            first = stt["first"]
                            stt["first"] = False
                            if g == 1:
                                if first:
                                    p.op("dve", lambda e: e.tensor_copy(acc[0:65, :], pa[0:65, :]), reads=[bpa], writes=[bacc])
                                else:
                                    p.op("dve", lambda e: e.tensor_tensor(acc[0:65, :], pa[0:65, :], acc[0:65, :], ALU.add), reads=[bpa], writes=[bacc])
                            else:
                                Fh = Ft[0:65, h * 2 + (0 if g == 0 else 1), :]
                                if first:
                                    p.op("dve", lambda e: e.tensor_tensor(acc[0:65, :], pa[0:65, :], Fh, ALU.mult), reads=[bpa, bcbt], writes=[bacc])
                                else:
                                    to, bto = tmpo_r.next()
                                    p.op("dve", lambda e: e.tensor_tensor(to[0:65, :], pa[0:65, :], Fh, ALU.mult), reads=[bpa, bcbt], writes=[bto])
                                    p.op("pool", lambda e: e.tensor_tensor(acc[0:65, :], acc[0:65, :], to[0:65, :], ALU.add), reads=[bto], writes=[bacc])

                        return qk, pv

                    PD = 2

                    pend = [None]

                    def run_streams(sts):
                        q = [[qk(i) for (qk, pv) in sts] for i in range(min(PD, T))]
                        for kt in range(T):
                            if kt + PD < T:
                                q.append([qk(kt + PD) for (qk, pv) in sts])
                            cur = q.pop(0)
                            for (qk, pv), pr in zip(sts, cur):
                                pv(kt, *pr)
                            if kt == 1 and pend[0] is not None:
                                f_ = pend[0]
                                pend[0] = None
                                f_()
                        if pend[0] is not None:
                            f_ = pend[0]
                            pend[0] = None
                            f_()

                    for qb in range(nq):
                        t0 = s0 + qb * 512
                        if not isC:
                            for j in range(2):
                                accs = [acc_r.next(), acc_r.next()]
                                run_streams([make_stream(qb, 2 * j + g, 0, accs[g][0], accs[g][1]) for g in range(2)])

                                def fin_b(accs=accs, j=j, t0=t0):
                                    for g in range(2):
                                        finalize_simple(accs[g][0], accs[g][1], 512, lambda h=2 * j + g, t0=t0: oT_h[:, 4 + h, t0:t0 + 512])
                                pend[0] = fin_b
                        else:
                            for h in range(4):
                                a1, ba1 = osum_r.next()
                                a2, ba2 = osum_r.next()
                                run_streams([make_stream(qb, h, 0, a1, ba1), make_stream(qb, h, 1, a2, ba2)])
                                pend[0] = (lambda h=h, t0=t0, a1=a1, ba1=ba1, a2=a2, ba2=ba2: finalize_c(h, t0, a1, ba1, a2, ba2))
                    if pend[0] is not None:
                        f_ = pend[0]
                        pend[0] = None
                        f_()

                import os
                for si in range(len(seqs)):
                    for name in os.environ.get("MK_MIX", "ABCD"):
                        load_mixer(name, si)
                        if name in ("A", "D"):
                            local_attn(name, si)
                        else:
                            dense_attn(name, si)
            p.barrier()
            import os
            if dbg and l == 0 and "d" not in os.environ.get("MK_SKIP", ""):
                with ExitStack() as es2:
                    sb2, _ = mk(es2)
                    dr = Ring([sb2([128, 8, 128], BF16) for _ in range(2)])
                    dbg_c = dbg_o["oT"].rearrange("(c p) n -> p c n", p=128)
                    for t in range(NTILES):
                        tt, btt = dr.next()
                        p.dma(lambda e, tt=tt, t=t: e.dma_start(out=tt[:], in_=oT_c[:, :, t * 128:(t + 1) * 128]), writes=[btt])
                        p.dma(lambda e, tt=tt, t=t: e.dma_start(out=dbg_c[:, :, t * 128:(t + 1) * 128], in_=tt[:]), reads=[btt])
                p.barrier()

        def phase3a(l):
            src = x_in if l == 0 else y
            with ExitStack() as es:
                sb, ps = mk(es)
                Wb = sb([128, 8, D], BF16)
                Wo = sb([128, 8, D], BF16)
                bW = Buf()
                bWo = Buf()
                p.dma(lambda e: e.dma_start(out=Wb[:], in_=w_br[l]), writes=[bW], queue="pool")
                p.dma(lambda e: e.dma_start(out=Wo[:], in_=w_out[l]), writes=[bWo], queue="pool")
                gT = sb([128, 16], F32)
                gpo = sb([128, D], F32)
                bP = Buf()
                p.dma(lambda e: e.dma_start(out=gT[:], in_=gT_pre[l]), writes=[bP])
                p.dma(lambda e: e.dma_start(out=gpo[:], in_=g_post[l, 0:1, :].to_broadcast([128, D])), writes=[bP])
                xin_r = Ring([sb([128, D], F32) for _ in range(4)])
                oTt_r = Ring([sb([128, 8, 128], BF16) for _ in range(3)])
                gt_r = Ring([sb([128, 4096], BF16) for _ in range(3)])
                mg_r = Ring([sb([128, D], F32) for _ in range(2)])
                tm_r = Ring([sb([128, 512], F32) for _ in range(3)])
                mbf_r = Ring([sb([128, D], BF16) for _ in range(2)])
                mT_r = Ring([sb([128, 8, 128], BF16) for _ in range(2)])
                st_r = Ring([sb([128, 16], F32) for _ in range(2)])
                xn_r = Ring([sb([128, D], F32) for _ in range(2)])
                hbf_r = Ring([sb([128, D], BF16) for _ in range(2)])
                hT_r = Ring([sb([128, 8, 128], BF16) for _ in range(2)])
                junk = sb([128, D], BF16)
                bjunk = Buf()
                br_r = Ring([ps([128, 512], F32) for _ in range(2)])
                pT_r = Ring([ps([128, 8, 128], BF16) for _ in range(1)])
                ob_r = Ring([ps([128, 512], F32) for _ in range(4)])
                pT2_r = Ring([ps([128, 8, 128], BF16) for _ in range(1)])
                def L3(t):
                    r0 = t * 128
                    xin, bxin = xin_r.next()
                    p.dma(lambda e, xin=xin, r0=r0: e.dma_start(out=xin[:], in_=src[r0:r0 + 128, :]), writes=[bxin])
                    oTt, boT = oTt_r.next()
                    p.dma(lambda e, oTt=oTt, r0=r0: e.dma_start(out=oTt[:], in_=oT_c[:, :, r0:r0 + 128]), writes=[boT])
                    gt, bgt = gt_r.next()
                    p.dma(lambda e, gt=gt, r0=r0: e.dma_start(out=gt[:], in_=g_d[r0:r0 + 128, :]), writes=[bgt])
                    return (xin, bxin, oTt, boT, gt, bgt, r0)

                def S1(lctx):
                    xin, bxin, oTt, boT, gt, bgt, r0 = lctx
                    mg, bmg = mg_r.next()
                    for nh in range(2):
                        for i in range(4):
                            br, bbr = br_r.next()
                            for c in range(2):
                                p.op("pe", lambda e, br=br, oTt=oTt, i=i, c=c, nh=nh: e.matmul(
                                    br[:], lhsT=oTt[:, 2 * i + c, :], rhs=Wb[:, 2 * i + c, nh * 512:(nh + 1) * 512], start=(c == 0), stop=(c == 1)),
                                    reads=[boT, bW], writes=[bbr])
                            gsl = gt[:, i * 1024 + nh * 512:i * 1024 + (nh + 1) * 512]
                            if i == 0:
                                p.op("dve", lambda e, mg=mg, br=br, gsl=gsl, nh=nh: e.tensor_tensor(mg[:, nh * 512:(nh + 1) * 512], br[:], gsl, ALU.mult),
                                     reads=[bbr, bgt], writes=[bmg])
                            else:
                                tm, btm = tm_r.next()
                                p.op("dve", lambda e, tm=tm, br=br, gsl=gsl: e.tensor_tensor(tm[:], br[:], gsl, ALU.mult), reads=[bbr, bgt], writes=[btm])
                                p.op("pool", lambda e, mg=mg, tm=tm, nh=nh: e.tensor_tensor(mg[:, nh * 512:(nh + 1) * 512], mg[:, nh * 512:(nh + 1) * 512], tm[:], ALU.add),
                                     reads=[btm], writes=[bmg])
                    mbf, bmbf = mbf_r.next()
                    p.op("act", lambda e, mbf=mbf, mg=mg: e.copy(mbf[:], mg[:]), reads=[bmg], writes=[bmbf])
                    return (mbf, bmbf, xin, bxin, r0)

                def S2(c1):
                    mbf, bmbf, xin, bxin, r0 = c1
                    pT, bpT = pT_r.next()
                    for k in range(8):
                        p.op("pe", lambda e, pT=pT, mbf=mbf, k=k: e.transpose(pT[:, k, :], mbf[:, k * 128:(k + 1) * 128], ident[:]),
                             reads=[bmbf, b_const], writes=[bpT])
                    mT, bmT = mT_r.next()
                    p.op("dve", lambda e, mT=mT, pT=pT: e.tensor_copy(mT[:], pT[:]), reads=[bpT], writes=[bmT])
                    st, bst = st_r.next()
                    p.op("pool", lambda e, st=st: e.memset(st[:], 0.0), writes=[bst])
                    obs = []
                    for nh in range(2):
                        ob, bob = ob_r.next()
                        obs.append((ob, bob))
                        for k in range(8):
                            p.op("pe", lambda e, ob=ob, mT=mT, k=k, nh=nh: e.matmul(ob[:], lhsT=mT[:, k, :], rhs=Wo[:, k, nh * 512:(nh + 1) * 512],
                                                                               start=(k == 0), stop=(k == 7)), reads=[bmT, bWo], writes=[bob])
                        p.op("act", lambda e, ob=ob, st=st, nh=nh: e.activation(junk[:, 0:512], ob[:], AF.Square, accum_out=st[:, nh:nh + 1]),
                             reads=[bob], writes=[bst, bjunk])
                    p.op("dve", lambda e, st=st: e.tensor_tensor(st[:, 2:3], st[:, 0:1], st[:, 1:2], ALU.add), reads=[bst], writes=[bst])
                    p.op("act", lambda e, st=st: e.activation(st[:, 3:4], st[:, 2:3], AF.Sqrt, bias=EPS, scale=1.0 / D), reads=[bst], writes=[bst])
                    p.op("dve", lambda e, st=st: e.reciprocal(st[:, 4:5], st[:, 3:4]), reads=[bst], writes=[bst])
                    xn, bxn = xn_r.next()
                    for nh in range(2):
                        ob, bob = obs[nh]
                        tm, btm = tm_r.next()
                        p.op("dve", lambda e, tm=tm, ob=ob, st=st, nh=nh: e.scalar_tensor_tensor(
                            out=tm[:], in0=ob[:], scalar=st[:, 4:5], in1=gpo[:, nh * 512:(nh + 1) * 512], op0=ALU.mult, op1=ALU.mult),
                            reads=[bob, bst, bP], writes=[btm])
                        p.op("pool", lambda e, xn=xn, tm=tm, xin=xin, nh=nh: e.tensor_tensor(xn[:, nh * 512:(nh + 1) * 512], tm[:], xin[:, nh * 512:(nh + 1) * 512], ALU.add),
                             reads=[btm, bxin], writes=[bxn])
                    p.dma(lambda e, xn=xn, r0=r0: e.dma_start(out=y[r0:r0 + 128, :], in_=xn[:]), reads=[bxn])
                    if dbg and l == 0:
                        p.dma(lambda e, xn=xn, r0=r0: e.dma_start(out=dbg_o["x1"][r0:r0 + 128, :], in_=xn[:]), reads=[bxn])
                    p.op("act", lambda e, xn=xn, st=st: e.activation(junk[:], xn[:], AF.Square, accum_out=st[:, 5:6]), reads=[bxn], writes=[bst, bjunk])
                    p.op("act", lambda e, st=st: e.activation(st[:, 6:7], st[:, 5:6], AF.Sqrt, bias=EPS, scale=1.0 / D), reads=[bst], writes=[bst])
                    p.op("dve", lambda e, st=st: e.reciprocal(st[:, 7:8], st[:, 6:7]), reads=[bst], writes=[bst])
                    hbf, bhbf = hbf_r.next()
                    p.op("dve", lambda e, hbf=hbf, xn=xn, st=st: e.tensor_scalar(hbf[:], xn[:], st[:, 7:8], None, op0=ALU.mult), reads=[bxn, bst], writes=[bhbf])
                    return (hbf, bhbf, r0)

                def S3(c2):
                    hbf, bhbf, r0 = c2
                    pT2, bpT2 = pT2_r.next()
                    for k in range(8):
                        p.op("pe", lambda e, pT2=pT2, hbf=hbf, k=k: e.transpose(pT2[:, k, :], hbf[:, k * 128:(k + 1) * 128], ident[:]),
                             reads=[bhbf, b_const], writes=[bpT2])
                    hTt, bhT = hT_r.next()
                    p.op("dve", lambda e, hTt=hTt, pT2=pT2: e.tensor_tensor(hTt[:], pT2[:], gT[:, 8:16].unsqueeze(2).to_broadcast([128, 8, 128]), ALU.mult),
                         reads=[bpT2, bP], writes=[bhT])
                    p.dma(lambda e, hTt=hTt, r0=r0: e.dma_start(out=hT_c[:, :, r0:r0 + 128], in_=hTt[:]), reads=[bhT])

                lq = [L3(i) for i in range(min(2, NTILES))]
                q1, q2 = [], []
                for t in range(NTILES + 2):
                    if t < NTILES:
                        if t + 2 < NTILES:
                            lq.append(L3(t + 2))
                        q1.append(S1(lq.pop(0)))
                    if 0 <= t - 1 < NTILES:
                        q2.append(S2(q1.pop(0)))
                    if t - 2 >= 0:
                        S3(q2.pop(0))
            p.barrier()

        def phase3b(l):
            with ExitStack() as es:
                sb, ps = mk(es)
                Wu = sb([128, 8, 5632], BF16)
                Wd = sb([128, 22, D], BF16)
                bWu = [Buf() for _ in range(8)]
                bWd = Buf()
                for k in range(8):
                    p.dma(lambda e, k=k: e.dma_start(out=Wu[:, k, :], in_=w_up[l, :, k, :]), writes=[bWu[k]], queue="pool")
                p.dma(lambda e: e.dma_start(out=Wd[:], in_=w_dn[l]), writes=[bWd], queue="pool")
                cw = sb([128, 44, 4], F32)
                gpo = sb([128, D], F32)
                bP = Buf()
                p.dma(lambda e: e.dma_start(out=cw[:], in_=convp[l]), writes=[bP])
                p.dma(lambda e: e.dma_start(out=gpo[:], in_=g_post[l, 1:2, :].to_broadcast([128, D])), writes=[bP])
                hs_r = Ring([sb([128, 8, 258], BF16) for _ in range(2)])
                xin_r = Ring([sb([128, D], F32) for _ in range(4)])
                cg_r = Ring([sb([128, 256], F32) for _ in range(2)])
                cv_r = Ring([sb([128, 256], F32) for _ in range(2)])
                gg_r = Ring([sb([128, 256], F32) for _ in range(2)])
                aT_r = Ring([sb([128, 22, 256], BF16) for _ in range(2)])
                tm_r = Ring([sb([128, 512], F32) for _ in range(2)])
                st_r = Ring([sb([128, 8], F32) for _ in range(2)])
                junk = sb([128, 512], BF16)
                bjunk = Buf()
                u_r = Ring([ps([128, 512], F32) for _ in range(4)])
                d_r = Ring([ps([128, 512], F32) for _ in range(4)])
                blocks = []
                for si, S in enumerate(seqs):
                    for b in range(S // 256):
                        blocks.append((seq_start[si] + b * 256, b == 0, b == S // 256 - 1))

                def LB(bi):
                    t0, first, last = blocks[bi]
                    hs, bhs = hs_r.next()
                    lo = 1 if first else 0
                    hi = 257 if last else 258
                    if first:
                        p.op("pool", lambda e: e.memset(hs[:, :, 0:1], 0.0), writes=[bhs])
                    if last:
                        p.op("pool", lambda e: e.memset(hs[:, :, 257:258], 0.0), writes=[bhs])
                    p.dma(lambda e: e.dma_start(out=hs[:, :, lo:hi], in_=hT_c[:, :, t0 - 1 + lo:t0 - 1 + hi]), writes=[bhs])
                    return (hs, bhs, t0)

                def UP(lctx):
                    hs, bhs, t0 = lctx
                    xs = []
                    for a in range(2):
                        r0 = t0 + a * 128
                        xin, bxin = xin_r.next()
                        p.dma(lambda e, xin=xin, r0=r0: e.dma_start(out=xin[:], in_=y[r0:r0 + 128, :]), writes=[bxin])
                        xs.append((xin, bxin, r0))
                    aT, baT = aT_r.next()
                    for fp in range(22):
                        us = []
                        for ch in (fp, fp + 22):
                            u, bu = u_r.next()
                            us.append((u, bu, ch))
                            for k in range(8):
                                p.op("pe", lambda e, u=u, k=k, ch=ch: e.matmul(u[:, 0:258], lhsT=Wu[:, k, ch * 128:(ch + 1) * 128], rhs=hs[:, k, :],
                                                                          start=(k == 0), stop=(k == 7)), reads=[bhs, bWu[k]], writes=[bu])
                        cg, bcg = cg_r.next()
                        cv, bcv = cv_r.next()
                        for (u, bu, ch), (c_, bc_) in zip(us, ((cg, bcg), (cv, bcv))):
                            p.op("act", lambda e, c_=c_, u=u, ch=ch: e.activation(c_[:], u[:, 0:256], AF.Identity, bias=cw[:, ch, 3:4], scale=cw[:, ch, 0:1]),
                                 reads=[bu, bP], writes=[bc_])
                            p.op("dve", lambda e, c_=c_, u=u, ch=ch: e.scalar_tensor_tensor(out=c_[:], in0=u[:, 1:257], scalar=cw[:, ch, 1:2], in1=c_[:], op0=ALU.mult, op1=ALU.add),
                                 reads=[bu, bP], writes=[bc_])
                            p.op("dve", lambda e, c_=c_, u=u, ch=ch: e.scalar_tensor_tensor(out=c_[:], in0=u[:, 2:258], scalar=cw[:, ch, 2:3], in1=c_[:], op0=ALU.mult, op1=ALU.add),
                                 reads=[bu, bP], writes=[bc_])
                        gg, bgg = gg_r.next()
                        p.op("act", lambda e, gg=gg, cg=cg: e.activation(gg[:], cg[:], AF.Gelu_apprx_tanh), reads=[bcg], writes=[bgg])
                        p.op("pool", lambda e, gg=gg, cv=cv, fp=fp: e.tensor_tensor(aT[:, fp, :], gg[:], cv[:], ALU.mult), reads=[bgg, bcv], writes=[baT])
                    return (aT, baT, xs)

                def DOWN(uctx):
                    aT, baT, xs = uctx
                    for a in range(2):
                        xin, bxin, r0 = xs[a]
                        st, bst = st_r.next()
                        p.op("pool", lambda e, st=st: e.memset(st[:], 0.0), writes=[bst])
                        dbs = []
                        for nh in range(2):
                            dbk, bdb = d_r.next()
                            dbs.append((dbk, bdb))
                            for f in range(22):
                                p.op("pe", lambda e, dbk=dbk, f=f, a=a, nh=nh: e.matmul(dbk[:], lhsT=aT[:, f, a * 128:(a + 1) * 128], rhs=Wd[:, f, nh * 512:(nh + 1) * 512],
                                                                                   start=(f == 0), stop=(f == 21)), reads=[baT, bWd], writes=[bdb])
                            p.op("act", lambda e, dbk=dbk, st=st, nh=nh: e.activation(junk[:], dbk[:], AF.Square, accum_out=st[:, nh:nh + 1]), reads=[bdb], writes=[bst, bjunk])
                        p.op("dve", lambda e, st=st: e.tensor_tensor(st[:, 2:3], st[:, 0:1], st[:, 1:2], ALU.add), reads=[bst], writes=[bst])
                        p.op("act", lambda e, st=st: e.activation(st[:, 3:4], st[:, 2:3], AF.Sqrt, bias=EPS, scale=1.0 / D), reads=[bst], writes=[bst])
                        p.op("dve", lambda e, st=st: e.reciprocal(st[:, 4:5], st[:, 3:4]), reads=[bst], writes=[bst])
                        for nh in range(2):
                            dbk, bdb = dbs[nh]
                            tm, btm = tm_r.next()
                            p.op("dve", lambda e, tm=tm, dbk=dbk, st=st, nh=nh: e.scalar_tensor_tensor(
                                out=tm[:], in0=dbk[:], scalar=st[:, 4:5], in1=gpo[:, nh * 512:(nh + 1) * 512], op0=ALU.mult, op1=ALU.mult),
                                reads=[bdb, bst, bP], writes=[btm])
                            p.op("pool", lambda e, tm=tm, xin=xin, nh=nh: e.tensor_tensor(xin[:, nh * 512:(nh + 1) * 512], tm[:], xin[:, nh * 512:(nh + 1) * 512], ALU.add),
                                 reads=[btm], writes=[bxin])
                        p.dma(lambda e, xin=xin, r0=r0: e.dma_start(out=y[r0:r0 + 128, :], in_=xin[:]), reads=[bxin])

                NB = len(blocks)
                lq = [LB(0)]
                prevu = None
                for bi in range(NB):
                    if bi + 1 < NB:
                        lq.append(LB(bi + 1))
                    uctx = UP(lq.pop(0))
                    if prevu is not None:
                        DOWN(prevu)
                    prevu = uctx
                DOWN(prevu)
            p.barrier()

        import os
        stop = int(os.environ.get("MK_STOP", "99"))
        for l in range(depth):
            phase1(l)
            if stop >= 2:
                phase2(l)
            if stop >= 3:
                phase3a(l)
            if stop >= 4:
                phase3b(l)
        p.emit()
    return nc, p


def prep_shared(inp, depth, smax):
    f = lambda a: np.ascontiguousarray(np.asarray(a, dtype=np.float32))
    w_in = f(inp["w_in"]).reshape(depth, 8, 128, 6656).transpose(0, 2, 1, 3)
    w_br = f(inp["w_branch"]).reshape(depth, 8, 128, D).transpose(0, 2, 1, 3)
    w_out = f(inp["w_out"]).reshape(depth, 8, 128, D).transpose(0, 2, 1, 3)
    w_up = f(inp["ffn_w_up"]).reshape(depth, 8, 128, 5632).transpose(0, 2, 1, 3)
    w_dn = f(inp["ffn_w_down"]).reshape(depth, 22, 128, D).transpose(0, 2, 1, 3)
    gT = np.concatenate([f(inp["norm_mix_pre"]).reshape(depth, 8, 128).transpose(0, 2, 1),
                         f(inp["norm_ffn_pre"]).reshape(depth, 8, 128).transpose(0, 2, 1)], axis=2)
    g_post = np.stack([f(inp["norm_mix_post"]), f(inp["norm_ffn_post"])], axis=1)
    cw = f(inp["ffn_conv_w"]).reshape(depth, 3, 44, 128)
    cb = f(inp["ffn_conv_b"]).reshape(depth, 1, 44, 128)
    convp = np.concatenate([cw, cb], axis=1).transpose(0, 3, 2, 1)
    cosf, sinf = rope_tables(smax)
    ropec = np.concatenate([cosf, sinf], axis=1)
    gqk = np.concatenate([np.tile(f(inp["gqa_q_norm"]), (1, 4)), np.tile(f(inp["gqa_k_norm"]), (1, 2))], axis=1)
    rpb = f(inp["na_rpb"])
    abias = np.stack([build_abias(rpb[l]) for l in range(depth)]).reshape(depth, 128, NA_SLABS * 512)
    dbias = build_dbias().reshape(128, 3 * 512)
    ii = np.arange(128, dtype=np.float32)[:, None]
    jj = np.arange(512, dtype=np.float32)[None, :]
    base = jj - ii
    cdist = np.concatenate([base] + [np.abs(base + d) for d in (0.0, -128.0, -256.0, -384.0)], axis=1).astype(np.float32)
    cbias = np.zeros((128, 512), np.float32)
    cF = np.zeros((8, 512), np.float32)
    i_ = np.arange(128, dtype=np.float64)
    j_ = np.arange(512, dtype=np.float64)
    for h in range(4):
        sl = DIFF_SLOPES[h]
        for n in range(32):
            cbias[:, h * 64 + n] = -sl * (128.0 * n - i_)
            cbias[:, h * 64 + 32 + n] = -sl * (i_ + 128.0 * n - 511.0)
        cF[h * 2 + 0] = np.exp(-sl * j_)
        cF[h * 2 + 1] = np.exp(-sl * (511.0 - j_))
    lamv = np.concatenate([f(inp["diff_lambda_q1"]), f(inp["diff_lambda_k1"]), f(inp["diff_lambda_q2"]), f(inp["diff_lambda_k2"])], axis=1)
    subln = f(inp["diff_subln"]).reshape(depth, 64, 1)
    sink = f(inp["swa_sink"])
    c = np.ascontiguousarray
    return dict(w_in=c(w_in), w_br=c(w_br), w_out=c(w_out), w_up=c(w_up), w_dn=c(w_dn), gT_pre=c(gT), g_post=c(g_post),
                convp=c(convp), ropec=c(ropec), gqk=c(gqk), abias=c(abias), dbias=c(dbias), cdist=c(cdist), cbias=c(cbias), cF=c(cF), lamv=c(lamv),
                subln=c(subln), sink=c(sink))


_CACHE = {}


def kernel(**inp):
    xp = np.asarray(inp["x_prompt"], dtype=np.float32)
    xs = np.asarray(inp["x_sample"], dtype=np.float32)
    depth = int(np.asarray(inp["w_in"]).shape[0])
    ncores = 8
    bp, sp_ = xp.shape[0], xp.shape[1]
    bs, ss_ = xs.shape[0], xs.shape[1]
    npc, nsc = bp // ncores, bs // ncores
    seqs = [sp_] * npc + [ss_] * nsc
    key = (tuple(seqs), depth)
    if key not in _CACHE:
        _CACHE[key] = build(seqs, depth)[0]
    nc = _CACHE[key]
    shared = prep_shared(inp, depth, max(seqs))
    in_maps = []
    for c in range(ncores):
        xc = np.concatenate([xp[c * npc:(c + 1) * npc].reshape(-1, D), xs[c * nsc:(c + 1) * nsc].reshape(-1, D)], axis=0)
        m = dict(shared)
        m["x"] = np.ascontiguousarray(xc)
        in_maps.append(m)
    res = run_bass_kernel_spmd(nc, in_maps, core_ids=list(range(ncores)))
    yp = np.empty_like(xp)
    ys = np.empty_like(xs)
    for c in range(ncores):
        yc = res.results[c]["y"]
        yp[c * npc:(c + 1) * npc] = yc[:npc * sp_].reshape(npc, sp_, D)
        ys[c * nsc:(c + 1) * nsc] = yc[npc * sp_:].reshape(nsc, ss_, D)
    return (yp, ys)
```

```python
import math
from contextlib import ExitStack
import numpy as np
import concourse.bass as bass
import concourse.mybir as mybir
from concourse.bass_utils import run_bass_kernel_spmd

F32 = mybir.dt.float32
BF16 = mybir.dt.bfloat16
AF = mybir.ActivationFunctionType
ALU = mybir.AluOpType
AX = mybir.AxisListType

COMPUTE = ("pe", "act", "dve", "pool")
ALLENG = COMPUTE + ("sp",)
NCHAINS = 48
EPS = 1e-6
NEG = -1e30


class Buf:
    __slots__ = ("w", "r")

    def __init__(self):
        self.w = None
        self.r = []


class Op:
    __slots__ = ("eng", "fn", "waits", "signal", "pos", "is_dma", "chain", "dma_val", "cnt")

    def __init__(self, eng, fn):
        self.eng = eng
        self.fn = fn
        self.waits = []
        self.signal = False
        self.pos = -1
        self.is_dma = False
        self.chain = None
        self.dma_val = 0
        self.cnt = 0


class Chain:
    __slots__ = ("sem", "count", "last")

    def __init__(self, sem):
        self.sem = sem
        self.count = 0
        self.last = None


class Prog:
    def __init__(self, nc):
        self.nc = nc
        self.ops = {e: [] for e in ALLENG}
        self.sem = {e: nc.alloc_semaphore(name="s_" + e) for e in ALLENG}
        self.known = {e: {p: -1 for p in ALLENG} for e in ALLENG}
        self.known_chain = {e: {} for e in ALLENG}
        self.chains = [Chain(nc.alloc_semaphore(name=f"c{i}")) for i in range(NCHAINS)]
        self.ci = 0
        self.nops = 0

    def _add_wait(self, op, d):
        E = op.eng
        if d.is_dma:
            kc = self.known_chain[E]
            if kc.get(id(d.chain), 0) >= d.dma_val:
                return
            kc[id(d.chain)] = d.dma_val
            op.waits.append(d)
        else:
            if d.eng == E and E == "pe":
                return
            kn = self.known[E]
            if kn[d.eng] >= d.pos:
                return
            kn[d.eng] = d.pos
            d.signal = True
            op.waits.append(d)

    def _deps(self, op, reads, writes):
        for b in reads:
            if b.w is not None:
                self._add_wait(op, b.w)
        for b in writes:
            if b.w is not None:
                self._add_wait(op, b.w)
            for r in b.r:
                self._add_wait(op, r)
        for b in reads:
            b.r.append(op)
        for b in writes:
            b.w = op
            b.r = []

    def op(self, eng, fn, reads=(), writes=()):
        o = Op(eng, fn)
        o.pos = len(self.ops[eng])
        self._deps(o, reads, writes)
        self.ops[eng].append(o)
        self.nops += 1
        return o

    def dma(self, fn, reads=(), writes=(), queue="sp"):
        ch = self.chains[self.ci]
        self.ci = (self.ci + 1) % NCHAINS
        o = Op(queue, fn)
        o.is_dma = True
        o.chain = ch
        o.pos = len(self.ops[queue])
        if ch.last is not None:
            self._add_wait(o, ch.last)
        self._deps(o, reads, writes)
        ch.count += 16
        o.dma_val = ch.count
        ch.last = o
        self.ops[queue].append(o)
        self.nops += 1
        return o

    def barrier(self):
        sp_wait = []
        for e in COMPUTE:
            if self.ops[e]:
                last = self.ops[e][-1]
                if last.is_dma or last.fn is None:
                    last = self.op(e, lambda en: en.nop())
                last.signal = True
                sp_wait.append(last)
        o = Op("sp", lambda en: en.nop())
        o.pos = len(self.ops["sp"])
        o.waits = sp_wait + [c.last for c in self.chains if c.last is not None]
        o.signal = True
        self.ops["sp"].append(o)
        for e in COMPUTE:
            w = Op(e, None)
            w.pos = len(self.ops[e])
            w.waits = [o]
            self.ops[e].append(w)
        for e in ALLENG:
            for q in ALLENG:
                self.known[e][q] = len(self.ops[q]) - 1
            for c in self.chains:
                self.known_chain[e][id(c)] = c.count

    def emit(self):
        nc = self.nc
        for e in ALLENG:
            c = 0
            for o in self.ops[e]:
                if o.signal and not o.is_dma:
                    c += 1
                o.cnt = c
        sem = self.sem

        def run(e, eng):
            for o in self.ops[e]:
                for d in o.waits:
                    if d.is_dma:
                        eng.wait_ge(d.chain.sem, d.dma_val)
                    else:
                        eng.wait_ge(sem[d.eng], d.cnt)
                if o.fn is None:
                    continue
                ins = o.fn(eng)
                if o.is_dma:
                    ins.then_inc(o.chain.sem, 16)
                elif o.signal:
                    ins.then_inc(sem[e], 1)

        with nc.Block() as block:
            @block.tensor
            def _(eng):
                run("pe", eng)

            @block.scalar
            def _(eng):
                run("act", eng)

            @block.vector
            def _(eng):
                run("dve", eng)

            @block.gpsimd
            def _(eng):
                run("pool", eng)

            @block.sync
            def _(eng):
                run("sp", eng)


class Ring:
    def __init__(self, tiles):
        self.t = tiles
        self.b = [Buf() for _ in tiles]
        self.i = -1

    def next(self):
        self.i = (self.i + 1) % len(self.t)
        return self.t[self.i], self.b[self.i]


D = 1024
NPJ = 2560
QK_SRC = [0, 128, 256, 384, 768, 896, 1024, 1280, 1408, 1536, 1664, 2048, 2176, 2304]
MIX = {
    "A": dict(qb=(0, 1), kb=(2, 3), vc=512, vw=256, nkv=4, idx=0),
    "B": dict(qb=(4, 5), kb=(6,), vc=1152, vw=128, nkv=2, idx=1),
    "C": dict(qb=(7, 8), kb=(9, 10), vc=1792, vw=256, nkv=4, idx=2),
    "D": dict(qb=(11, 12), kb=(13,), vc=2432, vw=128, nkv=2, idx=3),
}
_SL = 2.0 ** (-8.0 * np.arange(1, 9) / 8)
DIFF_SLOPES = [float(v) for v in _SL[0::2]]
SWA_SLOPES = [float(v) for v in _SL[1::2]]
NA_CLASSES = 5
NA_SLABS = 21


def na_tiles(m, T):
    if T <= 4:
        raise NotImplementedError
    if m == 0:
        return [(j, 5 + j) for j in range(4)]
    if m == 1:
        return [(j, 9 + j) for j in range(4)]
    if m == T - 2:
        return [(T - 4 + i, 13 + i) for i in range(4)]
    if m == T - 1:
        return [(T - 4 + i, 17 + i) for i in range(4)]
    return [(m - 2 + i, i) for i in range(5)]


def _na_slab_defs():
    T = 16
    reps = {0: 8, 1: 0, 2: 1, 3: T - 2, 4: T - 1}
    out = [None] * NA_SLABS
    for cls, m in reps.items():
        for (j, slab) in na_tiles(m, T):
            out[slab] = (m, j, T)
    return out


def build_abias(rpb):
    defs = _na_slab_defs()
    res = np.full((128, NA_SLABS, 4, 128), NEG, np.float32)
    kk = np.arange(128)
    qq = np.arange(128)
    for slab, (m, j, T) in enumerate(defs):
        rows = 2 * T
        qr = 2 * m + qq // 64
        qc = qq % 64
        kr = 2 * j + kk // 64
        kc = kk % 64
        r0 = np.clip(qr - 4, 0, rows - 8)
        c0 = np.clip(qc - 8, 0, 64 - 16)
        vr = (kr[:, None] >= r0[None, :]) & (kr[:, None] < r0[None, :] + 8)
        vc = (kc[:, None] >= c0[None, :]) & (kc[:, None] < c0[None, :] + 16)
        valid = vr & vc
        dr = np.clip(kr[:, None] - qr[None, :] + 7, 0, 14)
        dc = np.clip(kc[:, None] - qc[None, :] + 15, 0, 30)
        for hp, h in enumerate((0, 2, 1, 3)):
            g = rpb[h][dr, dc]
            res[:, slab, hp, :] = np.where(valid, g, np.float32(NEG))
    return res


def build_dbias():
    res = np.full((128, 3, 4, 128), NEG, np.float32)
    kk = np.arange(128)[:, None]
    qq = np.arange(128)[None, :]
    for di, dl in enumerate((-1, 0, 1)):
        rel = np.abs(dl * 128 + kk - qq)
        for hp, h in enumerate((0, 2, 1, 3)):
            res[:, di, hp, :] = np.where(rel <= 128, -np.float32(SWA_SLOPES[h]) * rel.astype(np.float32), np.float32(NEG))
    return res


def rope_tables(S):
    t = np.arange(S)
    row = (t // 64).astype(np.float32)
    col = (t % 64).astype(np.float32)
    inv = (10000.0 ** (-np.arange(0, 32, 2, dtype=np.float32) / 32)).astype(np.float32)
    ar = row[:, None] * inv
    ac = col[:, None] * inv
    cr, sr, cc, sc = np.cos(ar), np.sin(ar), np.cos(ac), np.sin(ac)
    cosf = np.concatenate([cr, cr, cc, cc], 1).astype(np.float32)
    sinf = np.concatenate([-sr, sr, -sc, sc], 1).astype(np.float32)
    return cosf, sinf


def build(seqs, depth, dbg=False):
    NT = sum(seqs)
    NTILES = NT // 128
    SMAX = max(seqs)
    TMAX = SMAX // 128
    seq_start = [sum(seqs[:i]) for i in range(len(seqs))]
    nc = bass.Bass("TRN2", target_bir_lowering=False)

    def din(name, shape, dt=F32):
        return nc.dram_tensor(name, list(shape), dt, kind="ExternalInput").ap()

    x_in = din("x", [NT, D])
    w_in = din("w_in", [depth, 128, 8, 6656])
    w_br = din("w_br", [depth, 128, 8, D])
    w_out = din("w_out", [depth, 128, 8, D])
    w_up = din("w_up", [depth, 128, 8, 5632])
    w_dn = din("w_dn", [depth, 128, 22, D])
    gT_pre = din("gT_pre", [depth, 128, 16])
    g_post = din("g_post", [depth, 2, D])
    convp = din("convp", [depth, 128, 44, 4])
    ropec = din("ropec", [SMAX, 128])
    gqk = din("gqk", [depth, 384])
    abias = din("abias", [depth, 128, NA_SLABS * 512])
    dbias = din("dbias", [128, 3 * 512])
    cdist = din("cdist", [128, 5 * 512])
    cbias_d = din("cbias", [128, 512])
    cF_d = din("cF", [8, 512])
    lamv = din("lamv", [depth, 128])
    subln = din("subln", [depth, 64, 1])
    sink = din("sink", [depth, 4])
    y = nc.dram_tensor("y", [NT, D], F32, kind="ExternalOutput").ap()
    pj_d = nc.dram_tensor("pj_s", [NT, NPJ], BF16).ap()
    qkT_d = nc.dram_tensor("qkT_s", [14 * 128, NT], BF16).ap()
    g_d = nc.dram_tensor("g_s", [NT, 4096], BF16).ap()
    oT_d = nc.dram_tensor("oT_s", [D, NT], BF16).ap()
    hT_d = nc.dram_tensor("hT_s", [D, NT], BF16).ap()
    dbg_o = {}
    if dbg:
        dbg_o["pj"] = nc.dram_tensor("dbg_pj", [NT, NPJ], BF16, kind="ExternalOutput").ap()
        dbg_o["oT"] = nc.dram_tensor("dbg_oT", [D, NT], BF16, kind="ExternalOutput").ap()
        dbg_o["x1"] = nc.dram_tensor("dbg_x1", [NT, D], F32, kind="ExternalOutput").ap()

    p = Prog(nc)
    qkT_r = qkT_d.rearrange("(b p) n -> p b n", p=128)
    oT_c = oT_d.rearrange("(c p) n -> p c n", p=128)
    oT_h = oT_d.rearrange("(h d) n -> d h n", d=64)
    hT_c = hT_d.rearrange("(c p) n -> p c n", p=128)

    with ExitStack() as top:
        uid = [0]

        def mk(es):
            def sb(shape, dt):
                uid[0] += 1
                return es.enter_context(nc.sbuf_tensor(f"sb{uid[0]}", list(shape), dt))

            def ps(shape, dt):
                uid[0] += 1
                return es.enter_context(nc.psum_tensor(f"ps{uid[0]}", list(shape), dt))
            return sb, ps

        sbT, _ = mk(top)
        ident = sbT([128, 128], BF16)
        identf = sbT([128, 128], F32)
        onesf = sbT([128, 64], F32)
        b_const = Buf()
        p.op("pool", lambda e: e.memset(identf[:], 0.0), writes=[b_const])
        p.op("pool", lambda e: e.affine_select(out=identf[:], in_=identf[:], compare_op=ALU.not_equal,
                                               fill=1.0, base=0, pattern=[[-1, 128]], channel_multiplier=1),
             writes=[b_const])
        p.op("pool", lambda e: e.tensor_copy(ident[:], identf[:]), writes=[b_const])
        p.op("pool", lambda e: e.memset(onesf[:], 1.0), writes=[b_const])

        def phase1(l):
            src = x_in if l == 0 else y
            with ExitStack() as es:
                sb, ps = mk(es)
                W = sb([128, 8, 6656], BF16)
                bW = [Buf() for _ in range(8)]
                for k in range(8):
                    p.dma(lambda e, k=k: e.dma_start(out=W[:, k, :], in_=w_in[l, :, k, :]), writes=[bW[k]], queue="pool")
                gT = sb([128, 16], F32)
                gqkt = sb([128, 384], F32)
                bP = Buf()
                p.dma(lambda e: e.dma_start(out=gT[:], in_=gT_pre[l]), writes=[bP])
                p.dma(lambda e: e.dma_start(out=gqkt[:], in_=gqk[l:l + 1, :].to_broadcast([128, 384])), writes=[bP])
                xin_r = Ring([sb([128, D], F32) for _ in range(3)])
                xbf_r = Ring([sb([128, D], BF16) for _ in range(2)])
                junk = sb([128, D], BF16)
                bjunk = Buf()
                st_r = Ring([sb([128, 8], F32) for _ in range(2)])
                xT_r = Ring([sb([128, 8, 128], BF16) for _ in range(2)])
                pj_r = Ring([sb([128, NPJ], BF16) for _ in range(3)])
                gt_r = Ring([sb([128, 4096], BF16) for _ in range(3)])
                bq_r = Ring([sb([128, 384], F32) for _ in range(2)])
                wk_r = Ring([sb([128, 5, 384], F32) for _ in range(2)])
                sb6_r = Ring([sb([128, 16], F32) for _ in range(2)])
                rp_r = Ring([sb([128, 128], F32) for _ in range(3)])
                qk_r = Ring([sb([128, 14, 128], BF16) for _ in range(2)])
                pT_r = Ring([ps([128, 8, 128], BF16) for _ in range(1)])
                po_r = Ring([ps([128, 512], F32) for _ in range(5)])
                pq_r = Ring([ps([128, 16, 128], BF16) for _ in range(1)])
                def loadA(t):
                    si = max(i for i in range(len(seqs)) if seq_start[i] <= t * 128)
                    pos = t * 128 - seq_start[si]
                    r0 = t * 128
                    xin, bxin = xin_r.next()
                    p.dma(lambda e, xin=xin, r0=r0: e.dma_start(out=xin[:], in_=src[r0:r0 + 128, :]), writes=[bxin])
                    rp, brp = rp_r.next()
                    p.dma(lambda e, rp=rp, pos=pos: e.dma_start(out=rp[:], in_=ropec[pos:pos + 128, :]), writes=[brp])
                    return (xin, bxin, rp, brp, r0)

                def prepA(lctx):
                    xin, bxin, rp, brp, r0 = lctx
                    st, bst = st_r.next()
                    p.op("pool", lambda e, st=st: e.memset(st[:], 0.0), writes=[bst])
                    p.op("act", lambda e, xin=xin, st=st: e.activation(junk[:], xin[:], AF.Square, accum_out=st[:, 0:1]),
                         reads=[bxin], writes=[bst, bjunk])
                    p.op("act", lambda e, st=st: e.activation(st[:, 1:2], st[:, 0:1], AF.Sqrt, bias=EPS, scale=1.0 / D),
                         reads=[bst], writes=[bst])
                    p.op("dve", lambda e, st=st: e.reciprocal(st[:, 2:3], st[:, 1:2]), reads=[bst], writes=[bst])
                    xbf, bxbf = xbf_r.next()
                    p.op("dve", lambda e, xbf=xbf, xin=xin: e.tensor_copy(xbf[:], xin[:]), reads=[bxin], writes=[bxbf])
                    pT, bpT = pT_r.next()
                    for k in range(8):
                        p.op("pe", lambda e, pT=pT, xbf=xbf, k=k: e.transpose(pT[:, k, :], xbf[:, k * 128:(k + 1) * 128], ident[:]),
                             reads=[bxbf, b_const], writes=[bpT])
                    xT, bxT = xT_r.next()
                    p.op("dve", lambda e, xT=xT, pT=pT: e.tensor_tensor(xT[:], pT[:], gT[:, 0:8].unsqueeze(2).to_broadcast([128, 8, 128]), ALU.mult),
                         reads=[bpT, bP], writes=[bxT])
                    return (rp, brp, r0, st, bst, xT, bxT)

                def mmA(pctx, hook):
                    rp, brp, r0, st, bst, xT, bxT = pctx
                    pj, bpj = pj_r.next()
                    gt, bgt = gt_r.next()
                    bq, bbq = bq_r.next()
                    rs = st[:, 2:3]
                    for n in range(13):
                        if n == 7:
                            hook()
                        po, bpo = po_r.next()
                        for k in range(8):
                            p.op("pe", lambda e, po=po, xT=xT, k=k, n=n: e.matmul(po[:], lhsT=xT[:, k, :], rhs=W[:, k, n * 512:(n + 1) * 512],
                                                                             start=(k == 0), stop=(k == 7)),
                                 reads=[bxT, bW[k]], writes=[bpo])
                        if n >= 5:
                            p.op("act", lambda e, po=po, gt=gt, n=n, rs=rs: e.activation(gt[:, (n - 5) * 512:(n - 4) * 512], po[:], AF.Sigmoid, scale=rs),
                                 reads=[bpo, bst], writes=[bgt])
                        elif n == 1:
                            p.op("dve", lambda e, po=po, pj=pj, rs=rs: e.tensor_scalar(pj[:, 512:768], po[:, 0:256], rs, None, op0=ALU.mult),
                                 reads=[bpo, bst], writes=[bpj])
                            p.op("dve", lambda e, po=po, bq=bq, rs=rs: e.tensor_scalar(bq[:, 0:256], po[:, 256:512], rs, None, op0=ALU.mult),
                                 reads=[bpo, bst], writes=[bbq])
                        elif n == 2:
                            p.op("dve", lambda e, po=po, bq=bq, rs=rs: e.tensor_scalar(bq[:, 256:384], po[:, 0:128], rs, None, op0=ALU.mult),
                                 reads=[bpo, bst], writes=[bbq])
                            p.op("dve", lambda e, po=po, pj=pj, rs=rs: e.tensor_scalar(pj[:, 1152:1536], po[:, 128:512], rs, None, op0=ALU.mult),
                                 reads=[bpo, bst], writes=[bpj])
                        else:
                            p.op("dve", lambda e, po=po, pj=pj, rs=rs, n=n: e.tensor_scalar(pj[:, n * 512:(n + 1) * 512], po[:], rs, None, op0=ALU.mult),
                                 reads=[bpo, bst], writes=[bpj])
                    wk, bwk = wk_r.next()
                    s6, bs6 = sb6_r.next()
                    bq3 = bq[:].rearrange("p (h d) -> p h d", d=64)
                    sq3 = wk[:, 0, :].rearrange("p (h d) -> p h d", d=64)
                    qn = wk[:, 1, :]
                    qn3 = qn.rearrange("p (h d) -> p h d", d=64)
                    qn4 = qn.rearrange("p (a b c) -> p a b c", b=2, c=16)
                    rot4 = wk[:, 2, :].rearrange("p (a b c) -> p a b c", b=2, c=16)
                    rot3 = wk[:, 2, :].rearrange("p (h d) -> p h d", d=64)
                    t13 = wk[:, 3, :].rearrange("p (h d) -> p h d", d=64)
                    t23 = wk[:, 4, :].rearrange("p (h d) -> p h d", d=64)
                    p.op("pool", lambda e, wk=wk, bq=bq: e.tensor_tensor(wk[:, 0, :], bq[:], bq[:], ALU.mult), reads=[bbq], writes=[bwk])
                    p.op("dve", lambda e, s6=s6, sq3=sq3: e.tensor_reduce(s6[:, 0:6], sq3, axis=AX.X, op=ALU.add), reads=[bwk], writes=[bs6])
                    p.op("act", lambda e, s6=s6: e.activation(s6[:, 6:12], s6[:, 0:6], AF.Sqrt, bias=EPS, scale=1.0 / 64), reads=[bs6], writes=[bs6])
                    p.op("dve", lambda e, s6=s6: e.reciprocal(s6[:, 0:6], s6[:, 6:12]), reads=[bs6], writes=[bs6])
                    p.op("dve", lambda e, qn3=qn3, bq3=bq3, s6=s6: e.tensor_tensor(qn3, bq3, s6[:, 0:6].unsqueeze(2).to_broadcast([128, 6, 64]), ALU.mult),
                         reads=[bbq, bs6], writes=[bwk])
                    p.op("pool", lambda e, qn=qn: e.tensor_tensor(qn, qn, gqkt[:], ALU.mult), reads=[bwk, bP], writes=[bwk])
                    p.op("pool", lambda e, rot4=rot4, qn4=qn4: e.tensor_copy(rot4[:, :, 0, :], qn4[:, :, 1, :]), reads=[bwk], writes=[bwk])
                    p.op("pool", lambda e, rot4=rot4, qn4=qn4: e.tensor_copy(rot4[:, :, 1, :], qn4[:, :, 0, :]), reads=[bwk], writes=[bwk])
                    p.op("dve", lambda e, t13=t13, qn3=qn3, rp=rp: e.tensor_tensor(t13, qn3, rp[:, 0:64].unsqueeze(1).to_broadcast([128, 6, 64]), ALU.mult),
                         reads=[bwk, brp], writes=[bwk])
                    p.op("pool", lambda e, t23=t23, rot3=rot3, rp=rp: e.tensor_tensor(t23, rot3, rp[:, 64:128].unsqueeze(1).to_broadcast([128, 6, 64]), ALU.mult),
                         reads=[bwk, brp], writes=[bwk])
                    p.op("dve", lambda e, pj=pj, wk=wk: e.tensor_tensor(pj[:, 768:1152], wk[:, 3, :], wk[:, 4, :], ALU.add),
                         reads=[bwk], writes=[bpj])
                    return (pj, bpj, gt, bgt, r0)

                def partB(ctx):
                    pj, bpj, gt, bgt, r0 = ctx
                    pq, bpq = pq_r.next()
                    for bi, c0 in enumerate(QK_SRC):
                        p.op("pe", lambda e, pq=pq, pj=pj, bi=bi, c0=c0: e.transpose(pq[:, bi, :], pj[:, c0:c0 + 128], ident[:]),
                             reads=[bpj, b_const], writes=[bpq])
                    qk, bqk = qk_r.next()
                    p.op("act", lambda e, qk=qk, pq=pq: e.copy(qk[:], pq[:, 0:14, :]), reads=[bpq], writes=[bqk])
                    p.dma(lambda e, pj=pj, r0=r0: e.dma_start(out=pj_d[r0:r0 + 128, :], in_=pj[:]), reads=[bpj])
                    p.dma(lambda e, gt=gt, r0=r0: e.dma_start(out=g_d[r0:r0 + 128, :], in_=gt[:]), reads=[bgt])
                    p.dma(lambda e, qk=qk, r0=r0: e.dma_start(out=qkT_r[:, :, r0:r0 + 128], in_=qk[:]), reads=[bqk])
                    if dbg and l == 0:
                        p.dma(lambda e, pj=pj, r0=r0: e.dma_start(out=dbg_o["pj"][r0:r0 + 128, :], in_=pj[:]), reads=[bpj])

                lq = [loadA(i) for i in range(min(2, NTILES))]
                pq = prepA(lq.pop(0))
                prevctx = None
                for t in range(NTILES):
                    if t + 2 < NTILES:
                        lq.append(loadA(t + 2))
                    nxt = [None]

                    def hook(t=t, prevctx=prevctx, nxt=nxt):
                        if t + 1 < NTILES:
                            nxt[0] = prepA(lq.pop(0))
                        if prevctx is not None:
                            partB(prevctx)
                    prevctx = mmA(pq, hook)
                    pq = nxt[0]
                partB(prevctx)
            p.barrier()

        def phase2(l):
            lambda_init = 0.8 - 0.6 * math.exp(-0.3 * l)
            with ExitStack() as es:
                sb, ps = mk(es)
                Qt = [sb([128, SMAX], BF16) for _ in range(2)]
                Kt = [sb([128, SMAX], BF16) for _ in range(2)]
                bQ = [Buf(), Buf()]
                bK = [Buf(), Buf()]
                Vraw = sb([128, TMAX, 256], BF16)
                bVraw = Buf()
                Va = [sb([128, TMAX, 65], BF16) for _ in range(4)]
                bVa = [Buf() for _ in range(4)]
                ab = sb([128, NA_SLABS * 512], BF16)
                db = sb([128, 3 * 512], F32)
                cd = sb([128, 5 * 512], F32)
                sm = sb([128, 64], F32)
                gcol = sb([64, 2], F32)
                bpar = Buf()
                p.dma(lambda e: e.dma_start(out=ab[:], in_=abias[l]), writes=[bpar], queue="pool")
                p.dma(lambda e: e.dma_start(out=db[:], in_=dbias), writes=[bpar])
                p.dma(lambda e: e.dma_start(out=cd[:], in_=cdist), writes=[bpar])
                bsm = Buf()
                p.op("dve", lambda e: e.memset(sm[:], 0.0), writes=[bsm])
                p.dma(lambda e: e.dma_start(out=sm[64:65, 0:4], in_=sink[l:l + 1, :]), writes=[bsm])
                lam4 = sb([128, 128], F32)
                p.op("dve", lambda e: e.memset(lam4[:], 0.0), writes=[bsm])
                p.dma(lambda e: e.dma_start(out=lam4[64:65, :], in_=lamv[l:l + 1, :]), writes=[bsm])
                p.dma(lambda e: e.dma_start(out=gcol[:, 0:1], in_=subln[l]), writes=[bsm])
                p.op("act", lambda e: e.activation(sm[64:65, 4:8], sm[64:65, 0:4], AF.Exp), reads=[bsm], writes=[bsm])
                p.op("dve", lambda e: e.tensor_tensor(lam4[64:65, 0:32], lam4[64:65, 0:32], lam4[64:65, 32:64], ALU.mult), reads=[bsm], writes=[bsm])
                p.op("dve", lambda e: e.tensor_tensor(lam4[64:65, 64:96], lam4[64:65, 64:96], lam4[64:65, 96:128], ALU.mult), reads=[bsm], writes=[bsm])
                p.op("dve", lambda e: e.tensor_reduce(sm[64:65, 8:9], lam4[64:65, 0:32], axis=AX.X, op=ALU.add), reads=[bsm], writes=[bsm])
                p.op("dve", lambda e: e.tensor_reduce(sm[64:65, 9:10], lam4[64:65, 64:96], axis=AX.X, op=ALU.add), reads=[bsm], writes=[bsm])
                p.op("act", lambda e: e.activation(sm[64:65, 10:12], sm[64:65, 8:10], AF.Exp), reads=[bsm], writes=[bsm])
                p.op("dve", lambda e: e.tensor_tensor(sm[64:65, 12:13], sm[64:65, 11:12], sm[64:65, 10:11], ALU.subtract), reads=[bsm], writes=[bsm])
                p.op("dve", lambda e: e.tensor_scalar(sm[64:65, 13:14], sm[64:65, 12:13], -lambda_init, None, op0=ALU.add), reads=[bsm], writes=[bsm])
                nlam = sm[64:65, 13:14]
                p.op("act", lambda e: e.mul(gcol[:, 1:2], gcol[:, 0:1], 1.0 - lambda_init), reads=[bsm], writes=[bsm])

                cbt = sb([128, 512], F32)
                Ft = sb([128, 8, 512], F32)
                bcbt = Buf()
                p.dma(lambda e: e.dma_start(out=cbt[:], in_=cbias_d), writes=[bcbt])
                for r_ in range(8):
                    p.dma(lambda e, r_=r_: e.dma_start(out=Ft[:, r_, :], in_=cF_d[r_:r_ + 1, :].to_broadcast([128, 512])), writes=[bcbt])
                osum_r = Ring([sb([128, 512], F32) for _ in range(4)])
                tmpo_r = Ring([sb([128, 512], F32) for _ in range(2)])

                sbank_r = Ring([ps([128, 512], F32) for _ in range(4)])
                acc_r = Ring([ps([128, 512], F32) for _ in range(4)])
                fin_r = sbank_r
                tmp_r = Ring([sb([128, 512], F32) for _ in range(4)])
                pt_r = Ring([sb([128, 512], BF16) for _ in range(16)])
                rl_r = Ring([sb([128, 512], F32) for _ in range(2)])
                bcs_r = Ring([sb([64, 512], F32) for _ in range(2)])
                ot_r = Ring([sb([64, 512], BF16) for _ in range(3)])
                f32w_r = Ring([sb([64, 512], F32) for _ in range(6)])

                def load_mixer(name, si):
                    S = seqs[si]
                    T = S // 128
                    s0 = seq_start[si]
                    mx = MIX[name]
                    for i, blk in enumerate(mx["qb"]):
                        p.dma(lambda e, i=i, blk=blk: e.dma_start(out=Qt[i][:, 0:S], in_=qkT_r[:, blk, s0:s0 + S]), writes=[bQ[i]])
                    if len(mx["kb"]) == 2:
                        for i, blk in enumerate(mx["kb"]):
                            p.dma(lambda e, i=i, blk=blk: e.dma_start(out=Kt[i][:, 0:S], in_=qkT_r[:, blk, s0:s0 + S]), writes=[bK[i]])
                    else:
                        blk = mx["kb"][0]
                        for j in range(2):
                            for half in range(2):
                                p.dma(lambda e, j=j, half=half: e.dma_start(out=Kt[j][half * 64:(half + 1) * 64, 0:S],
                                                                            in_=qkT_r[j * 64:(j + 1) * 64, blk, s0:s0 + S]), writes=[bK[j]])
                    vw = mx["vw"]
                    p.dma(lambda e: e.dma_start(out=Vraw[:, 0:T, 0:vw],
                                                in_=pj_d[s0:s0 + S, mx["vc"]:mx["vc"] + vw].rearrange("(t p) c -> p t c", p=128)),
                          writes=[bVraw])
                    for j in range(mx["nkv"]):
                        p.op("pool", lambda e, j=j: e.tensor_copy(Va[j][:, 0:T, 0:64], Vraw[:, 0:T, j * 64:(j + 1) * 64]),
                             reads=[bVraw], writes=[bVa[j]])
                        p.op("pool", lambda e, j=j: e.memset(Va[j][:, 0:T, 64:65], 1.0), writes=[bVa[j]])

                def finalize_simple(acc, bacc, ncols, dest_fn, add_sink=False, split4=False):
                    rl, brl = rl_r.next()
                    if add_sink:
                        for h in range(4):
                            p.op("dve", lambda e, h=h: e.tensor_scalar(rl[64:65, h * 128:(h + 1) * 128], acc[64:65, h * 128:(h + 1) * 128],
                                                                      sm[64:65, 4 + h:5 + h], None, op0=ALU.add), reads=[bacc, bsm], writes=[brl])
                        p.op("dve", lambda e: e.reciprocal(rl[64:65, 0:ncols], rl[64:65, 0:ncols]), reads=[brl], writes=[brl])
                    else:
                        p.op("dve", lambda e: e.reciprocal(rl[64:65, 0:ncols], acc[64:65, 0:ncols]), reads=[bacc], writes=[brl])
                    fin, bfin = fin_r.next()
                    p.op("pe", lambda e: e.matmul(fin[0:64, 0:ncols], lhsT=onesf[64:65, 0:64], rhs=rl[64:65, 0:ncols], start=True, stop=True),
                         reads=[brl, b_const], writes=[bfin])
                    bcs, bbcs = bcs_r.next()
                    p.op("act", lambda e: e.copy(bcs[:, 0:ncols], fin[0:64, 0:ncols]), reads=[bfin], writes=[bbcs])
                    ot, bot = ot_r.next()
                    p.op("dve", lambda e: e.tensor_tensor(ot[:, 0:ncols], acc[0:64, 0:ncols], bcs[:, 0:ncols], ALU.mult),
                         reads=[bacc, bbcs], writes=[bot])
                    if split4:
                        for h4 in range(4):
                            p.dma(lambda e, h4=h4: e.dma_start(out=dest_fn(h4), in_=ot[:, h4 * 128:(h4 + 1) * 128]), reads=[bot])
                    else:
                        p.dma(lambda e: e.dma_start(out=dest_fn(), in_=ot[:, 0:ncols]), reads=[bot])
                    return ot

                def local_attn(name, si):
                    import os
                    LSK = os.environ.get("MK_LSK", "")
                    S = seqs[si]
                    T = S // 128
                    s0 = seq_start[si]
                    mx = MIX[name]
                    isA = name == "A"
                    gqa = not isA

                    def keyt(m):
                        if isA:
                            return na_tiles(m, T)
                        return [(m + dl, dl + 1) for dl in (-1, 0, 1) if 0 <= m + dl < T]

                    def stage1(m):
                        pts = []
                        for (j, slab) in keyt(m):
                            tmp, btmp = tmp_r.next()
                            for g in range(2):
                                sbk, bsb = sbank_r.next()
                                pr = slice(64 * g, 64 * g + 64)
                                for i, h in enumerate((g, g + 2)):
                                    Kh, bKh = (Kt[h // 2], bK[h // 2])
                                    Qh, bQh = Qt[h // 2], bQ[h // 2]
                                    p.op("pe", lambda e, sbk=sbk, i=i, pr=pr, Kh=Kh, Qh=Qh, j=j, m=m: e.matmul(
                                        sbk[:, i * 128:(i + 1) * 128], lhsT=Kh[pr, j * 128:(j + 1) * 128], rhs=Qh[pr, m * 128:(m + 1) * 128],
                                        start=True, stop=True), reads=[bKh, bQh], writes=[bsb])
                                bias_g = (ab if isA else db)[:, slab * 512 + g * 256:slab * 512 + (g + 1) * 256]
                                p.op("dve", lambda e, tmp=tmp, sbk=sbk, g=g, bias_g=bias_g: e.scalar_tensor_tensor(
                                    out=tmp[:, g * 256:(g + 1) * 256], in0=sbk[:, 0:256], scalar=0.125, in1=bias_g, op0=ALU.mult, op1=ALU.add),
                                    reads=[bsb, bpar], writes=[btmp])
                            pt, bpt = pt_r.next()
                            p.op("act", lambda e, pt=pt, tmp=tmp: e.activation(pt[:], tmp[:], AF.Exp), reads=[btmp], writes=[bpt])
                            pts.append((j, pt, bpt))
                        return pts

                    def stage2(m, pts):
                        if "2" in LSK:
                            return
                        acc, bacc = acc_r.next()
                        for h in range(4):
                            kv = h if isA else h // 2
                            for ii, (j, pt, bpt) in enumerate(pts):
                                hp = (0, 2, 1, 3)[h]
                                p.op("pe", lambda e, acc=acc, h=h, kv=kv, j=j, pt=pt, ii=ii, hp=hp: e.matmul(
                                    acc[0:65, h * 128:(h + 1) * 128], lhsT=Va[kv][:, j, 0:65], rhs=pt[:, hp * 128:(hp + 1) * 128],
                                    start=(ii == 0), stop=(ii == len(pts) - 1)), reads=[bVa[kv], bpt], writes=[bacc])
                        t0 = s0 + m * 128
                        if "f" in LSK:
                            return
                        finalize_simple(acc, bacc, 512,
                                        lambda h4: oT_h[:, mx["idx"] * 4 + h4, t0:t0 + 128],
                                        add_sink=(name == "D"), split4=True)

                    prev = stage1(0)
                    for m in range(T):
                        nxt = stage1(m + 1) if m + 1 < T else None
                        stage2(m, prev)
                        prev = nxt

                def finalize_c(h, t0, a1, ba1, a2, ba2):
                    rl, brl = rl_r.next()
                    rl2, brl2 = rl_r.next()
                    p.op("dve", lambda e: e.reciprocal(rl[64:65, :], a1[64:65, :]), reads=[ba1], writes=[brl])
                    p.op("dve", lambda e: e.reciprocal(rl2[64:65, :], a2[64:65, :]), reads=[ba2], writes=[brl2])
                    p.op("dve", lambda e: e.tensor_scalar(rl2[64:65, :], rl2[64:65, :], nlam, None, op0=ALU.mult), reads=[brl2, bsm], writes=[brl2])
                    o1, bo1 = f32w_r.next()
                    o2, bo2 = f32w_r.next()
                    for (rr, brr, aa, baa, oo, boo) in ((rl, brl, a1, ba1, o1, bo1), (rl2, brl2, a2, ba2, o2, bo2)):
                        fin, bfin = fin_r.next()
                        p.op("pe", lambda e, fin=fin, rr=rr: e.matmul(fin[0:64, :], lhsT=onesf[64:65, 0:64], rhs=rr[64:65, :], start=True, stop=True),
                             reads=[brr, b_const], writes=[bfin])
                        bcs, bbcs = bcs_r.next()
                        p.op("act", lambda e, bcs=bcs, fin=fin: e.copy(bcs[:], fin[0:64, :]), reads=[bfin], writes=[bbcs])
                        p.op("dve", lambda e, oo=oo, aa=aa, bcs=bcs: e.tensor_tensor(oo[:], aa[0:64, :], bcs[:], ALU.mult),
                             reads=[baa, bbcs], writes=[boo])
                    p.op("pool", lambda e: e.tensor_tensor(o1[:], o1[:], o2[:], ALU.add), reads=[bo2], writes=[bo1])
                    p.op("pool", lambda e: e.tensor_tensor(o2[:], o1[:], o1[:], ALU.mult), reads=[bo1], writes=[bo2])
                    fin2, bfin2 = fin_r.next()
                    p.op("pe", lambda e: e.matmul(fin2[0:64, :], lhsT=onesf[0:64, 0:64], rhs=o2[:], start=True, stop=True),
                         reads=[bo2, b_const], writes=[bfin2])
                    sd, bsd = f32w_r.next()
                    p.op("act", lambda e: e.activation(sd[:], fin2[0:64, :], AF.Sqrt, bias=EPS, scale=1.0 / 64), reads=[bfin2], writes=[bsd])
                    p.op("dve", lambda e: e.reciprocal(sd[:], sd[:]), reads=[bsd], writes=[bsd])
                    ot, bot = ot_r.next()
                    p.op("dve", lambda e: e.scalar_tensor_tensor(out=ot[:], in0=o1[:], scalar=gcol[:, 1:2], in1=sd[:], op0=ALU.mult, op1=ALU.mult),
                         reads=[bo1, bsd, bsm], writes=[bot])
                    p.dma(lambda e: e.dma_start(out=oT_h[:, 8 + h, t0:t0 + 512], in_=ot[:]), reads=[bot])

                def dense_attn(name, si):
                    S = seqs[si]
                    T = S // 128
                    s0 = seq_start[si]
                    mx = MIX[name]
                    isC = name == "C"
                    nq = S // 512
                    scale = (32 ** -0.5) if isC else 0.125

                    def make_stream(qb, h, mp, acc, bacc):
                        if isC:
                            base = 64 * (h % 2) + 32 * mp
                            pr = slice(base, base + 32)
                            Kh, bKh, Qh, bQh = Kt[h // 2], bK[h // 2], Qt[h // 2], bQ[h // 2]
                            kv = h
                            cs = DIFF_SLOPES[h] / scale
                        else:
                            base = 64 * (h % 2)
                            pr = slice(base, base + 64)
                            Kh, bKh, Qh, bQh = Kt[h // 2], bK[h // 2], Qt[h // 2], bQ[h // 2]
                            kv = h // 2
                        kw = dict(tile_position=(base, 0)) if base == 96 else {}

                        def grp(kt):
                            return 0 if kt < 4 * qb else (1 if kt < 4 * qb + 4 else 2)

                        def qk(kt):
                            sbk, bsb = sbank_r.next()
                            p.op("pe", lambda e: e.matmul(sbk[:], lhsT=Kh[pr, kt * 128:(kt + 1) * 128], rhs=Qh[pr, qb * 512:(qb + 1) * 512],
                                                          start=True, stop=True, **kw), reads=[bKh, bQh], writes=[bsb])
                            pt, bpt = pt_r.next()
                            if not isC:
                                p.op("act", lambda e: e.activation(pt[:], sbk[:], AF.Exp, scale=scale), reads=[bsb], writes=[bpt])
                            else:
                                dl = qb * 512 - kt * 128
                                g = grp(kt)
                                if g != 1:
                                    n = abs(dl) // 128
                                    ci = h * 64 + (0 if g == 0 else 32) + n
                                    p.op("act", lambda e: e.activation(pt[:], sbk[:], AF.Exp, bias=cbt[:, ci:ci + 1], scale=scale),
                                         reads=[bsb, bcbt], writes=[bpt])
                                else:
                                    tmp, btmp = tmp_r.next()
                                    di = {0: 1, -128: 2, -256: 3, -384: 4}[dl]
                                    p.op("dve", lambda e: e.scalar_tensor_tensor(out=tmp[:], in0=cd[:, di * 512:(di + 1) * 512], scalar=-cs, in1=sbk[:],
                                                                                 op0=ALU.mult, op1=ALU.add), reads=[bsb, bpar], writes=[btmp])
                                    p.op("act", lambda e: e.activation(pt[:], tmp[:], AF.Exp, scale=scale), reads=[btmp], writes=[bpt])
                            return pt, bpt

                        if not isC:
                            def pv(kt, pt, bpt):
                                p.op("pe", lambda e: e.matmul(acc[0:65, :], lhsT=Va[kv][:, kt, 0:65], rhs=pt[:], start=(kt == 0), stop=(kt == T - 1)),
                                     reads=[bVa[kv], bpt], writes=[bacc])
                            return qk, pv

                        stt = {"acc": None, "first": True}

                        def pv(kt, pt, bpt):
                            g = grp(kt)
                            gfirst = (kt == 0) or grp(kt - 1) != g
                            glast = (kt == T - 1) or grp(kt + 1) != g
                            if gfirst:
                                stt["acc"] = acc_r.next()
                            pa, bpa = stt["acc"]
                            p.op("pe", lambda e: e.matmul(pa[0:65, :], lhsT=Va[kv][:, kt, 0:65], rhs=pt[:], start=gfirst, stop=glast),
                                 reads=[bVa[kv], bpt], writes=[bpa])
                            if not glast:
                                return
                            first = stt["first"]
                            stt["first"] = False
                            if g == 1:
                                if first:
                                    p.op("dve", lambda e: e.tensor_copy(acc[0:65, :], pa[0:65, :]), reads=[bpa], writes=[bacc])
                                else:
                                    p.op("dve", lambda e: e.tensor_tensor(acc[0:65, :], pa[0:65, :], acc[0:65, :], ALU.add), reads=[bpa], writes=[bacc])
                            else:
                                Fh = Ft[0:65, h * 2 + (0 if g == 0 else 1), :]
                                if first:
                                    p.op("dve", lambda e: e.tensor_tensor(acc[0:65, :], pa[0:65, :], Fh, ALU.mult), reads=[bpa, bcbt], writes=[bacc])
                                else:
                                    to, bto = tmpo_r.next()
                                    p.op("dve", lambda e: e.tensor_tensor(to[0:65, :], pa[0:65, :], Fh, ALU.mult), reads=[bpa, bcbt], writes=[bto])
                                    p.op("pool", lambda e: e.tensor_tensor(acc[0:65, :], acc[0:65, :], to[0:65, :], ALU.add), reads=[bto], writes=[bacc])

                        return qk, pv

                    PD = 2

                    pend = [None]

                    def run_streams(sts):
                        q = [[qk(i) for (qk, pv) in sts] for i in range(min(PD, T))]
                        for kt in range(T):
                            if kt + PD < T:
                                q.append([qk(kt + PD) for (qk, pv) in sts])
                            cur = q.pop(0)
                            for (qk, pv), pr in zip(sts, cur):
                                pv(kt, *pr)
                            if kt == 1 and pend[0] is not None:
                                f_ = pend[0]
                                pend[0] = None
                                f_()
                        if pend[0] is not None:
                            f_ = pend[0]
                            pend[0] = None
                            f_()

                    for qb in range(nq):
                        t0 = s0 + qb * 512
                        if not isC:
                            for j in range(2):
                                accs = [acc_r.next(), acc_r.next()]
                                run_streams([make_stream(qb, 2 * j + g, 0, accs[g][0], accs[g][1]) for g in range(2)])

                                def fin_b(accs=accs, j=j, t0=t0):
                                    for g in range(2):
                                        finalize_simple(accs[g][0], accs[g][1], 512, lambda h=2 * j + g, t0=t0: oT_h[:, 4 + h, t0:t0 + 512])
                                pend[0] = fin_b
                        else:
                            for h in range(4):
                                a1, ba1 = osum_r.next()
                                a2, ba2 = osum_r.next()
                                run_streams([make_stream(qb, h, 0, a1, ba1), make_stream(qb, h, 1, a2, ba2)])
                                pend[0] = (lambda h=h, t0=t0, a1=a1, ba1=ba1, a2=a2, ba2=ba2: finalize_c(h, t0, a1, ba1, a2, ba2))
                    if pend[0] is not None:
                        f_ = pend[0]
                        pend[0] = None
                        f_()

                import os
                for si in range(len(seqs)):
                    for name in os.environ.get("MK_MIX", "ABCD"):
                        load_mixer(name, si)
                        if name in ("A", "D"):
                            local_attn(name, si)
                        else:
                            dense_attn(name, si)
            p.barrier()
            import os
            if dbg and l == 0 and "d" not in os.environ.get("MK_SKIP", ""):
                with ExitStack() as es2:
                    sb2, _ = mk(es2)
                    dr = Ring([sb2([128, 8, 128], BF16) for _ in range(2)])
                    dbg_c = dbg_o["oT"].rearrange("(c p) n -> p c n", p=128)
                    for t in range(NTILES):
                        tt, btt = dr.next()
                        p.dma(lambda e, tt=tt, t=t: e.dma_start(out=tt[:], in_=oT_c[:, :, t * 128:(t + 1) * 128]), writes=[btt])
                        p.dma(lambda e, tt=tt, t=t: e.dma_start(out=dbg_c[:, :, t * 128:(t + 1) * 128], in_=tt[:]), reads=[btt])
                p.barrier()

        def phase3a(l):
            src = x_in if l == 0 else y
            with ExitStack() as es:
                sb, ps = mk(es)
                Wb = sb([128, 8, D], BF16)
                Wo = sb([128, 8, D], BF16)
                bW = Buf()
                bWo = Buf()
                p.dma(lambda e: e.dma_start(out=Wb[:], in_=w_br[l]), writes=[bW], queue="pool")
                p.dma(lambda e: e.dma_start(out=Wo[:], in_=w_out[l]), writes=[bWo], queue="pool")
                gT = sb([128, 16], F32)
                gpo = sb([128, D], F32)
                bP = Buf()
                p.dma(lambda e: e.dma_start(out=gT[:], in_=gT_pre[l]), writes=[bP])
                p.dma(lambda e: e.dma_start(out=gpo[:], in_=g_post[l, 0:1, :].to_broadcast([128, D])), writes=[bP])
                xin_r = Ring([sb([128, D], F32) for _ in range(4)])
                oTt_r = Ring([sb([128, 8, 128], BF16) for _ in range(3)])
                gt_r = Ring([sb([128, 4096], BF16) for _ in range(3)])
                mg_r = Ring([sb([128, D], F32) for _ in range(2)])
                tm_r = Ring([sb([128, 512], F32) for _ in range(3)])
                mbf_r = Ring([sb([128, D], BF16) for _ in range(2)])
                mT_r = Ring([sb([128, 8, 128], BF16) for _ in range(2)])
                st_r = Ring([sb([128, 16], F32) for _ in range(2)])
                xn_r = Ring([sb([128, D], F32) for _ in range(2)])
                hbf_r = Ring([sb([128, D], BF16) for _ in range(2)])
                hT_r = Ring([sb([128, 8, 128], BF16) for _ in range(2)])
                junk = sb([128, D], BF16)
                bjunk = Buf()
                br_r = Ring([ps([128, 512], F32) for _ in range(2)])
                pT_r = Ring([ps([128, 8, 128], BF16) for _ in range(1)])
                ob_r = Ring([ps([128, 512], F32) for _ in range(4)])
                pT2_r = Ring([ps([128, 8, 128], BF16) for _ in range(1)])
                def L3(t):
                    r0 = t * 128
                    xin, bxin = xin_r.next()
                    p.dma(lambda e, xin=xin, r0=r0: e.dma_start(out=xin[:], in_=src[r0:r0 + 128, :]), writes=[bxin])
                    oTt, boT = oTt_r.next()
                    p.dma(lambda e, oTt=oTt, r0=r0: e.dma_start(out=oTt[:], in_=oT_c[:, :, r0:r0 + 128]), writes=[boT])
                    gt, bgt = gt_r.next()
                    p.dma(lambda e, gt=gt, r0=r0: e.dma_start(out=gt[:], in_=g_d[r0:r0 + 128, :]), writes=[bgt])
                    return (xin, bxin, oTt, boT, gt, bgt, r0)

                def S1(lctx):
                    xin, bxin, oTt, boT, gt, bgt, r0 = lctx
                    mg, bmg = mg_r.next()
                    for nh in range(2):
                        for i in range(4):
                            br, bbr = br_r.next()
                            for c in range(2):
                                p.op("pe", lambda e, br=br, oTt=oTt, i=i, c=c, nh=nh: e.matmul(
                                    br[:], lhsT=oTt[:, 2 * i + c, :], rhs=Wb[:, 2 * i + c, nh * 512:(nh + 1) * 512], start=(c == 0), stop=(c == 1)),
                                    reads=[boT, bW], writes=[bbr])
                            gsl = gt[:, i * 1024 + nh * 512:i * 1024 + (nh + 1) * 512]
                            if i == 0:
                                p.op("dve", lambda e, mg=mg, br=br, gsl=gsl, nh=nh: e.tensor_tensor(mg[:, nh * 512:(nh + 1) * 512], br[:], gsl, ALU.mult),
                                     reads=[bbr, bgt], writes=[bmg])
                            else:
                                tm, btm = tm_r.next()
                                p.op("dve", lambda e, tm=tm, br=br, gsl=gsl: e.tensor_tensor(tm[:], br[:], gsl, ALU.mult), reads=[bbr, bgt], writes=[btm])
                                p.op("pool", lambda e, mg=mg, tm=tm, nh=nh: e.tensor_tensor(mg[:, nh * 512:(nh + 1) * 512], mg[:, nh * 512:(nh + 1) * 512], tm[:], ALU.add),
                                     reads=[btm], writes=[bmg])
                    mbf, bmbf = mbf_r.next()
                    p.op("act", lambda e, mbf=mbf, mg=mg: e.copy(mbf[:], mg[:]), reads=[bmg], writes=[bmbf])
                    return (mbf, bmbf, xin, bxin, r0)

                def S2a(c1):
                    mbf, bmbf, xin, bxin, r0 = c1
                    pT, bpT = pT_r.next()
                    for k in range(8):
                        p.op("pe", lambda e, pT=pT, mbf=mbf, k=k: e.transpose(pT[:, k, :], mbf[:, k * 128:(k + 1) * 128], ident[:]),
                             reads=[bmbf, b_const], writes=[bpT])
                    mT, bmT = mT_r.next()
                    p.op("dve", lambda e, mT=mT, pT=pT: e.tensor_copy(mT[:], pT[:]), reads=[bpT], writes=[bmT])
                    return (mT, bmT, xin, bxin, r0)

                def S2b(c1b):
                    mT, bmT, xin, bxin, r0 = c1b
                    st, bst = st_r.next()
                    p.op("pool", lambda e, st=st: e.memset(st[:], 0.0), writes=[bst])
                    obs = []
                    for nh in range(2):
                        ob, bob = ob_r.next()
                        obs.append((ob, bob))
                        for k in range(8):
                            p.op("pe", lambda e, ob=ob, mT=mT, k=k, nh=nh: e.matmul(ob[:], lhsT=mT[:, k, :], rhs=Wo[:, k, nh * 512:(nh + 1) * 512],
                                                                               start=(k == 0), stop=(k == 7)), reads=[bmT, bWo], writes=[bob])
                        p.op("act", lambda e, ob=ob, st=st, nh=nh: e.activation(junk[:, 0:512], ob[:], AF.Square, accum_out=st[:, nh:nh + 1]),
                             reads=[bob], writes=[bst, bjunk])
                    p.op("dve", lambda e, st=st: e.tensor_tensor(st[:, 2:3], st[:, 0:1], st[:, 1:2], ALU.add), reads=[bst], writes=[bst])
                    p.op("act", lambda e, st=st: e.activation(st[:, 3:4], st[:, 2:3], AF.Sqrt, bias=EPS, scale=1.0 / D), reads=[bst], writes=[bst])
                    p.op("dve", lambda e, st=st: e.reciprocal(st[:, 4:5], st[:, 3:4]), reads=[bst], writes=[bst])
                    xn, bxn = xn_r.next()
                    for nh in range(2):
                        ob, bob = obs[nh]
                        tm, btm = tm_r.next()
                        p.op("dve", lambda e, tm=tm, ob=ob, st=st, nh=nh: e.scalar_tensor_tensor(
                            out=tm[:], in0=ob[:], scalar=st[:, 4:5], in1=gpo[:, nh * 512:(nh + 1) * 512], op0=ALU.mult, op1=ALU.mult),
                            reads=[bob, bst, bP], writes=[btm])
                        p.op("pool", lambda e, xn=xn, tm=tm, xin=xin, nh=nh: e.tensor_tensor(xn[:, nh * 512:(nh + 1) * 512], tm[:], xin[:, nh * 512:(nh + 1) * 512], ALU.add),
                             reads=[btm, bxin], writes=[bxn])
                    p.dma(lambda e, xn=xn, r0=r0: e.dma_start(out=y[r0:r0 + 128, :], in_=xn[:]), reads=[bxn])
                    if dbg and l == 0:
                        p.dma(lambda e, xn=xn, r0=r0: e.dma_start(out=dbg_o["x1"][r0:r0 + 128, :], in_=xn[:]), reads=[bxn])
                    p.op("act", lambda e, xn=xn, st=st: e.activation(junk[:], xn[:], AF.Square, accum_out=st[:, 5:6]), reads=[bxn], writes=[bst, bjunk])
                    p.op("act", lambda e, st=st: e.activation(st[:, 6:7], st[:, 5:6], AF.Sqrt, bias=EPS, scale=1.0 / D), reads=[bst], writes=[bst])
                    p.op("dve", lambda e, st=st: e.reciprocal(st[:, 7:8], st[:, 6:7]), reads=[bst], writes=[bst])
                    hbf, bhbf = hbf_r.next()
                    p.op("dve", lambda e, hbf=hbf, xn=xn, st=st: e.tensor_scalar(hbf[:], xn[:], st[:, 7:8], None, op0=ALU.mult), reads=[bxn, bst], writes=[bhbf])
                    return (hbf, bhbf, r0)

                def S3(c2):
                    hbf, bhbf, r0 = c2
                    pT2, bpT2 = pT2_r.next()
                    for k in range(8):
                        p.op("pe", lambda e, pT2=pT2, hbf=hbf, k=k: e.transpose(pT2[:, k, :], hbf[:, k * 128:(k + 1) * 128], ident[:]),
                             reads=[bhbf, b_const], writes=[bpT2])
                    hTt, bhT = hT_r.next()
                    p.op("dve", lambda e, hTt=hTt, pT2=pT2: e.tensor_tensor(hTt[:], pT2[:], gT[:, 8:16].unsqueeze(2).to_broadcast([128, 8, 128]), ALU.mult),
                         reads=[bpT2, bP], writes=[bhT])
                    p.dma(lambda e, hTt=hTt, r0=r0: e.dma_start(out=hT_c[:, :, r0:r0 + 128], in_=hTt[:]), reads=[bhT])

                lq = [L3(i) for i in range(min(2, NTILES))]
                q1, q2 = [], []
                for t in range(NTILES + 2):
                    c1b = None
                    if 0 <= t - 1 < NTILES:
                        c1b = S2a(q1.pop(0))
                    if t < NTILES:
                        if t + 2 < NTILES:
                            lq.append(L3(t + 2))
                        q1.append(S1(lq.pop(0)))
                    if c1b is not None:
                        q2.append(S2b(c1b))
                    if t - 2 >= 0:
                        S3(q2.pop(0))
            p.barrier()

        def phase3b(l):
            with ExitStack() as es:
                sb, ps = mk(es)
                Wu = sb([128, 8, 5632], BF16)
                Wd = sb([128, 22, D], BF16)
                bWu = [Buf() for _ in range(8)]
                bWd = Buf()
                for k in range(8):
                    p.dma(lambda e, k=k: e.dma_start(out=Wu[:, k, :], in_=w_up[l, :, k, :]), writes=[bWu[k]], queue="pool")
                p.dma(lambda e: e.dma_start(out=Wd[:], in_=w_dn[l]), writes=[bWd], queue="pool")
                cw = sb([128, 44, 4], F32)
                gpo = sb([128, D], F32)
                bP = Buf()
                p.dma(lambda e: e.dma_start(out=cw[:], in_=convp[l]), writes=[bP])
                p.dma(lambda e: e.dma_start(out=gpo[:], in_=g_post[l, 1:2, :].to_broadcast([128, D])), writes=[bP])
                hs_r = Ring([sb([128, 8, 258], BF16) for _ in range(2)])
                xin_r = Ring([sb([128, D], F32) for _ in range(4)])
                cg_r = Ring([sb([128, 256], F32) for _ in range(2)])
                cv_r = Ring([sb([128, 256], F32) for _ in range(2)])
                gg_r = Ring([sb([128, 256], F32) for _ in range(2)])
                aT_r = Ring([sb([128, 22, 256], BF16) for _ in range(2)])
                tm_r = Ring([sb([128, 512], F32) for _ in range(2)])
                st_r = Ring([sb([128, 8], F32) for _ in range(2)])
                junk = sb([128, 512], BF16)
                bjunk = Buf()
                u_r = Ring([ps([128, 512], F32) for _ in range(4)])
                d_r = Ring([ps([128, 512], F32) for _ in range(4)])
                blocks = []
                for si, S in enumerate(seqs):
                    for b in range(S // 256):
                        blocks.append((seq_start[si] + b * 256, b == 0, b == S // 256 - 1))

                def LB(bi):
                    t0, first, last = blocks[bi]
                    hs, bhs = hs_r.next()
                    lo = 1 if first else 0
                    hi = 257 if last else 258
                    if first:
                        p.op("pool", lambda e: e.memset(hs[:, :, 0:1], 0.0), writes=[bhs])
                    if last:
                        p.op("pool", lambda e: e.memset(hs[:, :, 257:258], 0.0), writes=[bhs])
                    p.dma(lambda e: e.dma_start(out=hs[:, :, lo:hi], in_=hT_c[:, :, t0 - 1 + lo:t0 - 1 + hi]), writes=[bhs])
                    return (hs, bhs, t0)

                def UP(lctx):
                    hs, bhs, t0 = lctx
                    xs = []
                    for a in range(2):
                        r0 = t0 + a * 128
                        xin, bxin = xin_r.next()
                        p.dma(lambda e, xin=xin, r0=r0: e.dma_start(out=xin[:], in_=y[r0:r0 + 128, :]), writes=[bxin])
                        xs.append((xin, bxin, r0))
                    aT, baT = aT_r.next()
                    for fp in range(22):
                        us = []
                        for ch in (fp, fp + 22):
                            u, bu = u_r.next()
                            us.append((u, bu, ch))
                            for k in range(8):
                                p.op("pe", lambda e, u=u, k=k, ch=ch: e.matmul(u[:, 0:258], lhsT=Wu[:, k, ch * 128:(ch + 1) * 128], rhs=hs[:, k, :],
                                                                          start=(k == 0), stop=(k == 7)), reads=[bhs, bWu[k]], writes=[bu])
                        cg, bcg = cg_r.next()
                        cv, bcv = cv_r.next()
                        for (u, bu, ch), (c_, bc_) in zip(us, ((cg, bcg), (cv, bcv))):
                            p.op("act", lambda e, c_=c_, u=u, ch=ch: e.activation(c_[:], u[:, 0:256], AF.Identity, bias=cw[:, ch, 3:4], scale=cw[:, ch, 0:1]),
                                 reads=[bu, bP], writes=[bc_])
                            p.op("dve", lambda e, c_=c_, u=u, ch=ch: e.scalar_tensor_tensor(out=c_[:], in0=u[:, 1:257], scalar=cw[:, ch, 1:2], in1=c_[:], op0=ALU.mult, op1=ALU.add),
                                 reads=[bu, bP], writes=[bc_])
                            p.op("dve", lambda e, c_=c_, u=u, ch=ch: e.scalar_tensor_tensor(out=c_[:], in0=u[:, 2:258], scalar=cw[:, ch, 2:3], in1=c_[:], op0=ALU.mult, op1=ALU.add),
                                 reads=[bu, bP], writes=[bc_])
                        gg, bgg = gg_r.next()
                        p.op("act", lambda e, gg=gg, cg=cg: e.activation(gg[:], cg[:], AF.Gelu_apprx_tanh), reads=[bcg], writes=[bgg])
                        p.op("pool", lambda e, gg=gg, cv=cv, fp=fp: e.tensor_tensor(aT[:, fp, :], gg[:], cv[:], ALU.mult), reads=[bgg, bcv], writes=[baT])
                    return (aT, baT, xs)

                def DOWN(uctx):
                    aT, baT, xs = uctx
                    for a in range(2):
                        xin, bxin, r0 = xs[a]
                        st, bst = st_r.next()
                        p.op("pool", lambda e, st=st: e.memset(st[:], 0.0), writes=[bst])
                        dbs = []
                        for nh in range(2):
                            dbk, bdb = d_r.next()
                            dbs.append((dbk, bdb))
                            for f in range(22):
                                p.op("pe", lambda e, dbk=dbk, f=f, a=a, nh=nh: e.matmul(dbk[:], lhsT=aT[:, f, a * 128:(a + 1) * 128], rhs=Wd[:, f, nh * 512:(nh + 1) * 512],
                                                                                   start=(f == 0), stop=(f == 21)), reads=[baT, bWd], writes=[bdb])
                            p.op("act", lambda e, dbk=dbk, st=st, nh=nh: e.activation(junk[:], dbk[:], AF.Square, accum_out=st[:, nh:nh + 1]), reads=[bdb], writes=[bst, bjunk])
                        p.op("dve", lambda e, st=st: e.tensor_tensor(st[:, 2:3], st[:, 0:1], st[:, 1:2], ALU.add), reads=[bst], writes=[bst])
                        p.op("act", lambda e, st=st: e.activation(st[:, 3:4], st[:, 2:3], AF.Sqrt, bias=EPS, scale=1.0 / D), reads=[bst], writes=[bst])
                        p.op("dve", lambda e, st=st: e.reciprocal(st[:, 4:5], st[:, 3:4]), reads=[bst], writes=[bst])
                        for nh in range(2):
                            dbk, bdb = dbs[nh]
                            tm, btm = tm_r.next()
                            p.op("dve", lambda e, tm=tm, dbk=dbk, st=st, nh=nh: e.scalar_tensor_tensor(
                                out=tm[:], in0=dbk[:], scalar=st[:, 4:5], in1=gpo[:, nh * 512:(nh + 1) * 512], op0=ALU.mult, op1=ALU.mult),
                                reads=[bdb, bst, bP], writes=[btm])
                            p.op("pool", lambda e, tm=tm, xin=xin, nh=nh: e.tensor_tensor(xin[:, nh * 512:(nh + 1) * 512], tm[:], xin[:, nh * 512:(nh + 1) * 512], ALU.add),
                                 reads=[btm], writes=[bxin])
                        p.dma(lambda e, xin=xin, r0=r0: e.dma_start(out=y[r0:r0 + 128, :], in_=xin[:]), reads=[bxin])

                NB = len(blocks)
                lq = [LB(0)]
                prevu = None
                for bi in range(NB):
                    if bi + 1 < NB:
                        lq.append(LB(bi + 1))
                    uctx = UP(lq.pop(0))
                    if prevu is not None:
                        DOWN(prevu)
                    prevu = uctx
                DOWN(prevu)
            p.barrier()

        import os
        stop = int(os.environ.get("MK_STOP", "99"))
        for l in range(depth):
            phase1(l)
            if stop >= 2:
                phase2(l)
            if stop >= 3:
                phase3a(l)
            if stop >= 4:
                phase3b(l)
        p.emit()
    return nc, p


def prep_shared(inp, depth, smax):
    f = lambda a: np.ascontiguousarray(np.asarray(a, dtype=np.float32))
    w_in = f(inp["w_in"]).reshape(depth, 8, 128, 6656).transpose(0, 2, 1, 3)
    w_br = f(inp["w_branch"]).reshape(depth, 8, 128, D).transpose(0, 2, 1, 3)
    w_out = f(inp["w_out"]).reshape(depth, 8, 128, D).transpose(0, 2, 1, 3)
    w_up = f(inp["ffn_w_up"]).reshape(depth, 8, 128, 5632).transpose(0, 2, 1, 3)
    w_dn = f(inp["ffn_w_down"]).reshape(depth, 22, 128, D).transpose(0, 2, 1, 3)
    gT = np.concatenate([f(inp["norm_mix_pre"]).reshape(depth, 8, 128).transpose(0, 2, 1),
                         f(inp["norm_ffn_pre"]).reshape(depth, 8, 128).transpose(0, 2, 1)], axis=2)
    g_post = np.stack([f(inp["norm_mix_post"]), f(inp["norm_ffn_post"])], axis=1)
    cw = f(inp["ffn_conv_w"]).reshape(depth, 3, 44, 128)
    cb = f(inp["ffn_conv_b"]).reshape(depth, 1, 44, 128)
    convp = np.concatenate([cw, cb], axis=1).transpose(0, 3, 2, 1)
    cosf, sinf = rope_tables(smax)
    ropec = np.concatenate([cosf, sinf], axis=1)
    gqk = np.concatenate([np.tile(f(inp["gqa_q_norm"]), (1, 4)), np.tile(f(inp["gqa_k_norm"]), (1, 2))], axis=1)
    rpb = f(inp["na_rpb"])
    abias = np.stack([build_abias(rpb[l]) for l in range(depth)]).reshape(depth, 128, NA_SLABS * 512)
    dbias = build_dbias().reshape(128, 3 * 512)
    ii = np.arange(128, dtype=np.float32)[:, None]
    jj = np.arange(512, dtype=np.float32)[None, :]
    base = jj - ii
    cdist = np.concatenate([base] + [np.abs(base + d) for d in (0.0, -128.0, -256.0, -384.0)], axis=1).astype(np.float32)
    cbias = np.zeros((128, 512), np.float32)
    cF = np.zeros((8, 512), np.float32)
    i_ = np.arange(128, dtype=np.float64)
    j_ = np.arange(512, dtype=np.float64)
    for h in range(4):
        sl = DIFF_SLOPES[h]
        for n in range(32):
            cbias[:, h * 64 + n] = -sl * (128.0 * n - i_)
            cbias[:, h * 64 + 32 + n] = -sl * (i_ + 128.0 * n - 511.0)
        cF[h * 2 + 0] = np.exp(-sl * j_)
        cF[h * 2 + 1] = np.exp(-sl * (511.0 - j_))
    lamv = np.concatenate([f(inp["diff_lambda_q1"]), f(inp["diff_lambda_k1"]), f(inp["diff_lambda_q2"]), f(inp["diff_lambda_k2"])], axis=1)
    subln = f(inp["diff_subln"]).reshape(depth, 64, 1)
    sink = f(inp["swa_sink"])
    c = np.ascontiguousarray
    return dict(w_in=c(w_in), w_br=c(w_br), w_out=c(w_out), w_up=c(w_up), w_dn=c(w_dn), gT_pre=c(gT), g_post=c(g_post),
                convp=c(convp), ropec=c(ropec), gqk=c(gqk), abias=c(abias), dbias=c(dbias), cdist=c(cdist), cbias=c(cbias), cF=c(cF), lamv=c(lamv),
                subln=c(subln), sink=c(sink))


_CACHE = {}


def kernel(**inp):
    xp = np.asarray(inp["x_prompt"], dtype=np.float32)
    xs = np.asarray(inp["x_sample"], dtype=np.float32)
    depth = int(np.asarray(inp["w_in"]).shape[0])
    ncores = 8
    bp, sp_ = xp.shape[0], xp.shape[1]
    bs, ss_ = xs.shape[0], xs.shape[1]
    npc, nsc = bp // ncores, bs // ncores
    seqs = [sp_] * npc + [ss_] * nsc
    key = (tuple(seqs), depth)
    if key not in _CACHE:
        _CACHE[key] = build(seqs, depth)[0]
    nc = _CACHE[key]
    shared = prep_shared(inp, depth, max(seqs))
    in_maps = []
    for c in range(ncores):
        xc = np.concatenate([xp[c * npc:(c + 1) * npc].reshape(-1, D), xs[c * nsc:(c + 1) * nsc].reshape(-1, D)], axis=0)
        m = dict(shared)
        m["x"] = np.ascontiguousarray(xc)
        in_maps.append(m)
    res = run_bass_kernel_spmd(nc, in_maps, core_ids=list(range(ncores)))
    yp = np.empty_like(xp)
    ys = np.empty_like(xs)
    for c in range(ncores):
        yc = res.results[c]["y"]
        yp[c * npc:(c + 1) * npc] = yc[:npc * sp_].reshape(npc, sp_, D)
        ys[c * nsc:(c + 1) * nsc] = yc[npc * sp_:].reshape(nsc, ss_, D)
    return (yp, ys)
```
